# Optimizing a Trainium2 kernel written in Bass

```python
import math
import numpy as np
import jax
import jax.numpy as jnp
from jax import lax

D_MODEL = 1024
BATCH = 8
SEQ = 2048
DEPTH = 2

HEAD_DIM = 64
N_MIX_HEADS = 4
GROUP_W = N_MIX_HEADS * HEAD_DIM
MIX_W = 4 * GROUP_W
ROPE_THETA = 10000.0
EPS = 1e-6
Q_BLOCK = 128
NEG = -1e30
BIG = 1e9

DIL_PATTERNS = ((128, 1), (512, 4), (2048, 16))

MLA_Q_RANK = 256
MLA_KV_RANK = 128
MLA_NOPE = 64
MLA_ROPE = 32
MLA_V = 64

NSA_KV_DIM = 64
NSA_CMP_LEN = 32
NSA_CMP_STRIDE = 16
NSA_CMP_HID = 256
NSA_SEL_LEN = 64
NSA_N_SEL = 16
NSA_WINDOW = 512
SEL_Q_BLOCK = 64

IN_WIDTHS = (
    GROUP_W, GROUP_W, GROUP_W, GROUP_W,
    MLA_Q_RANK, MLA_KV_RANK, MLA_ROPE, GROUP_W,
    GROUP_W, NSA_KV_DIM, NSA_KV_DIM, NSA_KV_DIM, NSA_KV_DIM,
    NSA_KV_DIM, NSA_KV_DIM, 3 * N_MIX_HEADS, GROUP_W,
    GROUP_W, GROUP_W, GROUP_W, GROUP_W,
)
D_IN = sum(IN_WIDTHS)

kernel_name = "hybrid_parallel_heads_dilated_mla_nsa_stickbreak"


def rms_norm(x, g):
    xf = x.astype(jnp.float32)
    y = xf * lax.rsqrt(jnp.mean(xf * xf, axis=-1, keepdims=True) + EPS)
    return (y * g.astype(jnp.float32)).astype(x.dtype)


def rope(x, pos):
    half = x.shape[-1] // 2
    inv = ROPE_THETA ** (-jnp.arange(half, dtype=jnp.float32) / half)
    ang = pos.astype(jnp.float32)[:, None, :, None] * inv
    cos, sin = jnp.cos(ang), jnp.sin(ang)
    xf = x.astype(jnp.float32)
    x1, x2 = xf[..., :half], xf[..., half:]
    return jnp.concatenate([x1 * cos - x2 * sin, x1 * sin + x2 * cos], -1).astype(x.dtype)


def to_heads(t, n):
    b, s, _ = t.shape
    return t.reshape(b, s, n, -1).transpose(0, 2, 1, 3)


def from_heads(t):
    b, h, s, d = t.shape
    return t.transpose(0, 2, 1, 3).reshape(b, s, h * d)


def banded_attention(q, k, v, max_dist, block, scale):
    L = q.shape[-2]
    lead = q.shape[:-2]
    nb = -(-L // block)
    lp = nb * block
    n_prev = -(-max_dist // block)
    nz = [(0, 0)] * len(lead)
    qb = jnp.pad(q, nz + [(0, lp - L), (0, 0)]).reshape(*lead, nb, block, q.shape[-1])
    kr = jnp.pad(k, nz + [(n_prev * block, lp - L), (0, 0)]).reshape(*lead, nb + n_prev, block, k.shape[-1])
    vr = jnp.pad(v, nz + [(n_prev * block, lp - L), (0, 0)]).reshape(*lead, nb + n_prev, block, v.shape[-1])
    kb = jnp.concatenate([kr[..., o:o + nb, :, :] for o in range(n_prev + 1)], axis=-2)
    vb = jnp.concatenate([vr[..., o:o + nb, :, :] for o in range(n_prev + 1)], axis=-2)
    s = jnp.einsum('...nqd,...nkd->...nqk', qb, kb).astype(jnp.float32) * scale
    qpos = (jnp.arange(nb)[:, None] * block + jnp.arange(block)[None, :])[:, :, None]
    kpos = ((jnp.arange(nb)[:, None] - n_prev) * block
            + jnp.arange((n_prev + 1) * block)[None, :])[:, None, :]
    dist = qpos - kpos
    ok = (dist >= 0) & (dist <= max_dist) & (kpos >= 0)
    s = jnp.where(ok, s, NEG)
    m = jnp.max(s, axis=-1)
    p = jnp.where(ok, jnp.exp(s - m[..., None]), 0.0)
    l = jnp.sum(p, axis=-1)
    o = jnp.einsum('...nqk,...nkd->...nqd', p, vb.astype(jnp.float32)) / l[..., None]
    o = o.reshape(*lead, lp, v.shape[-1])[..., :L, :]
    return o, m.reshape(*lead, lp)[..., :L], l.reshape(*lead, lp)[..., :L]


def dilated_mixture_attention(q, k, v):
    b, h, s, d = q.shape
    outs, ms, ls = [], [], []
    for window, dil in DIL_PATTERNS:
        def regroup(t):
            return t.reshape(b, h, s // dil, dil, d).swapaxes(2, 3)
        o, m, l = banded_attention(regroup(q), regroup(k), regroup(v), window // dil, Q_BLOCK, d ** -0.5)
        outs.append(o.swapaxes(2, 3).reshape(b, h, s, d))
        ms.append(m.swapaxes(2, 3).reshape(b, h, s))
        ls.append(l.swapaxes(2, 3).reshape(b, h, s))
    m_all = jnp.stack(ms)
    wts = jnp.stack(ls) * jnp.exp(m_all - jnp.max(m_all, axis=0, keepdims=True))
    o = jnp.sum(wts[..., None] * jnp.stack(outs), axis=0) / jnp.sum(wts, axis=0)[..., None]
    return o.astype(q.dtype)


def blocked_causal_attention(q, k, v, scale):
    b, h, s, dq = q.shape
    nb = s // Q_BLOCK
    qb = q.reshape(b, h, nb, Q_BLOCK, dq).transpose(2, 0, 1, 3, 4)
    kpos = jnp.arange(s)

    def one(args):
        qi, blk = args
        sc = jnp.einsum('bhqd,bhkd->bhqk', qi, k).astype(jnp.float32) * scale
        qpos = blk * Q_BLOCK + jnp.arange(Q_BLOCK)
        sc = jnp.where(kpos[None, :] <= qpos[:, None], sc, NEG)
        p = jax.nn.softmax(sc, axis=-1)
        return jnp.einsum('bhqk,bhkd->bhqd', p, v.astype(jnp.float32))

    o = lax.map(one, (qb, jnp.arange(nb)))
    return o.transpose(1, 2, 0, 3, 4).reshape(b, h, s, v.shape[-1]).astype(q.dtype)


def mla_attention(c_q, c_kv, k_rope_in, pos, g_q, g_kv, w_uq, w_ukv):
    b, s, _ = c_q.shape
    q = to_heads(rms_norm(c_q, g_q) @ w_uq, N_MIX_HEADS)
    q = jnp.concatenate([q[..., :MLA_NOPE], rope(q[..., MLA_NOPE:], pos)], -1)
    kv = to_heads(rms_norm(c_kv, g_kv) @ w_ukv, N_MIX_HEADS)
    k_r = rope(k_rope_in[:, None], pos)
    k = jnp.concatenate([kv[..., :MLA_NOPE], jnp.broadcast_to(k_r, (b, N_MIX_HEADS, s, MLA_ROPE))], -1)
    v = kv[..., MLA_NOPE:]
    return blocked_causal_attention(q, k, v, (MLA_NOPE + MLA_ROPE) ** -0.5)


def compress_blocks(x, pos_emb, w1, w2):
    b, s, d = x.shape
    n = (s - NSA_CMP_LEN) // NSA_CMP_STRIDE + 1
    idx = jnp.arange(n)[:, None] * NSA_CMP_STRIDE + jnp.arange(NSA_CMP_LEN)[None, :]
    blocks = x[:, idx, :] + pos_emb
    return jax.nn.silu(blocks.reshape(b, n, NSA_CMP_LEN * d) @ w1) @ w2


def selected_block_attention(q, k, v, sel_idx, scale):
    b, h, s, d = q.shape
    n_top = sel_idx.shape[-1]
    kb = k.reshape(b, s // NSA_SEL_LEN, NSA_SEL_LEN, d)
    vb = v.reshape(b, s // NSA_SEL_LEN, NSA_SEL_LEN, d)
    nq = s // SEL_Q_BLOCK
    qb = q.reshape(b, h, nq, SEL_Q_BLOCK, d).transpose(2, 0, 1, 3, 4)
    ib = sel_idx.reshape(b, nq, SEL_Q_BLOCK, n_top).transpose(1, 0, 2, 3)
    gather = jax.vmap(lambda blocks, ix: blocks[ix])

    def one(args):
        qi, idx, blk = args
        kg = gather(kb, idx).reshape(b, SEL_Q_BLOCK, n_top * NSA_SEL_LEN, d)
        vg = gather(vb, idx).reshape(b, SEL_Q_BLOCK, n_top * NSA_SEL_LEN, d)
        kpos = (idx[..., None] * NSA_SEL_LEN + jnp.arange(NSA_SEL_LEN)).reshape(b, SEL_Q_BLOCK, -1)
        qpos = blk * SEL_Q_BLOCK + jnp.arange(SEL_Q_BLOCK)
        ok = kpos <= qpos[None, :, None]
        sc = jnp.einsum('bhqd,bqkd->bhqk', qi, kg).astype(jnp.float32) * scale
        p = jax.nn.softmax(jnp.where(ok[:, None], sc, NEG), axis=-1)
        return jnp.einsum('bhqk,bqkd->bhqd', p, vg.astype(jnp.float32))

    o = lax.map(one, (qb, ib, jnp.arange(nq)))
    return o.transpose(1, 2, 0, 3, 4).reshape(b, h, s, d)


def nsa_attention(q, k_cmp, v_cmp, k_slc, v_slc, k_win, v_win, gate_logits,
                  pos_k, pos_v, kw1, kw2, vw1, vw2):
    b, h, s, d = q.shape
    scale = d ** -0.5
    t = jnp.arange(s)
    kc = compress_blocks(k_cmp, pos_k, kw1, kw2)
    vc = compress_blocks(v_cmp, pos_v, vw1, vw2)
    n_cmp = kc.shape[1]
    cmp_ok = (jnp.arange(n_cmp) * NSA_CMP_STRIDE + NSA_CMP_LEN - 1)[None, :] <= t[:, None]
    sc = jnp.where(cmp_ok, jnp.einsum('bhtd,bnd->bhtn', q, kc).astype(jnp.float32) * scale, NEG)
    e = jnp.where(cmp_ok, jnp.exp(sc - jnp.max(sc, axis=-1, keepdims=True)), 0.0)
    den = jnp.sum(e, axis=-1, keepdims=True)
    p_cmp = e / jnp.maximum(den, 1e-30)
    o_cmp = jnp.einsum('bhtn,bnd->bhtd', p_cmp, vc.astype(jnp.float32))
    n_slc = s // NSA_SEL_LEN
    ci = jnp.arange(n_cmp)[:, None] * NSA_CMP_STRIDE
    sj = jnp.arange(n_slc)[None, :] * NSA_SEL_LEN
    overlap = ((ci < sj + NSA_SEL_LEN) & (ci + NSA_CMP_LEN > sj)).astype(jnp.float32)
    imp = jnp.einsum('bhtn,nj->btj', p_cmp, overlap)
    jj = jnp.arange(n_slc)[None, :]
    bt = (t // NSA_SEL_LEN)[:, None]
    forced = (jj == 0) | (jj == bt) | (jj == bt - 1)
    imp = jnp.where(forced, BIG, jnp.where(jj > bt, -BIG, imp))
    _, sel_idx = lax.top_k(imp, min(NSA_N_SEL, n_slc))
    o_slc = selected_block_attention(q, k_slc, v_slc, sel_idx, scale)
    kw = jnp.broadcast_to(k_win[:, None], (b, h, s, d))
    vw = jnp.broadcast_to(v_win[:, None], (b, h, s, d))
    o_win, _, _ = banded_attention(q, kw, vw, NSA_WINDOW - 1, Q_BLOCK, scale)
    g = jax.nn.sigmoid(gate_logits.astype(jnp.float32)).reshape(b, s, h, 3).transpose(0, 2, 1, 3)
    return (g[..., 0:1] * o_cmp + g[..., 1:2] * o_slc + g[..., 2:3] * o_win).astype(q.dtype)


def stick_breaking_attention(q, k, v):
    b, h, s, d = q.shape
    nb = s // Q_BLOCK
    qb = q.reshape(b, h, nb, Q_BLOCK, d).transpose(2, 0, 1, 3, 4)
    kpos = jnp.arange(s)

    def one(args):
        qi, blk = args
        z = jnp.einsum('bhqd,bhkd->bhqk', qi, k).astype(jnp.float32) * d ** -0.5
        qpos = blk * Q_BLOCK + jnp.arange(Q_BLOCK)
        strict = kpos[None, :] < qpos[:, None]
        log_keep = jnp.where(strict, jax.nn.log_sigmoid(-z), 0.0)
        rev = lax.cumsum(log_keep, axis=3, reverse=True)
        after = jnp.concatenate([rev[..., 1:], jnp.zeros_like(rev[..., :1])], axis=-1)
        a = jnp.where(strict, jnp.exp(jax.nn.log_sigmoid(z) + after), 0.0)
        return jnp.einsum('bhqk,bhkd->bhqd', a, v.astype(jnp.float32))

    o = lax.map(one, (qb, jnp.arange(nb)))
    return o.transpose(1, 2, 0, 3, 4).reshape(b, h, s, d).astype(q.dtype)


def hybrid_layer(x, pos, w_in, w_out, g_pre, g_post, mla_g_q, mla_g_kv, mla_w_uq, mla_w_ukv,
                 nsa_pos_k, nsa_pos_v, nsa_k_w1, nsa_k_w2, nsa_v_w1, nsa_v_w2):
    h = rms_norm(x, g_pre)
    proj = h @ w_in
    points = np.cumsum(IN_WIDTHS)[:-1].tolist()
    (a_q, a_k, a_v, a_g,
     b_cq, b_ckv, b_kr, b_g,
     c_q, c_kc, c_vc, c_ks, c_vs, c_kw, c_vw, c_gl, c_g,
     d_q, d_k, d_v, d_g) = jnp.split(proj, points, axis=-1)
    o_a = dilated_mixture_attention(rope(to_heads(a_q, N_MIX_HEADS), pos),
                                    rope(to_heads(a_k, N_MIX_HEADS), pos),
                                    to_heads(a_v, N_MIX_HEADS))
    o_b = mla_attention(b_cq, b_ckv, b_kr, pos, mla_g_q, mla_g_kv, mla_w_uq, mla_w_ukv)
    rope_k = lambda t: rope(t[:, None], pos)[:, 0]
    o_c = nsa_attention(rope(to_heads(c_q, N_MIX_HEADS), pos), rope_k(c_kc), c_vc, rope_k(c_ks), c_vs,
                        rope_k(c_kw), c_vw, c_gl, nsa_pos_k, nsa_pos_v,
                        nsa_k_w1, nsa_k_w2, nsa_v_w1, nsa_v_w2)
    o_d = stick_breaking_attention(to_heads(d_q, N_MIX_HEADS), to_heads(d_k, N_MIX_HEADS),
                                   to_heads(d_v, N_MIX_HEADS))
    mixed = jnp.concatenate([from_heads(o_a) * jax.nn.silu(a_g),
                             from_heads(o_b) * jax.nn.silu(b_g),
                             from_heads(o_c) * jax.nn.silu(c_g),
                             from_heads(o_d) * jax.nn.silu(d_g)], axis=-1).astype(x.dtype)
    return x + rms_norm(mixed @ w_out, g_post)


def setup_inputs(seed: int = 0) -> dict:
    key = jax.random.key(seed)
    ks = jax.random.split(key, 16)
    nrm = lambda k, shape, fan: jax.random.normal(k, shape, jnp.float32) * fan ** -0.5
    gain = lambda k, n: 1.0 + 0.05 * jax.random.normal(k, (DEPTH, n), jnp.float32)
    return {
        "x": jax.random.normal(ks[0], (BATCH, SEQ, D_MODEL), jnp.float32),
        "positions": jnp.broadcast_to(jnp.arange(SEQ, dtype=jnp.int32), (BATCH, SEQ)),
        "w_in": nrm(ks[1], (DEPTH, D_MODEL, D_IN), D_MODEL),
        "w_out": nrm(ks[2], (DEPTH, MIX_W, D_MODEL), MIX_W),
        "g_pre": gain(ks[3], D_MODEL),
        "g_post": gain(ks[4], D_MODEL),
        "mla_g_q": gain(ks[5], MLA_Q_RANK),
        "mla_g_kv": gain(ks[6], MLA_KV_RANK),
        "mla_w_uq": nrm(ks[7], (DEPTH, MLA_Q_RANK, N_MIX_HEADS * (MLA_NOPE + MLA_ROPE)), MLA_Q_RANK),
        "mla_w_ukv": nrm(ks[8], (DEPTH, MLA_KV_RANK, N_MIX_HEADS * (MLA_NOPE + MLA_V)), MLA_KV_RANK),
        "nsa_pos_k": 0.5 * jax.random.normal(ks[9], (DEPTH, NSA_CMP_LEN, NSA_KV_DIM), jnp.float32),
        "nsa_pos_v": 0.5 * jax.random.normal(ks[10], (DEPTH, NSA_CMP_LEN, NSA_KV_DIM), jnp.float32),
        "nsa_k_w1": nrm(ks[11], (DEPTH, NSA_CMP_LEN * NSA_KV_DIM, NSA_CMP_HID), NSA_CMP_LEN * NSA_KV_DIM),
        "nsa_k_w2": nrm(ks[12], (DEPTH, NSA_CMP_HID, NSA_KV_DIM), NSA_CMP_HID),
        "nsa_v_w1": nrm(ks[13], (DEPTH, NSA_CMP_LEN * NSA_KV_DIM, NSA_CMP_HID), NSA_CMP_LEN * NSA_KV_DIM),
        "nsa_v_w2": nrm(ks[14], (DEPTH, NSA_CMP_HID, NSA_KV_DIM), NSA_CMP_HID),
    }


def reference(x, positions, w_in, w_out, g_pre, g_post, mla_g_q, mla_g_kv, mla_w_uq, mla_w_ukv,
              nsa_pos_k, nsa_pos_v, nsa_k_w1, nsa_k_w2, nsa_v_w1, nsa_v_w2):
    for l in range(DEPTH):
        x = hybrid_layer(x, positions, w_in[l], w_out[l], g_pre[l], g_post[l],
                         mla_g_q[l], mla_g_kv[l], mla_w_uq[l], mla_w_ukv[l],
                         nsa_pos_k[l], nsa_pos_v[l], nsa_k_w1[l], nsa_k_w2[l],
                         nsa_v_w1[l], nsa_v_w2[l])
    return x
```

```python
import math
import numpy as np
import concourse.bass as bass
import concourse.mybir as mybir
from concourse.bass_utils import run_bass_kernel_spmd

F32 = mybir.dt.float32
BF16 = mybir.dt.bfloat16
I32 = mybir.dt.int32
ALU = mybir.AluOpType
AF = mybir.ActivationFunctionType
AX = mybir.AxisListType

SAME_ENGINE_SYNC = True
SEQ = 2048
DM = 1024
DIN = 3628
NEG = -30000.0
EPS = 1e-6
NCORES = 8


class Buf:
    __slots__ = ("name", "w", "r")

    def __init__(self, name):
        self.name = name
        self.w = {}
        self.r = {}


class Sched:
    def __init__(self, nc):
        self.nc = nc
        self.eng = dict(pe=nc.tensor, act=nc.scalar, dve=nc.vector, pool=nc.gpsimd, sp=nc.sync)
        self.prog = {k: [] for k in self.eng}
        self.csem = {k: nc.alloc_semaphore("c_" + k) for k in ("pe", "act", "dve", "pool")}
        self.cnt = {k: 0 for k in self.csem}
        self.waited = {k: {} for k in self.eng}
        self.dsem = {}
        self.dtotal = {}

    def _dma_sem(self, key):
        if key not in self.dsem:
            self.dsem[key] = self.nc.alloc_semaphore("d_" + key)
            self.dtotal[key] = 0
        return self.dsem[key]

    def op(self, e, fn, reads=(), writes=(), dma=None):
        if e != "pe":
            writes = list(writes) + [b for b in reads if b.name.startswith("ps") and b not in writes]
        deps = {}

        def need(tok):
            k, sem, val = tok
            if k not in deps or deps[k][1] < val:
                deps[k] = (sem, val)

        for b in reads:
            for tok in b.w.values():
                need(tok)
        for b in writes:
            for tok in b.w.values():
                need(tok)
            for tok in b.r.values():
                need(tok)
        waits = []
        for k, (sem, val) in deps.items():
            if k in self.dtotal:
                val = self.dtotal[k]
            elif k == e and (e == "pe" or not SAME_ENGINE_SYNC):
                continue
            if self.waited[e].get(k, 0) < val:
                waits.append((sem, val))
                self.waited[e][k] = val
        if dma is not None:
            sem = self._dma_sem(dma)
            self.dtotal[dma] += 16
            tok = (dma, sem, self.dtotal[dma])
            inc = 16
        else:
            self.cnt[e] += 1
            tok = (e, self.csem[e], self.cnt[e])
            inc = 1
        self.prog[e].append((waits, fn, tok[1], inc))
        for b in reads:
            b.r[tok[0]] = tok
        for b in writes:
            b.w[tok[0]] = tok
        return tok

    def emit(self):
        nc = self.nc
        fin = []
        for k, sem in self.dsem.items():
            if self.waited["sp"].get(k, 0) < self.dtotal[k]:
                fin.append((sem, self.dtotal[k]))

        def run(eng, k):
            for waits, fn, sem, inc in self.prog[k]:
                for s, v in waits:
                    eng.wait_ge(s, v)
                fn(eng).then_inc(sem, inc)
            if k == "sp":
                for s, v in fin:
                    eng.wait_ge(s, v)

        with nc.Block() as block:
            @block.tensor
            def _(eng):
                run(eng, "pe")

            @block.scalar
            def _(eng):
                run(eng, "act")

            @block.vector
            def _(eng):
                run(eng, "dve")

            @block.gpsimd
            def _(eng):
                run(eng, "pool")

            @block.sync
            def _(eng):
                run(eng, "sp")


CF = {}


def _build_consts():
    if CF:
        return CF
    import ml_dtypes
    k = np.arange(128)[:, None]
    q = np.arange(128)[None, :]
    offs = {"f": 0, "b": 0}
    cols = {"f": [], "b": []}

    def add(kind, name, arr):
        arr = np.asarray(arr, np.float32)
        a = np.zeros((128, arr.shape[1]), np.float32)
        a[: arr.shape[0]] = arr
        CF[name] = (kind, offs[kind], arr.shape[1])
        cols[kind].append(a)
        offs[kind] += arr.shape[1]

    zero = np.zeros((128, 128)); neg = np.full((128, 128), NEG)
    tri = np.where(q >= k, 0.0, NEG); tris = np.where(q > k, 0.0, NEG); atri = np.where(q < k, 0.0, NEG)
    add("b", "ident", np.eye(128))
    add("b", "Tc", np.concatenate([neg, neg, neg, tri, zero, zero, zero], 1))
    add("b", "Ts", np.concatenate([neg, neg, neg, tris, zero, zero, zero], 1))
    add("b", "Tw", np.concatenate([neg] * 3 + [tri, zero, zero, zero, atri] + [neg] * 3, 1))
    bh, bl = [], []
    for dl in range(-3, 9):
        if dl < 0:
            bh.append(zero); continue
        d = 128 * dl + q - k
        c = ((d >= 0) & (d <= 128)).astype(np.float64) + ((d >= 0) & (d % 4 == 0) & (d <= 512)) + ((d >= 0) & (d % 16 == 0))
        bh.append(c)
    add("b", "Ca", np.concatenate(bh, 1))
    add("b", "negU", np.where(k >= q, -1.0, 0.0))
    add("b", "negones", -np.ones((128, 128)))
    add("b", "onesb", np.ones((128, 128)))
    n = np.arange(127)[:, None]; j = np.arange(32)[None, :]
    add("b", "OV", ((16 * n < 64 * j + 64) & (16 * n + 32 > 64 * j)).astype(np.float32))
    t = np.arange(SEQ)[None, :]
    add("b", "Tcmp", np.where(t >= 16 * n + 31, 0.0, NEG))
    e = np.zeros((32, 16 * 128))
    for kb in range(16):
        for kk in range(128):
            e[2 * kb + (kk >= 64), kb * 128 + kk] = NEG
    add("b", "EXPNEG", e)
    add("f", "ones", np.ones((128, 128)))
    add("f", "identf", np.eye(128))
    nf = np.zeros((128, 16 * 32)); ad = np.zeros((128, 16 * 32))
    for tt in range(16):
        for p in range(128):
            bt = (128 * tt + p) // 64
            for jj in range(32):
                forced = jj == 0 or jj == bt or jj == bt - 1
                fut = jj > bt
                nf[p, tt * 32 + jj] = 0.0 if (forced or fut) else 1.0
                ad[p, tt * 32 + jj] = 1e9 if forced else (-1e9 if fut else 0.0)
    add("f", "NF", nf); add("f", "ADDT", ad)
    sr = np.zeros((128, 12))
    for r in range(12):
        sr[64 + r, r] = 1.0
    add("f", "SELCOL", sr)
    p = np.arange(128)
    inv32 = (10000.0 ** (-(np.arange(32, dtype=np.float32)) / np.float32(32))).astype(np.float32)
    inv16 = (10000.0 ** (-(np.arange(16, dtype=np.float32)) / np.float32(16))).astype(np.float32)
    add("f", "inv32", inv32[p % 32][:, None]); add("f", "inv16", inv16[p % 16][:, None])
    add("f", "sgn32", np.where(p % 64 < 32, -1.0, 1.0)[:, None]); add("f", "sgn16", np.where(p % 32 < 16, -1.0, 1.0)[:, None])
    CF["_f"] = np.concatenate(cols["f"], 1); CF["_b"] = np.concatenate(cols["b"], 1)
    return CF


def build(nlayers=2, debug=False, mixers="ABCD"):
    cf = _build_consts()
    NF_, NB_ = cf["_f"].shape[1], cf["_b"].shape[1]
    nc = bass.Bass("TRN2", target_bir_lowering=False)
    S = Sched(nc)

    def din(name, shape, dt=F32):
        return nc.dram_tensor(name, list(shape), dt, kind="ExternalInput")

    x_d = din("x", (SEQ, DM)); pos_d = din("positions", (1, SEQ), I32)
    w_in_d = din("w_in", (2, DM, DIN)); w_out_d = din("w_out", (2, DM, DM))
    g_pre_d = din("g_pre", (2, DM)); g_post_d = din("g_post", (2, DM))
    gq_d = din("mla_g_q", (2, 256)); gkv_d = din("mla_g_kv", (2, 128))
    wuq_d = din("mla_w_uq", (2, 256, 384)); wukv_d = din("mla_w_ukv", (2, 128, 512))
    posk_d = din("nsa_pos_k", (2, 32, 64)); posv_d = din("nsa_pos_v", (2, 32, 64))
    kw1_d = din("nsa_k_w1", (2, 2048, 256)); kw2_d = din("nsa_k_w2", (2, 256, 64))
    vw1_d = din("nsa_v_w1", (2, 2048, 256)); vw2_d = din("nsa_v_w2", (2, 256, 64))
    cff_d = din("cff", (128, NF_)); cfb_d = din("cfb", (128, NB_))
    out_d = nc.dram_tensor("out", [SEQ, DM], F32, kind="ExternalOutput")
    x1_d = nc.dram_tensor("x1s", [SEQ, DM], F32)
    tab_d = nc.dram_tensor("tabs_dram", [4, 128, SEQ], F32)
    mix_d = nc.dram_tensor("mix_dram", [16, 128, 8, 128], BF16, kind="ExternalOutput" if debug else "Internal")
    dbg_d = nc.dram_tensor("dbg", [4, 128, SEQ], BF16, kind="ExternalOutput") if debug else None
    ocmp_d = nc.dram_tensor("ocmp_dram", [16, 64, 512], F32)
    Bocmp = [Buf("ocmp%d" % i) for i in range(16)]
    Bx1 = Buf("x1"); Btab = Buf("tab"); Bout = Buf("out"); Bmixd = [Buf("mixd%d" % i) for i in range(16)]

    _cnt = [0]

    def sb(shape, dt, name=None):
        _cnt[0] += 1
        return nc.alloc_sbuf_tensor(name or ("t%d" % _cnt[0]), list(shape), dt)

    def op(e, fn, reads=(), writes=(), dma=None):
        return S.op(e, fn, reads, writes, dma)

    def dma(out, in_, reads, writes, key, **kw):
        op("sp", lambda e: e.dma_start(out=out, in_=in_, **kw), reads, writes, dma=key)

    def mm(out, lhsT, rhs, start, stop, reads, writes):
        op("pe", lambda e: e.matmul(out, lhsT=lhsT, rhs=rhs, start=start, stop=stop, skip_group_check=True), reads, writes)

    def act(out, in_, func, reads, writes, **kw):
        op("act", lambda e: e.activation(out=out, in_=in_, func=func, **kw), reads, writes)

    def tt(eng, out, in0, in1, alu, reads, writes):
        op(eng, lambda e: e.tensor_tensor(out=out, in0=in0, in1=in1, op=alu), reads, writes)

    def ts(eng, out, in0, s1, s2, op0, op1, reads, writes):
        if op1 is None:
            op(eng, lambda e: e.tensor_single_scalar(out=out, in_=in0, scalar=s1, op=op0), reads, writes)
        else:
            op(eng, lambda e: e.tensor_scalar(out=out, in0=in0, scalar1=s1, scalar2=s2, op0=op0, op1=op1), reads, writes)

    def stt(eng, out, in0, scalar, in1, op0, op1, reads, writes):
        op(eng, lambda e: e.scalar_tensor_tensor(out=out, in0=in0, scalar=scalar, in1=in1, op0=op0, op1=op1), reads, writes)

    def cp(eng, out, in_, reads, writes):
        if eng == "act":
            op(eng, lambda e: e.activation(out=out, in_=in_, func=AF.Copy), reads, writes)
        else:
            op(eng, lambda e: e.tensor_copy(out=out, in_=in_), reads, writes)

    def recip(out, in_, reads, writes):
        op("dve", lambda e: e.reciprocal(out=out, in_=in_), reads, writes)

    def recip_act(out, in_, reads, writes):
        act(out, in_, AF.Ln, reads, writes)
        act(out, out, AF.Exp, writes, writes, scale=-1.0)

    def memset(ap, val, writes):
        op("pool", lambda e: e.memset(ap, val), [], writes)

    def bc2(ap2):
        a = ap2.ap
        return bass.AP(ap2.tensor, ap2.offset, [list(a[0]), [0, 2], list(a[1])])

    def bc_last(ap2, n):
        a = ap2.ap
        return bass.AP(ap2.tensor, ap2.offset, [list(a[0]), list(a[1]), [0, n]])

    rot = {}

    def nxt(kind, n):
        v = rot.get(kind, 0)
        rot[kind] = (v + 1) % n
        return v

    cff = sb((128, NF_), F32, "cff_sb"); Bcf = Buf("cf")
    dma(cff[:], cff_d[:], [], [Bcf], "cst")
    CB = sb((128, NB_), BF16, "CB"); BCB = Buf("CB")
    stage = [sb((128, 8, 128), F32, "stage%d" % i) for i in range(2)]; Bst = [Buf("st%d" % i) for i in range(2)]

    def stage_flat(si, r0, r1, n):
        return stage[si][r0:r1].rearrange("p c n -> p (c n)")[:, 0:n]

    o = 0
    while o < NB_:
        n = min(1024, NB_ - o)
        si = nxt("st", 2)
        dma(stage_flat(si, 0, 128, n), cfb_d[:, o:o + n], [], [Bst[si]], "st%d" % si)
        cp("pool", CB[:, o:o + n], stage_flat(si, 0, 128, n), [Bst[si]], [BCB])
        o += n

    def cfa(name, r0=0, r1=128, c0=0, c1=None):
        _, o_, n_ = cf[name]
        c1 = n_ if c1 is None else c1
        return cff[r0:r1, o_ + c0:o_ + c1]

    def cba(name, r0=0, r1=128, c0=0, c1=None):
        _, o_, n_ = cf[name]
        c1 = n_ if c1 is None else c1
        return CB[r0:r1, o_ + c0:o_ + c1]

    identb = cba("ident")
    PS = [nc.alloc_psum_tensor("ps%d" % i, [128, 512], F32) for i in range(8)]
    BPS = [Buf("ps%d" % i) for i in range(8)]

    hT = sb((128, 8, SEQ), BF16, "hT"); BhT = [Buf("hT%d" % i) for i in range(16)]
    mixT = sb((128, 2, SEQ), BF16, "mixT"); Bmix = [[Buf("mix%d_%d" % (c, qc)) for qc in range(4)] for c in range(2)]
    mlt = [sb((128, 8, 128), BF16, "mlt%d" % i) for i in range(2)]; Bmlt = [Buf("mlt%d" % i) for i in range(2)]
    Wbf = sb((128, 8, 1100), BF16, "Wbf"); BWp = [Buf("W%d" % i) for i in range(9)]

    def BWr(c0, n):
        return BWp[c0 // 128:(c0 + n - 1) // 128 + 1]
    pend = {"store": None}

    def flush_store():
        if pend["store"] is not None:
            f = pend["store"]; pend["store"] = None
            f()
    xt = [sb((128, DM), F32, "xt%d" % i) for i in range(2)]; Bxt = [Buf("xt%d" % i) for i in range(2)]
    gpre = sb((128, 16), F32, "gpre"); Bgpre = Buf("gpre")
    dma(gpre[:, 0:8], g_pre_d[0].rearrange("(c p) -> p c", p=128), [], [Bgpre], "cst", allow_slow_non_contiguous=True)
    dma(gpre[:, 8:16], g_pre_d[1].rearrange("(c p) -> p c", p=128), [], [Bgpre], "cst", allow_slow_non_contiguous=True)
    small = sb((128, 64), F32, "small"); Bsmall = [Buf("small%d" % i) for i in range(64)]
    tabs = [sb((128, 2, 512), F32, "tabs%d" % i) for i in range(2)]; Btabs = [Buf("tabs%d" % i) for i in range(2)]
    sg = sb((128, SEQ), F32, "sg"); Bsg = [Buf("sg%d" % i) for i in range(4)]
    qT = sb((128, 2, SEQ), BF16, "qT"); BqT = [[Buf("qT%d_%d" % (a, b)) for b in range(4)] for a in range(2)]
    kT = sb((128, 2, SEQ), BF16, "kT"); BkT = [[Buf("kT%d_%d" % (a, b)) for b in range(4)] for a in range(2)]
    qx = sb((128, 2, SEQ), BF16, "qx"); BqX = [[Buf("qx%d_%d" % (a, b)) for b in range(4)] for a in range(2)]
    kx = sb((128, 2, SEQ), BF16, "kx"); BkX = [[Buf("kx%d_%d" % (a, b)) for b in range(4)] for a in range(2)]
    Vaug = sb((128, 16, 4, 128), BF16, "Vaug"); BV = [Buf("V%d" % i) for i in range(16)]
    Vv = Vaug[:].rearrange("p t (a h) n -> p t a h n", h=2)
    Pbig = sb((128, 4, 512), BF16, "Pbig")
    Pt = [Pbig[:, i, :] for i in range(4)]; BP = [Buf("P%d" % i) for i in range(4)]
    xn = Pbig[:, 0:2, :].rearrange("p a n -> p (a n)")
    NT = 5
    tmpf = [sb((128, 512), F32, "tmpf%d" % i) for i in range(NT)]; Btmp = [Buf("tmpf%d" % i) for i in range(NT)]
    ARENA_N = 14 * 1024
    arena = sb((128, ARENA_N), BF16, "arena")
    ar = {"ptr": 0, "cur": [], "prev": []}

    def arena_reset():
        ar["ptr"] = 0
        ar["prev"] = ar["prev"] + ar["cur"]
        ar["cur"] = []

    def carve(nelem, dt, name):
        nb = nelem * (4 if dt in (F32, I32) else 2)
        nb = (nb + 63) // 64 * 64
        o_ = ar["ptr"]
        assert o_ + nb // 2 <= ARENA_N, ("arena overflow", name)
        ar["ptr"] += nb // 2
        a = arena[:, o_:o_ + nb // 2]
        if dt != BF16:
            a = a.bitcast(dt)
        B = Buf(name)
        ar["cur"].append(B)
        return a[:, 0:nelem], B

    def arena_gate():
        memset(small[:, 63:64], 0.0, ar["prev"] + ar["cur"])
        ar["prev"] = []

    def tcol():
        i = nxt("sm", 60)
        return small[:, i:i + 1], Bsmall[i]

    def tf():
        i = nxt("T", NT)
        return tmpf[i], Btmp[i]

    def pbuf():
        i = nxt("P", 4)
        return Pt[i], BP[i]

    pools = {"M": [5, 6, 7], "O": [3, 4]}

    def mbank():
        p = pools["M"]
        return p[nxt("M%d" % len(p), len(p))]

    def sbank():
        return nxt("S", 3)

    def obank():
        p = pools["O"]
        return p[nxt("O%d" % len(p), len(p))]

    TWO_PI = 2.0 * math.pi
    for c4 in range(4):
        cs = slice(c4 * 512, (c4 + 1) * 512)
        ki, Bki = xt[0][:, 0:512], Bxt[0]; pf, Bpf = xt[0][:, 512:1024], Bxt[0]
        kii = ki.bitcast(I32)
        dma(kii, pos_d[0:1, cs].partition_broadcast(128), [], [Bki], "tabp")
        cp("dve", pf[:], kii, [Bki], [Bpf])
        for ti, (invn, sgnn, phase) in enumerate([("inv32", None, math.pi / 2), ("inv32", "sgn32", 0.0),
                                                  ("inv16", None, math.pi / 2), ("inv16", "sgn16", 0.0)]):
            an, Ban = xt[1][:, 0:512], Bxt[1]; tb, Btb = xt[1][:, 512:1024], Bxt[1]
            ts("dve", an[:], pf[:], cfa(invn), phase, ALU.mult, ALU.add, [Bpf, Bcf], [Ban])
            ts("dve", tb[:], an[:], 1.0 / TWO_PI, None, ALU.mult, None, [Ban], [Btb])
            cp("dve", kii, tb[:], [Btb], [Bki])
            cp("dve", tb[:], kii, [Bki], [Btb])
            stt("dve", an[:], tb[:], -TWO_PI, an[:], ALU.mult, ALU.add, [Btb, Ban], [Ban])
            ts("dve", an[:], an[:], math.pi, -math.pi, ALU.min, ALU.max, [Ban], [Ban])
            act(tb[:], an[:], AF.Sin, [Ban], [Btb])
            if sgnn:
                ts("dve", tb[:], tb[:], cfa(sgnn), None, ALU.mult, None, [Btb, Bcf], [Btb])
            dma(tab_d[ti, :, cs], tb[:], [Btb], [Btab], "tab")

    def load_w(l, pieces):
        for (sc, n, dc) in pieces:
            si = nxt("st", 2)
            dma(stage[si][:, :, 0:n], w_in_d[l, :, sc:sc + n].rearrange("(c p) n -> p c n", p=128), [], [Bst[si]], "st%d" % si)
            tt("pool", Wbf[:, :, dc:dc + n], stage[si][:, :, 0:n], bc_last(gpre[:, l * 8:(l + 1) * 8], n), ALU.mult,
               [Bst[si], Bgpre], BWr(dc, n))

    def pieces_range(s0, n, d0):
        out = []
        o_ = 0
        while o_ < n:
            m = min(128, n - o_)
            out.append((s0 + o_, m, d0 + o_))
            o_ += m
        return out

    C_PCS = [(2144, 64, 704), (2272, 64, 768), (2336, 12, 832), (2348, 128, 844), (2476, 128, 972), (2016, 64, 640),
             (1696, 128, 0), (1824, 128, 128), (1952, 64, 256), (1952, 64, 320), (2080, 64, 384), (2080, 64, 448),
             (2208, 64, 512), (2208, 64, 576)]
    W_PLAN = {"A": pieces_range(0, 1024, 0), "B": pieces_range(1024, 672, 0), "C": C_PCS, "D": pieces_range(2604, 1024, 0),
              "O": [(p * 128, 128, p * 128) for p in range(8)]}
    W_GATE = {"A": (768, 1024), "B": (416, 672), "C": (832, 1100), "D": (768, 1024)}
    w_done = set()

    def load_pieces(l, name, filt=None):
        for p in W_PLAN[name]:
            key = (l, name, p)
            if key in w_done or (filt is not None and not filt(p)):
                continue
            w_done.add(key)
            if name == "O":
                si = nxt("st", 2)
                dma(stage[si][:, :, :], w_out_d[l, :, p[0]:p[0] + 128].rearrange("(c p) n -> p c n", p=128), [], [Bst[si]], "st%d" % si)
                cp("pool", Wbf[:, :, p[2]:p[2] + 128], stage[si][:, :, :], [Bst[si]], BWr(p[2], 128))
            else:
                load_w(l, [p])

    def prefetch_w(l, cur, nxt_):
        g0, g1 = W_GATE[cur]
        load_pieces(l, nxt_, lambda p: p[2] + p[1] <= g0 or p[2] >= g1)

    def proj_fm(col0, M, tc, bank):
        for c in range(8):
            mm(PS[bank][0:M, :], Wbf[:, c, col0:col0 + M], hT[:, c, tc * 512:(tc + 1) * 512], c == 0, c == 7,
               BWr(col0, M) + BhT[4 * tc:4 * tc + 4], [BPS[bank]])

    def load_tabs(which, tc):
        i = nxt("tab", 2)
        cs = slice(tc * 512, (tc + 1) * 512)
        dma(tabs[i][:, 0, :], tab_d[2 * which, :, cs], [Btab], [Btabs[i]], "tabl%d" % i)
        dma(tabs[i][:, 1, :], tab_d[2 * which + 1, :, cs], [Btab], [Btabs[i]], "tabl%d" % i)
        return tabs[i], Btabs[i]

    def rope_evac(bank, dst, Bdst, tab, Btb_, scale, split=None):
        t1, B1 = tf(); t2, B2 = tf()
        stt("dve", t1[:, :], PS[bank][:, :], scale, tab[:, 0, :], ALU.mult, ALU.mult, [BPS[bank], Btb_], [B1])
        for b in range(4):
            src = b + 1 if b % 2 == 0 else b - 1
            if b != 1:
                act(t2[32 * b:32 * b + 32, :], PS[bank][32 * src:32 * src + 32, :], AF.Copy, [BPS[bank]], [B2], scale=scale)
            else:
                ts("dve", t2[32 * b:32 * b + 32, :], PS[bank][32 * src:32 * src + 32, :], scale, None, ALU.mult, None, [BPS[bank]], [B2])
        tt("dve", t2[:, :], t2[:, :], tab[:, 1, :], ALU.mult, [B2, Btb_], [B2])
        if split is None:
            tt("pool", dst, t1[:, :], t2[:, :], ALU.add, [B1, B2], [Bdst])
        else:
            for (r0, d_, Bd_) in split:
                tt("pool", d_[r0:r0 + 64, :], t1[r0:r0 + 64, :], t2[r0:r0 + 64, :], ALU.add, [B1, B2], [Bd_])

    def proj_v(col0, ncol, dst_fn):
        for t16 in range(16):
            bank = mbank()
            for c in range(8):
                mm(PS[bank][:, 0:ncol], hT[:, c, t16 * 128:(t16 + 1) * 128], Wbf[:, c, col0:col0 + ncol], c == 0, c == 7,
                   BWr(col0, ncol) + [BhT[t16]], [BPS[bank]])
            dst_fn(t16, bank)

    def v_heads(t16, bank):
        pv4 = PS[bank][:, 0:256].rearrange("p (a h d) -> p a h d", a=2, h=2)
        cp("dve", Vv[:, t16, :, 0, 64:128], pv4[:, :, 0, :], [BPS[bank]], [BV[t16]])
        cp("act", Vv[:, t16, :, 1, 0:64], pv4[:, :, 1, :], [BPS[bank]], [BV[t16]])

    def v_ones():
        memset(Vv[:, :, :, 0, 0:64], 1.0, BV)
        memset(Vv[:, :, :, 1, 64:128], 1.0, BV)

    def gate_head(col0, h):
        if h % 2:
            return
        for tc in range(4):
            bank = mbank()
            proj_fm(col0 + 64 * h, 128, tc, bank)
            act(sg[:, tc * 512:(tc + 1) * 512], PS[bank][:, :], AF.Silu, [BPS[bank]], [Bsg[tc]])

    def evac_rows(ob, po, dst, Bdst, src0=None, engs=("dve", "act")):
        if src0 is None:
            src0 = 64 - po
        for b in range(2):
            cp(engs[b], dst[po + 32 * b:po + 32 * b + 32, :], PS[ob][src0 + 32 * b:src0 + 32 * b + 32, :], [BPS[ob]], [Bdst])

    def epilogue_norm(ob, h, qc, light_act=False):
        po = 64 * (h % 2)
        cs = slice(qc * 512, (qc + 1) * 512)
        st = {}

        def early():
            st["t1"] = tf(); st["t2"] = tf()
            t1, B1 = st["t1"]; t2, B2 = st["t2"]
            if light_act:
                recip(t1[po:po + 64, :], PS[ob][po:po + 64, :], [BPS[ob]], [B1])
                evac_rows(ob, po, t2, B2, engs=("dve", "dve"))
            else:
                recip_act(t1[po:po + 64, :], PS[ob][po:po + 64, :], [BPS[ob]], [B1])
                evac_rows(ob, po, t2, B2, engs=("dve", "dve"))

        def late():
            t1, B1 = st["t1"]; t2, B2 = st["t2"]
            tt("pool", t2[po:po + 64, :], t2[po:po + 64, :], sg[po:po + 64, cs], ALU.mult, [B2, Bsg[qc]], [B2])
            tt("dve", mixT[po:po + 64, h // 2, cs], t1[po:po + 64, :], t2[po:po + 64, :], ALU.mult, [B1, B2], [Bmix[h // 2][qc]])
        return early, late

    class Pipe:
        def __init__(self):
            self.items = []

        def add(self, A, B, C, post=None):
            self.items.append((A, B, C, post))

        def run(self, skew=2, pskew=3):
            n = len(self.items)
            late = {}
            for t in range(n + skew + pskew + 1):
                if t < n:
                    self.items[t][0]()
                    self.items[t][1]()
                j = t - skew
                if 0 <= j < n:
                    self.items[j][2]()
                    if self.items[j][3]:
                        pa, pb = self.items[j][3]
                        pa()
                        late.setdefault(min(t + pskew, n + skew + pskew), []).append(pb)
                for f in late.pop(t, []):
                    f()

    def attn_tiles(pipe, ob, qa, qbufs, kfn, kbs, bias_fn, vfn, M, rows=128, maskfn=None, post=None, crange=None):
        for i, kb in enumerate(kbs):
            sbk = sbank()
            P_, BP_ = pbuf()
            c0, c1 = crange(kb) if crange is not None else (0, 512)

            def A(kb=kb, sbk=sbk, c0=c0, c1=c1):
                ka, kbufs = kfn(kb)
                bl = bias_fn(kb)
                mm(PS[sbk][0:rows, c0:c1], ka, qa[:, c0:c1], True, len(bl) == 0, qbufs + kbufs, [BPS[sbk]])
                for bi, (bl_l, bl_r, bl_b) in enumerate(bl):
                    mm(PS[sbk][0:rows, c0:c1], bl_l, bl_r[:, c0:c1], False, bi == len(bl) - 1, bl_b, [BPS[sbk]])

            def B(kb=kb, sbk=sbk, P_=P_, BP_=BP_, c0=c0, c1=c1):
                act(P_[0:rows, c0:c1], PS[sbk][0:rows, c0:c1], AF.Exp, [BPS[sbk]], [BP_])
                if maskfn is not None:
                    ma, mb = maskfn(kb)
                    tt("dve", P_[0:rows, c0:c1], P_[0:rows, c0:c1], ma[:, c0:c1], ALU.mult, [BP_] + mb, [BP_])

            def C(kb=kb, i=i, P_=P_, BP_=BP_, c0=c0, c1=c1):
                va, vbufs = vfn(kb)
                mm(PS[ob][0:M, c0:c1], va, P_[0:rows, c0:c1], i == 0, i == len(kbs) - 1, vbufs + [BP_], [BPS[ob]])

            pipe.add(A, B, C, post if i == len(kbs) - 1 else None)

    def causal_range(qc):
        def f(kb):
            d = kb - 4 * qc
            return (128 * d, 512) if d > 0 else (0, 512)
        return f

    def window_range(qc):
        def f(kb):
            d = kb - 4 * qc
            if d >= 0:
                return (128 * d, 512)
            return (0, 128 * (d + 5))
        return f

    def table_bias(name, qc, nblk):
        def f(kb):
            d0 = 4 * qc - kb + 3
            if d0 + 4 > nblk:
                d0 = nblk - 4
            return [(identb, cba(name, 0, 128, d0 * 128, d0 * 128 + 512), [BCB])]
        return f

    def zero_kz():
        for h in range(4):
            r0 = 64 * (1 - h % 2)
            for tc in range(4):
                memset(kB[h][r0:r0 + 64, tc * 512:(tc + 1) * 512], 0.0, [BkB[h][tc]])

    def store_mix(m):
        def f():
            if m == 3 and not debug:
                return
            for t16 in range(16):
                dma(mix_d[t16, :, 2 * m:2 * m + 2, :], mixT[:, :, t16 * 128:(t16 + 1) * 128],
                    [Bmix[0][t16 // 4], Bmix[1][t16 // 4]], [Bmixd[t16]], "mixst")
        flush_store()
        pend["store"] = f

    def zero_mixer(m):
        for c in range(2):
            for qc in range(4):
                memset(mixT[:, c, qc * 512:(qc + 1) * 512], 0.0, [Bmix[c][qc]])
        store_mix(m)

    def mixer_a(l):
        import os
        KSTOP = int(os.environ.get("KSTOP", "99"))
        load_pieces(l, "A")
        flush_store()
        if KSTOP <= 1:
            return zero_mixer(0)
        v_ones()
        zero_kz()
        proj_v(512, 256, v_heads)
        if KSTOP <= 2:
            return zero_mixer(0)
        for tc in range(4):
            tab, Btb_ = load_tabs(0, tc)
            cs = slice(tc * 512, (tc + 1) * 512)
            for pair in range(2):
                bank = mbank()
                proj_fm(128 * pair, 128, tc, bank)
                rope_evac(bank, qT[:, pair, cs], BqT[pair][tc], tab, Btb_, 0.125)
                bank = mbank()
                proj_fm(256 + 128 * pair, 128, tc, bank)
                rope_evac(bank, None, None, tab, Btb_, 1.0,
                          split=[(0, kB[2 * pair][:, cs], BkB[2 * pair][tc]), (64, kB[2 * pair + 1][:, cs], BkB[2 * pair + 1][tc])])
        pipe = Pipe()
        gate_head(768, 0)
        for h in range(4):
            pair, po = h // 2, 64 * (h % 2)
            for qc in range(4):
                ob = obank()

                def mask_a(kb, qc=qc):
                    d0 = min(4 * qc - kb + 3, 8)
                    return cba("Ca", 0, 128, d0 * 128, d0 * 128 + 512), [BCB]

                e_, l_ = epilogue_norm(ob, h, qc)

                def late(l_=l_, h=h, qc=qc):
                    l_()
                    if qc == 3 and h < 3:
                        gate_head(768, h + 1)
                post = (e_, late)
                attn_tiles(pipe, ob, qT[:, pair, qc * 512:(qc + 1) * 512], [BqT[pair][qc]],
                           lambda kb, h=h: (kB[h][:, kb * 128:(kb + 1) * 128], [BkB[h][kb // 4]]),
                           list(range(4 * qc + 4)), lambda kb: [],
                           lambda kb, h=h: (Vaug[:, kb, h, :], [BV[kb]]), 128, maskfn=mask_a, post=post, crange=causal_range(qc))
        prefetch_w(l, "A", "B")
        pipe.run(skew=3)
        store_mix(0)

    def mixer_d(l):
        arena_reset()
        Ssum, BSs = carve(512, F32, "Ssum"); Ssb, BSsb = carve(512, BF16, "Ssb")
        spb = [carve(512, BF16, "spb%d" % i) for i in range(3)]
        Ssb2 = [(Ssb, BSsb), carve(512, BF16, "Ssb1")]
        load_pieces(l, "D")
        flush_store()
        arena_gate()
        zero_kz()
        proj_v(512, 256, v_heads)
        for tc in range(4):
            cs = slice(tc * 512, (tc + 1) * 512)
            for pair in range(2):
                bank = mbank()
                proj_fm(128 * pair, 128, tc, bank)
                act(qT[:, pair, cs], PS[bank][:, :], AF.Copy, [BPS[bank]], [BqT[pair][tc]], scale=0.125)
                bank = mbank()
                proj_fm(256 + 128 * pair, 128, tc, bank)
                cp("dve", kB[2 * pair][0:64, cs], PS[bank][0:64, :], [BPS[bank]], [BkB[2 * pair][tc]])
                cp("dve", kB[2 * pair + 1][64:128, cs], PS[bank][64:128, :], [BPS[bank]], [BkB[2 * pair + 1][tc]])
        negU = cba("negU"); negones = cba("negones")
        tiles = []
        for h in range(4):
            pair, po = h // 2, 64 * (h % 2)
            for qc in range(4):
                ob = obank()
                kbs = list(range(4 * qc + 3, -1, -1))
                for i, kb in enumerate(kbs):
                    tiles.append(dict(h=h, pair=pair, po=po, qc=qc, ob=ob, i=i, kb=kb, n=len(kbs), zb=sbank(), eb=mbank(),
                                      sp=spb[len(tiles) % 3], ssb=Ssb2[len(tiles) % 2], P=pbuf()))

        def stA(T):
            cs = slice(T["qc"] * 512, (T["qc"] + 1) * 512); ks_ = slice(T["kb"] * 128, (T["kb"] + 1) * 128)
            po, pair, zb = T["po"], T["pair"], T["zb"]
            diag = T["kb"] >= 4 * T["qc"]; d0 = 4 * T["qc"] - T["kb"] + 3
            rd = [BqT[pair][T["qc"]], BkB[T["h"]][T["kb"] // 4]]
            c0 = 128 * max(0, T["kb"] - 4 * T["qc"]); T["c0"] = c0
            qs = slice(T["qc"] * 512 + c0, (T["qc"] + 1) * 512)
            mm(PS[zb][:, c0:], kB[T["h"]][:, ks_], qT[:, pair, qs], True, not diag, rd, [BPS[zb]])
            if diag:
                mm(PS[zb][:, c0:], identb, cba("Ts", 0, 128, d0 * 128 + c0, d0 * 128 + 512), False, True, [BCB], [BPS[zb]])

        def stB(T):
            t1, B1 = tf()
            c0 = T["c0"]
            act(t1[:, c0:], PS[T["zb"]][:, c0:], AF.Exp, [BPS[T["zb"]]], [B1])
            sp_, Bsp_ = T["sp"]
            act(sp_[:, c0:], t1[:, c0:], AF.Ln, [B1], [Bsp_], bias=1.0)

        def stC(T, prev):
            cs = slice(T["qc"] * 512, (T["qc"] + 1) * 512); ks_ = slice(T["kb"] * 128, (T["kb"] + 1) * 128)
            po, pair, eb = T["po"], T["pair"], T["eb"]
            diag = T["kb"] >= 4 * T["qc"]; d0 = 4 * T["qc"] - T["kb"] + 3
            rd = [BqT[pair][T["qc"]], BkB[T["h"]][T["kb"] // 4]]
            sp_, Bsp_ = T["sp"]
            c0 = T["c0"]
            qs = slice(T["qc"] * 512 + c0, (T["qc"] + 1) * 512)
            mm(PS[eb][:, c0:], kB[T["h"]][:, ks_], qT[:, pair, qs], True, False, rd, [BPS[eb]])
            if diag:
                mm(PS[eb][:, c0:], identb, cba("Ts", 0, 128, d0 * 128 + c0, d0 * 128 + 512), False, False, [BCB], [BPS[eb]])
            mm(PS[eb][:, c0:], negU, sp_[:, c0:], False, T["i"] == 0, [BCB, Bsp_], [BPS[eb]])
            if T["i"] > 0:
                ssb_, Bssb_ = prev["ssb"]
                mm(PS[eb][:, c0:], negones, ssb_[:, c0:], False, True, [BCB, Bssb_], [BPS[eb]])
            P_, BP_ = T["P"]
            act(P_[:, c0:], PS[eb][:, c0:], AF.Exp, [BPS[eb]], [BP_])

        def stU(T):
            if T["i"] == T["n"] - 1:
                return
            sp_, Bsp_ = T["sp"]; ssb_, Bssb_ = T["ssb"]
            c0 = T["c0"]
            if T["i"] == 0:
                if c0 > 0:
                    memset(Ssum[:, 0:c0], 0.0, [BSs])
                cp("dve", Ssum[:, c0:], sp_[:, c0:], [Bsp_], [BSs])
            else:
                tt("dve", Ssum[:, c0:], Ssum[:, c0:], sp_[:, c0:], ALU.add, [BSs, Bsp_], [BSs])
            cp("dve", ssb_, Ssum, [BSs], [Bssb_])

        def stE(T):
            P_, BP_ = T["P"]
            ob, h, kb = T["ob"], T["h"], T["kb"]
            c0 = T["c0"]
            mm(PS[ob][:, c0:], Vaug[:, kb, h, :], P_[:, c0:], T["i"] == 0, T["i"] == T["n"] - 1, [BV[kb], BP_], [BPS[ob]])
            if T["i"] == T["n"] - 1:
                cs = slice(T["qc"] * 512, (T["qc"] + 1) * 512)
                t3, B3 = tf()
                po = T["po"]
                evac_rows(ob, po, t3, B3, engs=("dve", "dve"))
                tt("pool", mixT[po:po + 64, T["pair"], cs], t3[po:po + 64, :], sg[po:po + 64, cs], ALU.mult, [B3, Bsg[T["qc"]]], [Bmix[T["pair"]][T["qc"]]])
                if T["qc"] == 3 and h < 3:
                    gate_head(768, h + 1)

        gate_head(768, 0)
        prefetch_w(l, "D", "O")
        n = len(tiles)
        for t in range(n + 2):
            if t < n:
                stA(tiles[t]); stB(tiles[t])
            if 0 <= t - 1 < n:
                stC(tiles[t - 1], tiles[t - 2] if t - 2 >= 0 else None)
            if t < n:
                stU(tiles[t])
            if 0 <= t - 2 < n:
                stE(tiles[t - 2])
        store_mix(3)

    qB = [qT[:, 0, :], qT[:, 1, :], qx[:, 0, :], qx[:, 1, :]]
    kB = [kT[:, 0, :], kT[:, 1, :], kx[:, 0, :], kx[:, 1, :]]
    BqB = [BqT[0], BqT[1], BqX[0], BqX[1]]
    BkB = [BkT[0], BkT[1], BkX[0], BkX[1]]

    def mixer_b(l):
        arena_reset()
        uq_st, Buqst = carve(768, F32, "uq_st"); uq_st = uq_st.rearrange("p (c n) -> p c n", c=2)
        ukv_st, Bukvst = carve(512, F32, "ukv_st")
        Wuq, BWuq = carve(768, BF16, "Wuq"); Wuq = Wuq.rearrange("p (c n) -> p c n", c=2)
        Wuqs, _ = carve(768, BF16, "Wuqs"); Wuqs = Wuqs.rearrange("p (c n) -> p c n", c=2)
        Wkp, BWkv = carve(384, BF16, "Wkp"); Wkp = Wkp.rearrange("p (h n) -> p h n", h=4)
        Wv, _ = carve(256, BF16, "Wv")
        Wkr, BWkr = carve(768, BF16, "Wkr"); Wkr = Wkr.rearrange("p (c n) -> p c n", c=8)
        Wkrs, _ = carve(768, BF16, "Wkrs"); Wkrs = Wkrs.rearrange("p (c n) -> p c n", c=8)
        gqk, Bgqk = carve(4, F32, "gqk")
        cqn, Bcqn = carve(1536, BF16, "cqn"); cqn = cqn.rearrange("p (c n) -> p c n", c=3)
        krr, Bkrr = carve(SEQ, BF16, "krr")
        load_pieces(l, "B")
        flush_store()
        arena_gate()
        dma(gqk[:, 0:2], gq_d[l].rearrange("(c p) -> p c", p=128), [], [Bgqk], "cst3", allow_slow_non_contiguous=True)
        dma(gqk[:, 2:3], gkv_d[l].rearrange("(c p) -> p c", p=128), [], [Bgqk], "cst3", allow_slow_non_contiguous=True)
        dma(uq_st, wuq_d[l].rearrange("(c p) n -> p c n", p=128), [], [Buqst], "cst3")
        dma(ukv_st, wukv_d[l], [], [Bukvst], "cst3")
        tt("pool", Wuq, uq_st, bc_last(gqk[:, 0:2], 384), ALU.mult, [Buqst, Bgqk], [BWuq])
        cp("pool", Wuqs, Wuq, [BWuq], [BWuq])
        for hh in range(4):
            b = hh * 96 + 64
            cp("pool", Wuqs[:, :, b:b + 16], Wuq[:, :, b + 16:b + 32], [BWuq], [BWuq])
            cp("pool", Wuqs[:, :, b + 16:b + 32], Wuq[:, :, b:b + 16], [BWuq], [BWuq])
        memset(Wkp, 0.0, [BWkv])
        ukv4 = ukv_st.rearrange("p (h d) -> p h d", h=4)
        ts("dve", Wkp[:, :, 0:64], ukv4[:, :, 0:64], gqk[:, 2:3], None, ALU.mult, None, [Bukvst, Bgqk], [BWkv])
        ts("dve", Wv.rearrange("p (h d) -> p h d", h=4), ukv4[:, :, 64:128], gqk[:, 2:3], None, ALU.mult, None, [Bukvst, Bgqk], [BWkv])
        memset(Wkr, 0.0, [BWkr]); memset(Wkrs, 0.0, [BWkr])
        cp("pool", Wkr[:, :, 64:96], Wbf[:, :, 384:416], BWr(384, 32), [BWkr])
        cp("pool", Wkrs[:, :, 64:80], Wbf[:, :, 400:416], BWr(384, 32), [BWkr])
        cp("pool", Wkrs[:, :, 80:96], Wbf[:, :, 384:400], BWr(384, 32), [BWkr])
        v_ones()
        sc = 96.0 ** -0.5
        for tc in range(4):
            cs = slice(tc * 512, (tc + 1) * 512)
            tab, Btb_ = load_tabs(1, tc)
            banks = [mbank(), mbank(), sbank()]
            sq = []
            for j in range(3):
                proj_fm(128 * j, 128, tc, banks[j])
                t_, B_ = tf()
                act(t_[:, :], PS[banks[j]][:, :], AF.Square, [BPS[banks[j]]], [B_])
                sq.append((t_, B_))
            ssq = obank(); ssk = obank()
            mm(PS[ssq][:, :], cfa("ones"), sq[0][0][:, :], True, False, [Bcf, sq[0][1]], [BPS[ssq]])
            mm(PS[ssq][:, :], cfa("ones"), sq[1][0][:, :], False, True, [Bcf, sq[1][1]], [BPS[ssq]])
            mm(PS[ssk][:, :], cfa("ones"), sq[2][0][:, :], True, True, [Bcf, sq[2][1]], [BPS[ssk]])
            for (sbk, denom, js) in ((ssq, 256.0, (0, 1)), (ssk, 128.0, (2,))):
                r_, Br_ = tf()
                act(r_[:, :], PS[sbk][:, :], AF.Ln, [BPS[sbk]], [Br_], scale=1.0 / denom, bias=EPS)
                act(r_[:, :], r_[:, :], AF.Exp, [Br_], [Br_], scale=-0.5)
                for j in js:
                    tt("dve", cqn[:, j, :], PS[banks[j]][:, :], r_[:, :], ALU.mult, [BPS[banks[j]], Br_], [Bcqn])
            b1 = mbank(); b2 = mbank()
            for (bank, W_) in ((b1, Wkr), (b2, Wkrs)):
                for c in range(8):
                    mm(PS[bank][0:96, :], W_[:, c, :], hT[:, c, cs], c == 0, c == 7, [BWkr] + BhT[4 * tc:4 * tc + 4], [BPS[bank]])
            t1, B1 = tf(); t2, B2 = tf()
            tt("dve", t1[64:96, :], PS[b1][64:96, :], tab[64:96, 0, :], ALU.mult, [BPS[b1], Btb_], [B1])
            tt("dve", t2[64:96, :], PS[b2][64:96, :], tab[64:96, 1, :], ALU.mult, [BPS[b2], Btb_], [B2])
            tt("pool", krr[64:96, cs], t1[64:96, :], t2[64:96, :], ALU.add, [B1, B2], [Bkrr])
            for t4 in range(4):
                t16 = 4 * tc + t4
                bank = mbank()
                mm(PS[bank][:, 0:256], cqn[:, 2, t4 * 128:(t4 + 1) * 128], Wv, True, True, [Bcqn, BWkv], [BPS[bank]])
                v_heads(t16, bank)
            for hh in range(4):
                b1 = mbank(); b2 = mbank()
                for (bank, W_) in ((b1, Wuq), (b2, Wuqs)):
                    for c in range(2):
                        mm(PS[bank][0:96, :], W_[:, c, hh * 96:(hh + 1) * 96], cqn[:, c, :], c == 0, c == 1, [BWuq, Bcqn], [BPS[bank]])
                act(qB[hh][0:64, cs], PS[b1][0:64, :], AF.Copy, [BPS[b1]], [BqB[hh][tc]], scale=sc)
                t1, B1 = tf(); t2, B2 = tf()
                stt("dve", t1[64:96, :], PS[b1][64:96, :], sc, tab[64:96, 0, :], ALU.mult, ALU.mult, [BPS[b1], Btb_], [B1])
                stt("dve", t2[64:96, :], PS[b2][64:96, :], sc, tab[64:96, 1, :], ALU.mult, ALU.mult, [BPS[b2], Btb_], [B2])
                tt("pool", qB[hh][64:96, cs], t1[64:96, :], t2[64:96, :], ALU.add, [B1, B2], [BqB[hh][tc]])
                bank = mbank()
                mm(PS[bank][0:96, :], Wkp[:, hh, :], cqn[:, 2, :], True, True, [BWkv, Bcqn], [BPS[bank]])
                cp("dve", kB[hh][0:64, cs], PS[bank][0:64, :], [BPS[bank]], [BkB[hh][tc]])
                cp("pool", kB[hh][64:96, cs], krr[64:96, cs], [Bkrr], [BkB[hh][tc]])
        if debug and l == 0:
            dma(dbg_d[0], qB[0], BqB[0], [], "dbg")
            dma(dbg_d[1], kB[0], BkB[0], [], "dbg")
            dma(dbg_d[2], qB[3], BqB[3], [], "dbg")
            dma(dbg_d[3], kB[3], BkB[3], [], "dbg")
        pipe = Pipe()
        gate_head(416, 0)
        for h in range(4):
            for qc in range(4):
                ob = obank()
                tb_ = table_bias("Tc", qc, 7)

                e_, l_ = epilogue_norm(ob, h, qc, light_act=True)

                def late(l_=l_, h=h, qc=qc):
                    l_()
                    if qc == 3 and h < 3:
                        gate_head(416, h + 1)
                post = (e_, late)
                attn_tiles(pipe, ob, qB[h][0:96, qc * 512:(qc + 1) * 512], [BqB[h][qc]],
                           lambda kb, h=h: (kB[h][0:96, kb * 128:(kb + 1) * 128], [BkB[h][kb // 4]]),
                           list(range(4 * qc + 4)),
                           lambda kb, qc=qc, tb_=tb_: (tb_(kb) if kb >= 4 * qc else []),
                           lambda kb, h=h: (Vaug[:, kb, h, :], [BV[kb]]), 128, post=post, crange=causal_range(qc))
        prefetch_w(l, "B", "C")
        pipe.run(skew=3)
        store_mix(1)

    def mixer_c(l):
        arena_reset()
        V76s, BVs = carve(16 * 128, BF16, "V76s"); V76s = V76s.rearrange("p (t n) -> p t n", t=16)
        V76w, BVw = carve(16 * 128, BF16, "V76w"); V76w = V76w.rearrange("p (t n) -> p t n", t=16)
        vcT = qx[:, 1, :]
        nselT, BnselT = carve(SEQ, BF16, "nselT")
        posT, BposT = carve(64, F32, "posT")
        W1b = [carve(1024, BF16, "W1b%d" % i) for i in range(2)]
        w2st, Bw2st = carve(128, F32, "w2st")
        w2b, Bw2b = carve(256, BF16, "w2b")
        hidS, BhidS = carve(256, BF16, "hidS")
        kcc, Bkcc = carve(128, BF16, "kcc"); vcc, Bvcc = carve(64, BF16, "vcc")
        xp_off = ar["ptr"]
        Xp, BXp = carve(32 * 127, BF16, "Xp")
        pcs = [(2144, 64, 704), (2272, 64, 768), (2336, 12, 832), (2348, 128, 844), (2476, 128, 972), (2016, 64, 640),
               (1696, 128, 0), (1824, 128, 128), (1952, 64, 256), (1952, 64, 320), (2080, 64, 384), (2080, 64, 448),
               (2208, 64, 512), (2208, 64, 576)]
        load_pieces(l, "C")
        flush_store()
        arena_gate()
        qz = [qT[:, 0, :], qT[:, 1, :], kT[:, 0, :], kT[:, 1, :]]
        Bqz = [BqT[0], BqT[1], BkT[0], BkT[1]]
        for h_ in range(4):
            r0_ = 64 * (1 - h_ % 2)
            for tc_ in range(4):
                memset(qz[h_][r0_:r0_ + 64, tc_ * 512:(tc_ + 1) * 512], 0.0, [Bqz[h_][tc_]])
        memset(nselT[:, :], 0.0, [BnselT])
        KC2, KS2, KW2 = kx[:, 0, :], kx[:, 1, :], qx[:, 0, :]
        BKC2, BKS2, BKW2 = BkX[0], BkX[1], BqX[0]
        memset(V76s[:, :, 64:128], 1.0, [BVs]); memset(V76w[:, :, 64:128], 1.0, [BVw])

        def v_c(t16, bank):
            cp("dve", V76s[:, t16, 0:64], PS[bank][:, 0:64], [BPS[bank]], [BVs])
            cp("dve", V76w[:, t16, 0:64], PS[bank][:, 64:128], [BPS[bank]], [BVw])
        proj_v(704, 128, v_c)
        for tc in range(4):
            tab, Btb_ = load_tabs(0, tc)
            cs = slice(tc * 512, (tc + 1) * 512)
            for pair in range(2):
                bank = mbank()
                proj_fm(128 * pair, 128, tc, bank)
                rope_evac(bank, None, None, tab, Btb_, 0.125,
                          split=[(0, qz[2 * pair][:, cs], Bqz[2 * pair][tc]), (64, qz[2 * pair + 1][:, cs], Bqz[2 * pair + 1][tc])])
            for (c0, dst, Bd) in ((256, KC2, BKC2), (384, KS2, BKS2), (512, KW2, BKW2)):
                bank = mbank()
                proj_fm(c0, 128, tc, bank)
                rope_evac(bank, dst[:, cs], Bd[tc], tab, Btb_, 1.0)
            bank = mbank()
            proj_fm(640, 64, tc, bank)
            cp("dve", vcT[0:64, cs], PS[bank][0:64, :], [BPS[bank]], [BqX[1][tc]])

        def compress(srcT, Bsrc, pos_dram, w1_d, w2_d, is_k):
            dma(posT[0:64, 0:32], pos_dram[l].rearrange("l d -> d l"), [], [BposT], "cst4", allow_slow_non_contiguous=True)
            base = srcT[0:64, 0:1]
            in0 = bass.AP(base.tensor, base.offset, [list(base.ap[0]), [1, 32], [16, 127]])
            tt("pool", Xp.rearrange("p (l n) -> p l n", l=32)[0:64], in0, bc_last(posT[0:64, 0:32], 127), ALU.add, Bsrc + [BposT], [BXp])
            Xp3 = Xp.rearrange("p (l n) -> p l n", l=32)
            hb = [mbank(), mbank()]
            for pc in range(8):
                si = nxt("st", 2)
                dma(stage_flat(si, 0, 64, 1024).rearrange("p (l h) -> p l h", l=4),
                    w1_d[l, pc * 256:(pc + 1) * 256, :].rearrange("(l d) h -> d l h", d=64), [], [Bst[si]], "st%d" % si)
                wb, Bwb = W1b[pc % 2]
                cp("act" if pc % 2 else "dve", wb[0:64, :], stage_flat(si, 0, 64, 1024), [Bst[si]], [Bwb])
                wb3 = wb.rearrange("p (l h) -> p l h", l=4)
                for li in range(4):
                    lidx = pc * 4 + li
                    for half in range(2):
                        mm(PS[hb[half]][:, 0:127], wb3[0:64, li, half * 128:(half + 1) * 128], Xp3[0:64, lidx, :],
                           lidx == 0, lidx == 31, [Bwb, BXp], [BPS[hb[half]]])
            hs3 = hidS.rearrange("p (c n) -> p c n", c=2)
            for half in range(2):
                act(hs3[:, half, 0:127], PS[hb[half]][:, 0:127], AF.Silu, [BPS[hb[half]]], [BhidS])
            dma(w2st.rearrange("p (c n) -> p c n", c=2), w2_d[l].rearrange("(c p) n -> p c n", p=128), [], [Bw2st], "cst4")
            w2b3 = w2b.rearrange("p (c n) -> p c n", c=2)
            cp("pool", w2b3[:, :, 0:64], w2st.rearrange("p (c n) -> p c n", c=2), [Bw2st], [Bw2b])
            cp("pool", w2b3[:, :, 64:128], w2st.rearrange("p (c n) -> p c n", c=2), [Bw2st], [Bw2b])
            bank = mbank()
            if is_k:
                for c in range(2):
                    mm(PS[bank][:, 0:127], w2b3[:, c, :], hs3[:, c, 0:127], c == 0, c == 1, [Bw2b, BhidS], [BPS[bank]])
                cp("dve", kcc[:, 0:127], PS[bank][:, 0:127], [BPS[bank]], [Bkcc])
            else:
                for c in range(2):
                    mm(PS[bank][0:127, 0:64], hs3[:, c, 0:127], w2b3[:, c, 0:64], c == 0, c == 1, [Bw2b, BhidS], [BPS[bank]])
                cp("dve", vcc[0:127, :], PS[bank][0:127, 0:64], [BPS[bank]], [Bvcc])
        compress(KC2, BKC2, posk_d, kw1_d, kw2_d, True)
        compress(vcT, BqX[1], posv_d, vw1_d, vw2_d, False)
        save_ptr = ar["ptr"]; ar["ptr"] = xp_off
        impm, Bimpm = carve(128, F32, "impm"); rank, Brank = carve(128, F32, "rank"); cmp3, Bcmp3 = carve(1024, F32, "cmp3")
        ar["ptr"] = max(save_ptr, ar["ptr"])
        memset(small[:, 62:63], 0.0, [BXp, Bimpm, Brank, Bcmp3])

        c3 = cmp3.rearrange("p (a b) -> p a b", a=32)
        def select_blocks(qc, ib):
            cs = slice(qc * 512, (qc + 1) * 512)
            impq, Bimpq = tf()
            act(impq[0:32, :], PS[ib][0:32, :], AF.Copy, [BPS[ib]], [Bimpq])
            tb_ = mbank()
            for t4 in range(4):
                op("pe", lambda e, t4=t4, tb_=tb_, impq=impq: e.transpose(PS[tb_][:, t4 * 32:(t4 + 1) * 32], impq[0:32, t4 * 128:(t4 + 1) * 128], cfa("identf", 0, 32, 0, 32)),
                   [Bimpq, Bcf], [BPS[tb_]])
            tt("dve", impm, PS[tb_][:, 0:128], cfa("NF", 0, 128, qc * 128, (qc + 1) * 128), ALU.mult, [BPS[tb_], Bcf], [Bimpm])
            tt("dve", impm, impm, cfa("ADDT", 0, 128, qc * 128, (qc + 1) * 128), ALU.add, [Bimpm, Bcf], [Bimpm])
            for t4 in range(4):
                x_ = impm[:, t4 * 32:t4 * 32 + 1]
                xj2 = bass.AP(x_.tensor, x_.offset, [list(x_.ap[0]), [0, 32], [1, 32]])
                xj = bass.AP(x_.tensor, x_.offset, [list(x_.ap[0]), [1, 32], [0, 32]])
                tt("dve", c3, xj2, xj, ALU.is_gt, [Bimpm], [Bcmp3])
                op("dve", lambda e, t4=t4: e.reduce_sum(out=rank[:, t4 * 32:(t4 + 1) * 32], in_=c3, axis=AX.X), [Bcmp3], [Brank])
            ts("dve", rank, rank, 15.5, None, ALU.is_ge, None, [Brank], [Brank])

        def select_part2(qc):
            cs = slice(qc * 512, (qc + 1) * 512)
            tb2 = mbank()
            for t4 in range(4):
                op("pe", lambda e, t4=t4, tb2=tb2: e.transpose(PS[tb2][0:32, t4 * 128:(t4 + 1) * 128], rank[:, t4 * 32:(t4 + 1) * 32], cfa("identf")),
                   [Brank, Bcf], [BPS[tb2]])
            cp("dve", nselT[0:32, cs], PS[tb2][0:32, :], [BPS[tb2]], [BnselT])

        items = [dict(qc=qc, h=h) for qc in range(4) for h in range(4)]
        pending_sel = []
        ibs = {}

        def cA(T):
            qc, h = T["qc"], T["h"]
            cs = slice(qc * 512, (qc + 1) * 512)
            sbk = sbank(); T["sbk"] = sbk
            mm(PS[sbk][0:127, :], kcc[:, 0:127], qz[h][:, cs], True, False, [Bkcc, Bqz[h][qc]], [BPS[sbk]])
            mm(PS[sbk][0:127, :], cba("ident", 0, 127, 0, 127), cba("Tcmp", 0, 127, qc * 512, (qc + 1) * 512), False, True, [BCB], [BPS[sbk]])
            T["E"] = pbuf()
            E_, BE_ = T["E"]
            act(E_[0:127, :], PS[sbk][0:127, :], AF.Exp, [BPS[sbk]], [BE_])

        def cC(T):
            E_, BE_ = T["E"]
            db = mbank()
            mm(PS[db][:, :], cba("onesb", 0, 127), E_[0:127, :], True, True, [BCB, BE_], [BPS[db]])
            r_, Br_ = tf()
            act(r_[:, :], PS[db][:, :], AF.Ln, [BPS[db]], [Br_], bias=1e-30)
            act(r_[:, :], r_[:, :], AF.Exp, [Br_], [Br_], scale=-1.0)
            tt("pool", E_[0:127, :], E_[0:127, :], r_[0:127, :], ALU.mult, [BE_, Br_], [BE_])

        def cE(T):
            qc, h = T["qc"], T["h"]
            Pn, BPn = T["E"]
            if h == 0:
                ibs[qc] = obank()
            ib = ibs[qc]
            mm(PS[ib][0:32, :], cba("OV", 0, 127), Pn[0:127, :], h == 0, h == 3, [BCB, BPn], [BPS[ib]])
            cb_ = mbank()
            mm(PS[cb_][0:64, :], vcc[0:127, 0:64], Pn[0:127, :], True, True, [Bvcc, BPn], [BPS[cb_]])
            oc_, Boc_ = tf()
            cp("act", oc_[0:64, :], PS[cb_][0:64, :], [BPS[cb_]], [Boc_])
            dma(ocmp_d[h * 4 + qc], oc_[0:64, :], [Boc_], [Bocmp[h * 4 + qc]], "ocst")
            if h == 3:
                select_blocks(qc, ib)
                pending_sel.append([3, qc])

        n_it = len(items)
        for t in range(n_it + 2):
            if t < n_it:
                cA(items[t])
            if 0 <= t - 1 < n_it:
                cC(items[t - 1])
            if 0 <= t - 2 < n_it:
                cE(items[t - 2])
            for ps_ in list(pending_sel):
                if ps_[0] == 0:
                    select_part2(ps_[1]); pending_sel.remove(ps_)
                else:
                    ps_[0] -= 1
        for ps_ in pending_sel:
            select_part2(ps_[1])
        pools["M"] = [7]; pools["O"] = [3, 4, 5, 6]
        pipe = Pipe()
        gate_head(844, 0)
        for h in range(4):
            pair, po = h // 2, 64 * (h % 2)
            for qc in range(4):
                cs = slice(qc * 512, (qc + 1) * 512)
                qa = qz[h][:, cs]; qb_ = [Bqz[h][qc]]
                osb = obank(); owb = obank()
                tbc = table_bias("Tc", qc, 7)

                def bias_s(kb, qc=qc, tbc=tbc, cs=cs):
                    bl = [(cba("EXPNEG", 0, 128, kb * 128, (kb + 1) * 128), nselT[:, cs], [BCB, BnselT])]
                    if kb >= 4 * qc:
                        bl += tbc(kb)
                    return bl

                def early(h=h, qc=qc, pair=pair, po=po, cs=cs, osb=osb, owb=owb):
                    acc, Bacc = xt[0][:, 0:512], Bxt[0]
                    dma(acc[po:po + 64, :], ocmp_d[h * 4 + qc], [Bocmp[h * 4 + qc]], [Bacc], "ocld")
                    w1_, Bw1_ = xt[1][:, 0:512], Bxt[1]; w2_, Bw2_ = xt[1][:, 512:1024], Bxt[1]; sgl, Bsgl = xt[0][:, 512:1024], Bxt[0]
                    gb_ = mbank()
                    proj_fm(832, 12, qc, gb_)
                    act(sgl[64:76, :], PS[gb_][0:12, :], AF.Exp, [BPS[gb_]], [Bsgl], scale=-1.0)
                    act(sgl[64:76, :], sgl[64:76, :], AF.Ln, [Bsgl], [Bsgl], bias=1.0)
                    act(sgl[64:76, :], sgl[64:76, :], AF.Exp, [Bsgl], [Bsgl], scale=-1.0)
                    recip_act(w1_[64:76, :], PS[osb][64:76, :], [BPS[osb]], [Bw1_])
                    tt("dve", w1_[64:76, :], w1_[64:76, :], sgl[64:76, :], ALU.mult, [Bw1_, Bsgl], [Bw1_])
                    recip_act(w2_[64:76, :], PS[owb][64:76, :], [BPS[owb]], [Bw2_])
                    tt("dve", w2_[64:76, :], w2_[64:76, :], sgl[64:76, :], ALU.mult, [Bw2_, Bsgl], [Bw2_])
                    for i, (w_, Bw_) in enumerate(((sgl, Bsgl), (w1_, Bw1_), (w2_, Bw2_))):
                        ts("dve", w_[64:76, :], w_[64:76, :], cfa("SELCOL", 64, 76, 3 * h + i, 3 * h + i + 1), None, ALU.mult, None, [Bw_, Bcf], [Bw_])

                def late(h=h, qc=qc, pair=pair, po=po, cs=cs, osb=osb, owb=owb):
                    acc, Bacc = xt[0][:, 0:512], Bxt[0]
                    w1_, Bw1_ = xt[1][:, 0:512], Bxt[1]; w2_, Bw2_ = xt[1][:, 512:1024], Bxt[1]; sgl, Bsgl = xt[0][:, 512:1024], Bxt[0]
                    bbs = [mbank(), sbank(), sbank()]
                    for i, (wsrc, Bws) in enumerate(((sgl[64:76, :], Bsgl), (w1_[64:76, :], Bw1_), (w2_[64:76, :], Bw2_))):
                        mm(PS[bbs[i]][:, :], cfa("ones", 64, 76, 0, 128), wsrc, True, True, [Bcf, Bws], [BPS[bbs[i]]])
                    for i, obr in enumerate((None, osb, owb)):
                        bb = bbs[i]
                        if obr is None:
                            tt("dve", acc[po:po + 64, :], PS[bb][po:po + 64, :], acc[po:po + 64, :], ALU.mult, [Bacc, BPS[bb]], [Bacc])
                        else:
                            t_, Bt_ = tf()
                            evac_rows(obr, po, t_, Bt_, src0=0)
                            tt("dve", t_[po:po + 64, :], PS[bb][po:po + 64, :], t_[po:po + 64, :], ALU.mult, [Bt_, BPS[bb]], [Bt_])
                            tt("pool", acc[po:po + 64, :], acc[po:po + 64, :], t_[po:po + 64, :], ALU.add, [Bacc, Bt_], [Bacc])
                    tt("dve", mixT[po:po + 64, pair, cs], acc[po:po + 64, :], sg[po:po + 64, cs], ALU.mult, [Bacc, Bsg[qc]], [Bmix[pair][qc]])
                    if qc == 3 and h < 3:
                        gate_head(844, h + 1)
                post = (early, late)
                attn_tiles(pipe, osb, qa, qb_, lambda kb: (KS2[:, kb * 128:(kb + 1) * 128], [BKS2[kb // 4]]),
                           list(range(4 * qc + 4)), bias_s, lambda kb: (V76s[:, kb, :], [BVs]), 128, crange=causal_range(qc))
                attn_tiles(pipe, owb, qa, qb_, lambda kb: (KW2[:, kb * 128:(kb + 1) * 128], [BKW2[kb // 4]]),
                           list(range(max(0, 4 * qc - 4), 4 * qc + 4)), table_bias("Tw", qc, 11), lambda kb: (V76w[:, kb, :], [BVw]), 128, post=post, crange=window_range(qc))
        prefetch_w(l, "C", "D")
        pipe.run(skew=3, pskew=3)
        pools["M"] = [5, 6, 7]; pools["O"] = [3, 4]
        store_mix(2)

    def layer(l, xsrc, Bxsrc, xdst, Bxdst):
        for t16 in range(16):
            xi = t16 % 2
            dma(xt[xi][:], xsrc[t16 * 128:(t16 + 1) * 128, :], [Bxsrc], [Bxt[xi]], "xt%d" % xi)
            ssc, Bss = tcol()
            memset(ssc, 0.0, [Bss])
            junk = mlt[0][:].rearrange("p c n -> p (c n)")
            act(junk, xt[xi][:], AF.Square, [Bxt[xi], Bss], [Bmlt[0], Bss], accum_out=ssc)
            act(ssc, ssc, AF.Ln, [Bss], [Bss], scale=1.0 / DM, bias=EPS)
            act(ssc, ssc, AF.Exp, [Bss], [Bss], scale=-0.5)
            xn_, Bxn_ = (xn, BP[0]) if t16 % 2 == 0 else (mlt[1][:].rearrange("p c n -> p (c n)"), Bmlt[1])
            if t16 % 2 == 0:
                Bxn_ = BP[0]
            Bxn_l = [BP[0], BP[1]] if t16 % 2 == 0 else [Bmlt[1]]
            ts("dve", xn_, xt[xi][:], ssc, None, ALU.mult, None, [Bxt[xi], Bss], Bxn_l)
            bank = mbank()
            pv = PS[bank][:].bitcast(BF16)
            for c in range(8):
                op("pe", lambda e, c=c, pv=pv, xn_=xn_: e.transpose(pv[:, c * 128:(c + 1) * 128], xn_[:, c * 128:(c + 1) * 128], identb),
                   Bxn_l + [BCB], [BPS[bank]])
            cp("act" if t16 % 2 else "dve", hT[:, :, t16 * 128:(t16 + 1) * 128], pv.rearrange("p (c t) -> p c t", c=8), [BPS[bank]], [BhT[t16]])
        for m, (name, fn) in enumerate((("A", mixer_a), ("B", mixer_b), ("C", mixer_c), ("D", mixer_d))):
            if name in mixers:
                fn(l)
            else:
                zero_mixer(m)
        load_pieces(l, "O")
        flush_store()
        gpost = sg[:, 0:DM]; Bgpost = Bsg[0]
        dma(gpost, g_post_d[l:l + 1, :].partition_broadcast(128), [], [Bsg[0], Bsg[1]], "cst2")
        xb = [xt[0][:], xt[1][:], qx[:, 0, :].bitcast(F32), qx[:, 1, :].bitcast(F32)]
        Bxb = [[Bxt[0]], [Bxt[1]], BqX[0], BqX[1]]

        def xload(t16):
            xj = t16 % 4
            op("pool", lambda e: e.dma_start(out=xb[xj], in_=xsrc[t16 * 128:(t16 + 1) * 128, :]), [Bxsrc], Bxb[xj], dma="xs%d" % xj)
        for t_ in range(4):
            xload(t_)
        for t16 in range(16):
            xi = t16 % 2
            xj = t16 % 4
            dma(mlt[xi][:, 0:6, :], mix_d[t16, :, 0:6, :], [Bmixd[t16]], [Bmlt[xi]], "mlt%d" % xi)
            b0 = nxt("P5", 8); b1 = nxt("P5", 8)
            ssc, Bss = tcol(); ssc2, Bss2 = tcol()
            for half, bank in ((0, b0), (1, b1)):
                for c in range(8):
                    lt_ = mlt[xi][:, c, :] if c < 6 else mixT[:, c - 6, t16 * 128:(t16 + 1) * 128]
                    lb_ = Bmlt[xi] if c < 6 else Bmix[c - 6][t16 // 4]
                    mm(PS[bank][:, :], lt_, Wbf[:, c, half * 512:(half + 1) * 512], c == 0, c == 7,
                       BWr(half * 512, 512) + [lb_], [BPS[bank]])
            op("dve", lambda e, a=ssc: e.memset(a, 0.0), [], [Bss]); op("dve", lambda e, a=ssc2: e.memset(a, 0.0), [], [Bss2])
            act(xn[:, 0:512], PS[b0][:, :], AF.Square, [BPS[b0], Bss], [BP[0], Bss], accum_out=ssc)
            act(xn[:, 512:1024], PS[b1][:, :], AF.Square, [BPS[b1], Bss2], [BP[1], Bss2], accum_out=ssc2)
            tt("dve", ssc, ssc, ssc2, ALU.add, [Bss, Bss2], [Bss])
            act(ssc, ssc, AF.Ln, [Bss], [Bss], scale=1.0 / DM, bias=EPS)
            act(ssc, ssc, AF.Exp, [Bss], [Bss], scale=-0.5)
            for half, bank in ((0, b0), (1, b1)):
                hs = slice(half * 512, (half + 1) * 512)
                t_, Bt_ = tf()
                stt("dve", t_[:, :], PS[bank][:, :], ssc, gpost[:, hs], ALU.mult, ALU.mult, [BPS[bank], Bss, Bsg[0], Bsg[1]], [Bt_])
                tt("dve", xb[xj][:, hs], xb[xj][:, hs], t_[:, :], ALU.add, [Bt_] + Bxb[xj], Bxb[xj])
            op("pool", lambda e, t16=t16, xj=xj: e.dma_start(out=xdst[t16 * 128:(t16 + 1) * 128, :], in_=xb[xj]), Bxb[xj], [Bxdst], dma="xs%d" % xj)
            if t16 + 4 < 16:
                xload(t16 + 4)

    if nlayers == 1:
        layer(0, x_d, Buf("xin"), out_d, Bout)
    else:
        layer(0, x_d, Buf("xin"), x1_d, Bx1)
        layer(1, x1_d, Bx1, out_d, Bout)
    S.emit()
    if debug:
        print("sbuf bytes remaining", nc.sbuf_bytes_remaining, {k: len(v) for k, v in S.prog.items()})
    return nc


_NC_CACHE = {}


def _in_maps(inputs):
    cf = _build_consts()
    maps = []
    shared = {k: np.ascontiguousarray(np.asarray(v, dtype=np.float32)) for k, v in inputs.items() if k not in ("x", "positions")}
    x = np.asarray(inputs["x"], dtype=np.float32)
    pos = np.asarray(inputs["positions"]).astype(np.int32)
    for b in range(x.shape[0]):
        m = dict(shared)
        m["x"] = np.ascontiguousarray(x[b])
        m["positions"] = np.ascontiguousarray(pos[b:b + 1])
        m["cff"] = cf["_f"]
        m["cfb"] = cf["_b"]
        maps.append(m)
    return maps


def kernel(**inputs):
    if "nc" not in _NC_CACHE:
        _NC_CACHE["nc"] = build()
    nc = _NC_CACHE["nc"]
    res = run_bass_kernel_spmd(nc, _in_maps(inputs), core_ids=list(range(NCORES)))
    return np.stack([np.asarray(r["out"], dtype=np.float32) for r in res.results], axis=0)
```

```python
import math
import numpy as np
import concourse.bass as bass
import concourse.mybir as mybir
from concourse.bass_utils import run_bass_kernel_spmd

F32 = mybir.dt.float32
BF16 = mybir.dt.bfloat16
I32 = mybir.dt.int32
ALU = mybir.AluOpType
AF = mybir.ActivationFunctionType
AX = mybir.AxisListType

SAME_ENGINE_SYNC = True
SEQ = 2048
DM = 1024
DIN = 3628
NEG = -30000.0
EPS = 1e-6
NCORES = 8


class Buf:
    __slots__ = ("name", "w", "r")

    def __init__(self, name):
        self.name = name
        self.w = {}
        self.r = {}


class Sched:
    def __init__(self, nc):
        self.nc = nc
        self.eng = dict(pe=nc.tensor, act=nc.scalar, dve=nc.vector, pool=nc.gpsimd, sp=nc.sync)
        self.prog = {k: [] for k in self.eng}
        self.csem = {k: nc.alloc_semaphore("c_" + k) for k in ("pe", "act", "dve", "pool")}
        self.cnt = {k: 0 for k in self.csem}
        self.waited = {k: {} for k in self.eng}
        self.dsem = {}
        self.dtotal = {}

    def _dma_sem(self, key):
        if key not in self.dsem:
            self.dsem[key] = self.nc.alloc_semaphore("d_" + key)
            self.dtotal[key] = 0
        return self.dsem[key]

    def op(self, e, fn, reads=(), writes=(), dma=None):
        if e != "pe":
            writes = list(writes) + [b for b in reads if b.name.startswith("ps") and b not in writes]
        deps = {}

        def need(tok):
            k, sem, val = tok
            if k not in deps or deps[k][1] < val:
                deps[k] = (sem, val)

        for b in reads:
            for tok in b.w.values():
                need(tok)
        for b in writes:
            for tok in b.w.values():
                need(tok)
            for tok in b.r.values():
                need(tok)
        waits = []
        for k, (sem, val) in deps.items():
            if k in self.dtotal:
                val = self.dtotal[k]
            elif k == e and (e == "pe" or not SAME_ENGINE_SYNC):
                continue
            if self.waited[e].get(k, 0) < val:
                waits.append((sem, val))
                self.waited[e][k] = val
        if dma is not None:
            sem = self._dma_sem(dma)
            self.dtotal[dma] += 16
            tok = (dma, sem, self.dtotal[dma])
            inc = 16
        else:
            self.cnt[e] += 1
            tok = (e, self.csem[e], self.cnt[e])
            inc = 1
        self.prog[e].append((waits, fn, tok[1], inc))
        for b in reads:
            b.r[tok[0]] = tok
        for b in writes:
            b.w[tok[0]] = tok
        return tok

    def emit(self):
        nc = self.nc
        fin = []
        for k, sem in self.dsem.items():
            if self.waited["sp"].get(k, 0) < self.dtotal[k]:
                fin.append((sem, self.dtotal[k]))

        def run(eng, k):
            for waits, fn, sem, inc in self.prog[k]:
                for s, v in waits:
                    eng.wait_ge(s, v)
                fn(eng).then_inc(sem, inc)
            if k == "sp":
                for s, v in fin:
                    eng.wait_ge(s, v)

        with nc.Block() as block:
            @block.tensor
            def _(eng):
                run(eng, "pe")

            @block.scalar
            def _(eng):
                run(eng, "act")

            @block.vector
            def _(eng):
                run(eng, "dve")

            @block.gpsimd
            def _(eng):
                run(eng, "pool")

            @block.sync
            def _(eng):
                run(eng, "sp")


CF = {}


def _build_consts():
    if CF:
        return CF
    import ml_dtypes
    k = np.arange(128)[:, None]
    q = np.arange(128)[None, :]
    offs = {"f": 0, "b": 0}
    cols = {"f": [], "b": []}

    def add(kind, name, arr):
        arr = np.asarray(arr, np.float32)
        a = np.zeros((128, arr.shape[1]), np.float32)
        a[: arr.shape[0]] = arr
        CF[name] = (kind, offs[kind], arr.shape[1])
        cols[kind].append(a)
        offs[kind] += arr.shape[1]

    zero = np.zeros((128, 128)); neg = np.full((128, 128), NEG)
    tri = np.where(q >= k, 0.0, NEG); tris = np.where(q > k, 0.0, NEG); atri = np.where(q < k, 0.0, NEG)
    add("b", "ident", np.eye(128))
    add("b", "Tc", np.concatenate([neg, neg, neg, tri, zero, zero, zero], 1))
    add("b", "Ts", np.concatenate([neg, neg, neg, tris, zero, zero, zero], 1))
    add("b", "Tw", np.concatenate([neg] * 3 + [tri, zero, zero, zero, atri] + [neg] * 3, 1))
    bh, bl = [], []
    for dl in range(-3, 9):
        if dl < 0:
            bh.append(zero); continue
        d = 128 * dl + q - k
        c = ((d >= 0) & (d <= 128)).astype(np.float64) + ((d >= 0) & (d % 4 == 0) & (d <= 512)) + ((d >= 0) & (d % 16 == 0))
        bh.append(c)
    add("b", "Ca", np.concatenate(bh, 1))
    add("b", "negU", np.where(k >= q, -1.0, 0.0))
    add("b", "negones", -np.ones((128, 128)))
    add("b", "onesb", np.ones((128, 128)))
    n = np.arange(127)[:, None]; j = np.arange(32)[None, :]
    add("b", "OV", ((16 * n < 64 * j + 64) & (16 * n + 32 > 64 * j)).astype(np.float32))
    t = np.arange(SEQ)[None, :]
    add("b", "Tcmp", np.where(t >= 16 * n + 31, 0.0, NEG))
    e = np.zeros((32, 16 * 128))
    for kb in range(16):
        for kk in range(128):
            e[2 * kb + (kk >= 64), kb * 128 + kk] = NEG
    add("b", "EXPNEG", e)
    add("f", "ones", np.ones((128, 128)))
    add("f", "identf", np.eye(128))
    nf = np.zeros((128, 16 * 32)); ad = np.zeros((128, 16 * 32))
    for tt in range(16):
        for p in range(128):
            bt = (128 * tt + p) // 64
            for jj in range(32):
                forced = jj == 0 or jj == bt or jj == bt - 1
                fut = jj > bt
                nf[p, tt * 32 + jj] = 0.0 if (forced or fut) else 1.0
                ad[p, tt * 32 + jj] = 1e9 if forced else (-1e9 if fut else 0.0)
    add("f", "NF", nf); add("f", "ADDT", ad)
    sr = np.zeros((128, 12))
    for r in range(12):
        sr[64 + r, r] = 1.0
    add("f", "SELCOL", sr)
    p = np.arange(128)
    inv32 = (10000.0 ** (-(np.arange(32, dtype=np.float32)) / np.float32(32))).astype(np.float32)
    inv16 = (10000.0 ** (-(np.arange(16, dtype=np.float32)) / np.float32(16))).astype(np.float32)
    add("f", "inv32", inv32[p % 32][:, None]); add("f", "inv16", inv16[p % 16][:, None])
    add("f", "sgn32", np.where(p % 64 < 32, -1.0, 1.0)[:, None]); add("f", "sgn16", np.where(p % 32 < 16, -1.0, 1.0)[:, None])
    CF["_f"] = np.concatenate(cols["f"], 1); CF["_b"] = np.concatenate(cols["b"], 1)
    return CF


def build(nlayers=2, debug=False, mixers="ABCD"):
    cf = _build_consts()
    NF_, NB_ = cf["_f"].shape[1], cf["_b"].shape[1]
    nc = bass.Bass("TRN2", target_bir_lowering=False)
    S = Sched(nc)

    def din(name, shape, dt=F32):
        return nc.dram_tensor(name, list(shape), dt, kind="ExternalInput")

    x_d = din("x", (SEQ, DM)); pos_d = din("positions", (1, SEQ), I32)
    w_in_d = din("w_in", (2, DM, DIN)); w_out_d = din("w_out", (2, DM, DM))
    g_pre_d = din("g_pre", (2, DM)); g_post_d = din("g_post", (2, DM))
    gq_d = din("mla_g_q", (2, 256)); gkv_d = din("mla_g_kv", (2, 128))
    wuq_d = din("mla_w_uq", (2, 256, 384)); wukv_d = din("mla_w_ukv", (2, 128, 512))
    posk_d = din("nsa_pos_k", (2, 32, 64)); posv_d = din("nsa_pos_v", (2, 32, 64))
    kw1_d = din("nsa_k_w1", (2, 2048, 256)); kw2_d = din("nsa_k_w2", (2, 256, 64))
    vw1_d = din("nsa_v_w1", (2, 2048, 256)); vw2_d = din("nsa_v_w2", (2, 256, 64))
    cff_d = din("cff", (128, NF_)); cfb_d = din("cfb", (128, NB_))
    out_d = nc.dram_tensor("out", [SEQ, DM], F32, kind="ExternalOutput")
    x1_d = nc.dram_tensor("x1s", [SEQ, DM], F32)
    tab_d = nc.dram_tensor("tabs_dram", [4, 128, SEQ], F32)
    mix_d = nc.dram_tensor("mix_dram", [16, 128, 8, 128], BF16, kind="ExternalOutput" if debug else "Internal")
    dbg_d = nc.dram_tensor("dbg", [4, 128, SEQ], BF16, kind="ExternalOutput") if debug else None
    ocmp_d = nc.dram_tensor("ocmp_dram", [16, 64, 512], F32)
    Bocmp = [Buf("ocmp%d" % i) for i in range(16)]
    Bx1 = Buf("x1"); Btab = Buf("tab"); Bout = Buf("out"); Bmixd = [Buf("mixd%d" % i) for i in range(16)]

    _cnt = [0]

    def sb(shape, dt, name=None):
        _cnt[0] += 1
        return nc.alloc_sbuf_tensor(name or ("t%d" % _cnt[0]), list(shape), dt)

    def op(e, fn, reads=(), writes=(), dma=None):
        return S.op(e, fn, reads, writes, dma)

    def dma(out, in_, reads, writes, key, **kw):
        op("sp", lambda e: e.dma_start(out=out, in_=in_, **kw), reads, writes, dma=key)

    def mm(out, lhsT, rhs, start, stop, reads, writes):
        op("pe", lambda e: e.matmul(out, lhsT=lhsT, rhs=rhs, start=start, stop=stop, skip_group_check=True), reads, writes)

    def act(out, in_, func, reads, writes, **kw):
        op("act", lambda e: e.activation(out=out, in_=in_, func=func, **kw), reads, writes)

    def tt(eng, out, in0, in1, alu, reads, writes):
        op(eng, lambda e: e.tensor_tensor(out=out, in0=in0, in1=in1, op=alu), reads, writes)

    def ts(eng, out, in0, s1, s2, op0, op1, reads, writes):
        if op1 is None:
            op(eng, lambda e: e.tensor_single_scalar(out=out, in_=in0, scalar=s1, op=op0), reads, writes)
        else:
            op(eng, lambda e: e.tensor_scalar(out=out, in0=in0, scalar1=s1, scalar2=s2, op0=op0, op1=op1), reads, writes)

    def stt(eng, out, in0, scalar, in1, op0, op1, reads, writes):
        op(eng, lambda e: e.scalar_tensor_tensor(out=out, in0=in0, scalar=scalar, in1=in1, op0=op0, op1=op1), reads, writes)

    def cp(eng, out, in_, reads, writes):
        if eng == "act":
            op(eng, lambda e: e.activation(out=out, in_=in_, func=AF.Copy), reads, writes)
        else:
            op(eng, lambda e: e.tensor_copy(out=out, in_=in_), reads, writes)

    def recip(out, in_, reads, writes):
        op("dve", lambda e: e.reciprocal(out=out, in_=in_), reads, writes)

    def recip_act(out, in_, reads, writes):
        act(out, in_, AF.Ln, reads, writes)
        act(out, out, AF.Exp, writes, writes, scale=-1.0)

    def memset(ap, val, writes):
        op("pool", lambda e: e.memset(ap, val), [], writes)

    def bc2(ap2):
        a = ap2.ap
        return bass.AP(ap2.tensor, ap2.offset, [list(a[0]), [0, 2], list(a[1])])

    def bc_last(ap2, n):
        a = ap2.ap
        return bass.AP(ap2.tensor, ap2.offset, [list(a[0]), list(a[1]), [0, n]])

    rot = {}

    def nxt(kind, n):
        v = rot.get(kind, 0)
        rot[kind] = (v + 1) % n
        return v

    cff = sb((128, NF_), F32, "cff_sb"); Bcf = Buf("cf")
    dma(cff[:], cff_d[:], [], [Bcf], "cst")
    CB = sb((128, NB_), BF16, "CB"); BCB = Buf("CB")
    stage = [sb((128, 8, 128), F32, "stage%d" % i) for i in range(2)]; Bst = [Buf("st%d" % i) for i in range(2)]

    def stage_flat(si, r0, r1, n):
        return stage[si][r0:r1].rearrange("p c n -> p (c n)")[:, 0:n]

    o = 0
    while o < NB_:
        n = min(1024, NB_ - o)
        si = nxt("st", 2)
        dma(stage_flat(si, 0, 128, n), cfb_d[:, o:o + n], [], [Bst[si]], "st%d" % si)
        cp("pool", CB[:, o:o + n], stage_flat(si, 0, 128, n), [Bst[si]], [BCB])
        o += n

    def cfa(name, r0=0, r1=128, c0=0, c1=None):
        _, o_, n_ = cf[name]
        c1 = n_ if c1 is None else c1
        return cff[r0:r1, o_ + c0:o_ + c1]

    def cba(name, r0=0, r1=128, c0=0, c1=None):
        _, o_, n_ = cf[name]
        c1 = n_ if c1 is None else c1
        return CB[r0:r1, o_ + c0:o_ + c1]

    identb = cba("ident")
    PS = [nc.alloc_psum_tensor("ps%d" % i, [128, 512], F32) for i in range(8)]
    BPS = [Buf("ps%d" % i) for i in range(8)]

    hT = sb((128, 8, SEQ), BF16, "hT"); BhT = [Buf("hT%d" % i) for i in range(16)]
    mixT = sb((128, 2, SEQ), BF16, "mixT"); Bmix = [[Buf("mix%d_%d" % (c, qc)) for qc in range(4)] for c in range(2)]
    mlt = [sb((128, 8, 128), BF16, "mlt%d" % i) for i in range(2)]; Bmlt = [Buf("mlt%d" % i) for i in range(2)]
    Wbf = sb((128, 8, 1100), BF16, "Wbf"); BWp = [Buf("W%d" % i) for i in range(9)]

    def BWr(c0, n):
        return BWp[c0 // 128:(c0 + n - 1) // 128 + 1]
    pend = {"store": None}

    def flush_store():
        if pend["store"] is not None:
            f = pend["store"]; pend["store"] = None
            f()
    xt = [sb((128, DM), F32, "xt%d" % i) for i in range(2)]; Bxt = [Buf("xt%d" % i) for i in range(2)]
    gpre = sb((128, 16), F32, "gpre"); Bgpre = Buf("gpre")
    dma(gpre[:, 0:8], g_pre_d[0].rearrange("(c p) -> p c", p=128), [], [Bgpre], "cst", allow_slow_non_contiguous=True)
    dma(gpre[:, 8:16], g_pre_d[1].rearrange("(c p) -> p c", p=128), [], [Bgpre], "cst", allow_slow_non_contiguous=True)
    small = sb((128, 64), F32, "small"); Bsmall = [Buf("small%d" % i) for i in range(64)]
    tabs = [sb((128, 2, 512), F32, "tabs%d" % i) for i in range(2)]; Btabs = [Buf("tabs%d" % i) for i in range(2)]
    sg = sb((128, SEQ), F32, "sg"); Bsg = [Buf("sg%d" % i) for i in range(4)]
    qT = sb((128, 2, SEQ), BF16, "qT"); BqT = [[Buf("qT%d_%d" % (a, b)) for b in range(4)] for a in range(2)]
    kT = sb((128, 2, SEQ), BF16, "kT"); BkT = [[Buf("kT%d_%d" % (a, b)) for b in range(4)] for a in range(2)]
    qx = sb((128, 2, SEQ), BF16, "qx"); BqX = [[Buf("qx%d_%d" % (a, b)) for b in range(4)] for a in range(2)]
    kx = sb((128, 2, SEQ), BF16, "kx"); BkX = [[Buf("kx%d_%d" % (a, b)) for b in range(4)] for a in range(2)]
    Vaug = sb((128, 16, 4, 128), BF16, "Vaug"); BV = [Buf("V%d" % i) for i in range(16)]
    Vv = Vaug[:].rearrange("p t (a h) n -> p t a h n", h=2)
    Pbig = sb((128, 4, 512), BF16, "Pbig")
    Pt = [Pbig[:, i, :] for i in range(4)]; BP = [Buf("P%d" % i) for i in range(4)]
    xn = Pbig[:, 0:2, :].rearrange("p a n -> p (a n)")
    NT = 5
    tmpf = [sb((128, 512), F32, "tmpf%d" % i) for i in range(NT)]; Btmp = [Buf("tmpf%d" % i) for i in range(NT)]
    ARENA_N = 14 * 1024
    arena = sb((128, ARENA_N), BF16, "arena")
    ar = {"ptr": 0, "cur": [], "prev": []}

    def arena_reset():
        ar["ptr"] = 0
        ar["prev"] = ar["prev"] + ar["cur"]
        ar["cur"] = []

    def carve(nelem, dt, name):
        nb = nelem * (4 if dt in (F32, I32) else 2)
        nb = (nb + 63) // 64 * 64
        o_ = ar["ptr"]
        assert o_ + nb // 2 <= ARENA_N, ("arena overflow", name)
        ar["ptr"] += nb // 2
        a = arena[:, o_:o_ + nb // 2]
        if dt != BF16:
            a = a.bitcast(dt)
        B = Buf(name)
        ar["cur"].append(B)
        return a[:, 0:nelem], B

    def arena_gate():
        memset(small[:, 63:64], 0.0, ar["prev"] + ar["cur"])
        ar["prev"] = []

    def tcol():
        i = nxt("sm", 60)
        return small[:, i:i + 1], Bsmall[i]

    def tf():
        i = nxt("T", NT)
        return tmpf[i], Btmp[i]

    def pbuf():
        i = nxt("P", 4)
        return Pt[i], BP[i]

    pools = {"M": [5, 6, 7], "O": [3, 4]}

    def mbank():
        p = pools["M"]
        return p[nxt("M%d" % len(p), len(p))]

    def sbank():
        return nxt("S", 3)

    def obank():
        p = pools["O"]
        return p[nxt("O%d" % len(p), len(p))]

    TWO_PI = 2.0 * math.pi
    for c4 in range(4):
        cs = slice(c4 * 512, (c4 + 1) * 512)
        ki, Bki = xt[0][:, 0:512], Bxt[0]; pf, Bpf = xt[0][:, 512:1024], Bxt[0]
        kii = ki.bitcast(I32)
        dma(kii, pos_d[0:1, cs].partition_broadcast(128), [], [Bki], "tabp")
        cp("dve", pf[:], kii, [Bki], [Bpf])
        for ti, (invn, sgnn, phase) in enumerate([("inv32", None, math.pi / 2), ("inv32", "sgn32", 0.0),
                                                  ("inv16", None, math.pi / 2), ("inv16", "sgn16", 0.0)]):
            an, Ban = xt[1][:, 0:512], Bxt[1]; tb, Btb = xt[1][:, 512:1024], Bxt[1]
            ts("dve", an[:], pf[:], cfa(invn), phase, ALU.mult, ALU.add, [Bpf, Bcf], [Ban])
            ts("dve", tb[:], an[:], 1.0 / TWO_PI, None, ALU.mult, None, [Ban], [Btb])
            cp("dve", kii, tb[:], [Btb], [Bki])
            cp("dve", tb[:], kii, [Bki], [Btb])
            stt("dve", an[:], tb[:], -TWO_PI, an[:], ALU.mult, ALU.add, [Btb, Ban], [Ban])
            ts("dve", an[:], an[:], math.pi, -math.pi, ALU.min, ALU.max, [Ban], [Ban])
            act(tb[:], an[:], AF.Sin, [Ban], [Btb])
            if sgnn:
                ts("dve", tb[:], tb[:], cfa(sgnn), None, ALU.mult, None, [Btb, Bcf], [Btb])
            dma(tab_d[ti, :, cs], tb[:], [Btb], [Btab], "tab")

    def load_w(l, pieces):
        for (sc, n, dc) in pieces:
            si = nxt("st", 2)
            dma(stage[si][:, :, 0:n], w_in_d[l, :, sc:sc + n].rearrange("(c p) n -> p c n", p=128), [], [Bst[si]], "st%d" % si)
            tt("pool", Wbf[:, :, dc:dc + n], stage[si][:, :, 0:n], bc_last(gpre[:, l * 8:(l + 1) * 8], n), ALU.mult,
               [Bst[si], Bgpre], BWr(dc, n))

    def pieces_range(s0, n, d0):
        out = []
        o_ = 0
        while o_ < n:
            m = min(128, n - o_)
            out.append((s0 + o_, m, d0 + o_))
            o_ += m
        return out

    C_PCS = [(2144, 64, 704), (2272, 64, 768), (2336, 12, 832), (2348, 128, 844), (2476, 128, 972), (2016, 64, 640),
             (1696, 128, 0), (1824, 128, 128), (1952, 64, 256), (1952, 64, 320), (2080, 64, 384), (2080, 64, 448),
             (2208, 64, 512), (2208, 64, 576)]
    W_PLAN = {"A": pieces_range(0, 1024, 0), "B": pieces_range(1024, 672, 0), "C": C_PCS, "D": pieces_range(2604, 1024, 0),
              "O": [(p * 128, 128, p * 128) for p in range(8)]}
    W_GATE = {"A": (768, 1024), "B": (416, 672), "C": (832, 1100), "D": (768, 1024)}
    w_done = set()

    def load_pieces(l, name, filt=None):
        for p in W_PLAN[name]:
            key = (l, name, p)
            if key in w_done or (filt is not None and not filt(p)):
                continue
            w_done.add(key)
            if name == "O":
                si = nxt("st", 2)
                dma(stage[si][:, :, :], w_out_d[l, :, p[0]:p[0] + 128].rearrange("(c p) n -> p c n", p=128), [], [Bst[si]], "st%d" % si)
                cp("pool", Wbf[:, :, p[2]:p[2] + 128], stage[si][:, :, :], [Bst[si]], BWr(p[2], 128))
            else:
                load_w(l, [p])

    def prefetch_w(l, cur, nxt_):
        g0, g1 = W_GATE[cur]
        load_pieces(l, nxt_, lambda p: p[2] + p[1] <= g0 or p[2] >= g1)

    def proj_fm(col0, M, tc, bank):
        for c in range(8):
            mm(PS[bank][0:M, :], Wbf[:, c, col0:col0 + M], hT[:, c, tc * 512:(tc + 1) * 512], c == 0, c == 7,
               BWr(col0, M) + BhT[4 * tc:4 * tc + 4], [BPS[bank]])

    def load_tabs(which, tc):
        i = nxt("tab", 2)
        cs = slice(tc * 512, (tc + 1) * 512)
        dma(tabs[i][:, 0, :], tab_d[2 * which, :, cs], [Btab], [Btabs[i]], "tabl%d" % i)
        dma(tabs[i][:, 1, :], tab_d[2 * which + 1, :, cs], [Btab], [Btabs[i]], "tabl%d" % i)
        return tabs[i], Btabs[i]

    def rope_evac(bank, dst, Bdst, tab, Btb_, scale, split=None):
        t1, B1 = tf(); t2, B2 = tf()
        stt("dve", t1[:, :], PS[bank][:, :], scale, tab[:, 0, :], ALU.mult, ALU.mult, [BPS[bank], Btb_], [B1])
        for b in range(4):
            src = b + 1 if b % 2 == 0 else b - 1
            if b != 1:
                act(t2[32 * b:32 * b + 32, :], PS[bank][32 * src:32 * src + 32, :], AF.Copy, [BPS[bank]], [B2], scale=scale)
            else:
                ts("dve", t2[32 * b:32 * b + 32, :], PS[bank][32 * src:32 * src + 32, :], scale, None, ALU.mult, None, [BPS[bank]], [B2])
        tt("dve", t2[:, :], t2[:, :], tab[:, 1, :], ALU.mult, [B2, Btb_], [B2])
        if split is None:
            tt("pool", dst, t1[:, :], t2[:, :], ALU.add, [B1, B2], [Bdst])
        else:
            for (r0, d_, Bd_) in split:
                tt("pool", d_[r0:r0 + 64, :], t1[r0:r0 + 64, :], t2[r0:r0 + 64, :], ALU.add, [B1, B2], [Bd_])

    def proj_v(col0, ncol, dst_fn):
        for t16 in range(16):
            bank = mbank()
            for c in range(8):
                mm(PS[bank][:, 0:ncol], hT[:, c, t16 * 128:(t16 + 1) * 128], Wbf[:, c, col0:col0 + ncol], c == 0, c == 7,
                   BWr(col0, ncol) + [BhT[t16]], [BPS[bank]])
            dst_fn(t16, bank)

    def v_heads(t16, bank):
        pv4 = PS[bank][:, 0:256].rearrange("p (a h d) -> p a h d", a=2, h=2)
        cp("dve", Vv[:, t16, :, 0, 64:128], pv4[:, :, 0, :], [BPS[bank]], [BV[t16]])
        cp("act", Vv[:, t16, :, 1, 0:64], pv4[:, :, 1, :], [BPS[bank]], [BV[t16]])

    def v_ones():
        memset(Vv[:, :, :, 0, 0:64], 1.0, BV)
        memset(Vv[:, :, :, 1, 64:128], 1.0, BV)

    def gate_head(col0, h):
        if h % 2:
            return
        for tc in range(4):
            bank = mbank()
            proj_fm(col0 + 64 * h, 128, tc, bank)
            act(sg[:, tc * 512:(tc + 1) * 512], PS[bank][:, :], AF.Silu, [BPS[bank]], [Bsg[tc]])

    def evac_rows(ob, po, dst, Bdst, src0=None, engs=("dve", "act")):
        if src0 is None:
            src0 = 64 - po
        for b in range(2):
            cp(engs[b], dst[po + 32 * b:po + 32 * b + 32, :], PS[ob][src0 + 32 * b:src0 + 32 * b + 32, :], [BPS[ob]], [Bdst])

    def epilogue_norm(ob, h, qc, light_act=False):
        po = 64 * (h % 2)
        cs = slice(qc * 512, (qc + 1) * 512)
        st = {}

        def early():
            st["t1"] = tf(); st["t2"] = tf()
            t1, B1 = st["t1"]; t2, B2 = st["t2"]
            if light_act:
                recip(t1[po:po + 64, :], PS[ob][po:po + 64, :], [BPS[ob]], [B1])
                evac_rows(ob, po, t2, B2, engs=("dve", "dve"))
            else:
                recip_act(t1[po:po + 64, :], PS[ob][po:po + 64, :], [BPS[ob]], [B1])
                evac_rows(ob, po, t2, B2, engs=("dve", "dve"))

        def late():
            t1, B1 = st["t1"]; t2, B2 = st["t2"]
            tt("pool", t2[po:po + 64, :], t2[po:po + 64, :], sg[po:po + 64, cs], ALU.mult, [B2, Bsg[qc]], [B2])
            tt("dve", mixT[po:po + 64, h // 2, cs], t1[po:po + 64, :], t2[po:po + 64, :], ALU.mult, [B1, B2], [Bmix[h // 2][qc]])
        return early, late

    class Pipe:
        def __init__(self):
            self.items = []

        def add(self, A, B, C, post=None):
            self.items.append((A, B, C, post))

        def run(self, skew=2, pskew=3):
            n = len(self.items)
            late = {}
            for t in range(n + skew + pskew + 1):
                if t < n:
                    self.items[t][0]()
                    self.items[t][1]()
                j = t - skew
                if 0 <= j < n:
                    self.items[j][2]()
                    if self.items[j][3]:
                        pa, pb = self.items[j][3]
                        pa()
                        late.setdefault(min(t + pskew, n + skew + pskew), []).append(pb)
                for f in late.pop(t, []):
                    f()

    def attn_tiles(pipe, ob, qa, qbufs, kfn, kbs, bias_fn, vfn, M, rows=128, maskfn=None, post=None, crange=None):
        for i, kb in enumerate(kbs):
            sbk = sbank()
            P_, BP_ = pbuf()
            c0, c1 = crange(kb) if crange is not None else (0, 512)

            def A(kb=kb, sbk=sbk, c0=c0, c1=c1):
                ka, kbufs = kfn(kb)
                bl = bias_fn(kb)
                mm(PS[sbk][0:rows, c0:c1], ka, qa[:, c0:c1], True, len(bl) == 0, qbufs + kbufs, [BPS[sbk]])
                for bi, (bl_l, bl_r, bl_b) in enumerate(bl):
                    mm(PS[sbk][0:rows, c0:c1], bl_l, bl_r[:, c0:c1], False, bi == len(bl) - 1, bl_b, [BPS[sbk]])

            def B(kb=kb, sbk=sbk, P_=P_, BP_=BP_, c0=c0, c1=c1):
                act(P_[0:rows, c0:c1], PS[sbk][0:rows, c0:c1], AF.Exp, [BPS[sbk]], [BP_])
                if maskfn is not None:
                    ma, mb = maskfn(kb)
                    tt("dve", P_[0:rows, c0:c1], P_[0:rows, c0:c1], ma[:, c0:c1], ALU.mult, [BP_] + mb, [BP_])

            def C(kb=kb, i=i, P_=P_, BP_=BP_, c0=c0, c1=c1):
                va, vbufs = vfn(kb)
                mm(PS[ob][0:M, c0:c1], va, P_[0:rows, c0:c1], i == 0, i == len(kbs) - 1, vbufs + [BP_], [BPS[ob]])

            pipe.add(A, B, C, post if i == len(kbs) - 1 else None)

    def causal_range(qc):
        def f(kb):
            d = kb - 4 * qc
            return (128 * d, 512) if d > 0 else (0, 512)
        return f

    def window_range(qc):
        def f(kb):
            d = kb - 4 * qc
            if d >= 0:
                return (128 * d, 512)
            return (0, 128 * (d + 5))
        return f

    def table_bias(name, qc, nblk):
        def f(kb):
            d0 = 4 * qc - kb + 3
            if d0 + 4 > nblk:
                d0 = nblk - 4
            return [(identb, cba(name, 0, 128, d0 * 128, d0 * 128 + 512), [BCB])]
        return f

    def zero_kz():
        for h in range(4):
            r0 = 64 * (1 - h % 2)
            for tc in range(4):
                memset(kB[h][r0:r0 + 64, tc * 512:(tc + 1) * 512], 0.0, [BkB[h][tc]])

    def store_mix(m):
        def f():
            if m == 3 and not debug:
                return
            for t16 in range(16):
                dma(mix_d[t16, :, 2 * m:2 * m + 2, :], mixT[:, :, t16 * 128:(t16 + 1) * 128],
                    [Bmix[0][t16 // 4], Bmix[1][t16 // 4]], [Bmixd[t16]], "mixst")
        flush_store()
        pend["store"] = f

    def zero_mixer(m):
        for c in range(2):
            for qc in range(4):
                memset(mixT[:, c, qc * 512:(qc + 1) * 512], 0.0, [Bmix[c][qc]])
        store_mix(m)

    def mixer_a(l):
        import os
        KSTOP = int(os.environ.get("KSTOP", "99"))
        load_pieces(l, "A")
        flush_store()
        if KSTOP <= 1:
            return zero_mixer(0)
        v_ones()
        zero_kz()
        proj_v(512, 256, v_heads)
        if KSTOP <= 2:
            return zero_mixer(0)
        for tc in range(4):
            tab, Btb_ = load_tabs(0, tc)
            cs = slice(tc * 512, (tc + 1) * 512)
            for pair in range(2):
                bank = mbank()
                proj_fm(128 * pair, 128, tc, bank)
                rope_evac(bank, qT[:, pair, cs], BqT[pair][tc], tab, Btb_, 0.125)
                bank = mbank()
                proj_fm(256 + 128 * pair, 128, tc, bank)
                rope_evac(bank, None, None, tab, Btb_, 1.0,
                          split=[(0, kB[2 * pair][:, cs], BkB[2 * pair][tc]), (64, kB[2 * pair + 1][:, cs], BkB[2 * pair + 1][tc])])
        pipe = Pipe()
        gate_head(768, 0)
        for h in range(4):
            pair, po = h // 2, 64 * (h % 2)
            for qc in range(4):
                ob = obank()

                def mask_a(kb, qc=qc):
                    d0 = min(4 * qc - kb + 3, 8)
                    return cba("Ca", 0, 128, d0 * 128, d0 * 128 + 512), [BCB]

                e_, l_ = epilogue_norm(ob, h, qc)

                def late(l_=l_, h=h, qc=qc):
                    l_()
                    if qc == 3 and h < 3:
                        gate_head(768, h + 1)
                post = (e_, late)
                attn_tiles(pipe, ob, qT[:, pair, qc * 512:(qc + 1) * 512], [BqT[pair][qc]],
                           lambda kb, h=h: (kB[h][:, kb * 128:(kb + 1) * 128], [BkB[h][kb // 4]]),
                           list(range(4 * qc + 4)), lambda kb: [],
                           lambda kb, h=h: (Vaug[:, kb, h, :], [BV[kb]]), 128, maskfn=mask_a, post=post, crange=causal_range(qc))
        prefetch_w(l, "A", "B")
        pipe.run(skew=3)
        store_mix(0)

    def mixer_d(l):
        arena_reset()
        Ssum, BSs = carve(512, F32, "Ssum"); Ssb, BSsb = carve(512, BF16, "Ssb")
        spb = [carve(512, BF16, "spb%d" % i) for i in range(3)]
        Ssb2 = [(Ssb, BSsb), carve(512, BF16, "Ssb1")]
        load_pieces(l, "D")
        flush_store()
        arena_gate()
        zero_kz()
        proj_v(512, 256, v_heads)
        for tc in range(4):
            cs = slice(tc * 512, (tc + 1) * 512)
            for pair in range(2):
                bank = mbank()
                proj_fm(128 * pair, 128, tc, bank)
                act(qT[:, pair, cs], PS[bank][:, :], AF.Copy, [BPS[bank]], [BqT[pair][tc]], scale=0.125)
                bank = mbank()
                proj_fm(256 + 128 * pair, 128, tc, bank)
                cp("dve", kB[2 * pair][0:64, cs], PS[bank][0:64, :], [BPS[bank]], [BkB[2 * pair][tc]])
                cp("dve", kB[2 * pair + 1][64:128, cs], PS[bank][64:128, :], [BPS[bank]], [BkB[2 * pair + 1][tc]])
        negU = cba("negU"); negones = cba("negones")
        tiles = []
        for h in range(4):
            pair, po = h // 2, 64 * (h % 2)
            for qc in range(4):
                ob = obank()
                kbs = list(range(4 * qc + 3, -1, -1))
                for i, kb in enumerate(kbs):
                    tiles.append(dict(h=h, pair=pair, po=po, qc=qc, ob=ob, i=i, kb=kb, n=len(kbs), zb=sbank(), eb=mbank(),
                                      sp=spb[len(tiles) % 3], ssb=Ssb2[len(tiles) % 2], P=pbuf()))

        def stA(T):
            cs = slice(T["qc"] * 512, (T["qc"] + 1) * 512); ks_ = slice(T["kb"] * 128, (T["kb"] + 1) * 128)
            po, pair, zb = T["po"], T["pair"], T["zb"]
            diag = T["kb"] >= 4 * T["qc"]; d0 = 4 * T["qc"] - T["kb"] + 3
            rd = [BqT[pair][T["qc"]], BkB[T["h"]][T["kb"] // 4]]
            c0 = 128 * max(0, T["kb"] - 4 * T["qc"]); T["c0"] = c0
            qs = slice(T["qc"] * 512 + c0, (T["qc"] + 1) * 512)
            mm(PS[zb][:, c0:], kB[T["h"]][:, ks_], qT[:, pair, qs], True, not diag, rd, [BPS[zb]])
            if diag:
                mm(PS[zb][:, c0:], identb, cba("Ts", 0, 128, d0 * 128 + c0, d0 * 128 + 512), False, True, [BCB], [BPS[zb]])

        def stB(T):
            t1, B1 = tf()
            c0 = T["c0"]
            act(t1[:, c0:], PS[T["zb"]][:, c0:], AF.Exp, [BPS[T["zb"]]], [B1])
            sp_, Bsp_ = T["sp"]
            act(sp_[:, c0:], t1[:, c0:], AF.Ln, [B1], [Bsp_], bias=1.0)

        def stC(T, prev):
            cs = slice(T["qc"] * 512, (T["qc"] + 1) * 512); ks_ = slice(T["kb"] * 128, (T["kb"] + 1) * 128)
            po, pair, eb = T["po"], T["pair"], T["eb"]
            diag = T["kb"] >= 4 * T["qc"]; d0 = 4 * T["qc"] - T["kb"] + 3
            rd = [BqT[pair][T["qc"]], BkB[T["h"]][T["kb"] // 4]]
            sp_, Bsp_ = T["sp"]
            c0 = T["c0"]
            qs = slice(T["qc"] * 512 + c0, (T["qc"] + 1) * 512)
            mm(PS[eb][:, c0:], kB[T["h"]][:, ks_], qT[:, pair, qs], True, False, rd, [BPS[eb]])
            if diag:
                mm(PS[eb][:, c0:], identb, cba("Ts", 0, 128, d0 * 128 + c0, d0 * 128 + 512), False, False, [BCB], [BPS[eb]])
            mm(PS[eb][:, c0:], negU, sp_[:, c0:], False, T["i"] == 0, [BCB, Bsp_], [BPS[eb]])
            if T["i"] > 0:
                ssb_, Bssb_ = prev["ssb"]
                mm(PS[eb][:, c0:], negones, ssb_[:, c0:], False, True, [BCB, Bssb_], [BPS[eb]])
            P_, BP_ = T["P"]
            act(P_[:, c0:], PS[eb][:, c0:], AF.Exp, [BPS[eb]], [BP_])

        def stU(T):
            if T["i"] == T["n"] - 1:
                return
            sp_, Bsp_ = T["sp"]; ssb_, Bssb_ = T["ssb"]
            c0 = T["c0"]
            if T["i"] == 0:
                if c0 > 0:
                    memset(Ssum[:, 0:c0], 0.0, [BSs])
                cp("dve", Ssum[:, c0:], sp_[:, c0:], [Bsp_], [BSs])
            else:
                tt("dve", Ssum[:, c0:], Ssum[:, c0:], sp_[:, c0:], ALU.add, [BSs, Bsp_], [BSs])
            cp("dve", ssb_, Ssum, [BSs], [Bssb_])

        def stE(T):
            P_, BP_ = T["P"]
            ob, h, kb = T["ob"], T["h"], T["kb"]
            c0 = T["c0"]
            mm(PS[ob][:, c0:], Vaug[:, kb, h, :], P_[:, c0:], T["i"] == 0, T["i"] == T["n"] - 1, [BV[kb], BP_], [BPS[ob]])
            if T["i"] == T["n"] - 1:
                cs = slice(T["qc"] * 512, (T["qc"] + 1) * 512)
                t3, B3 = tf()
                po = T["po"]
                evac_rows(ob, po, t3, B3, engs=("dve", "dve"))
                tt("pool", mixT[po:po + 64, T["pair"], cs], t3[po:po + 64, :], sg[po:po + 64, cs], ALU.mult, [B3, Bsg[T["qc"]]], [Bmix[T["pair"]][T["qc"]]])
                if T["qc"] == 3 and h < 3:
                    gate_head(768, h + 1)

        gate_head(768, 0)
        prefetch_w(l, "D", "O")
        n = len(tiles)
        for t in range(n + 2):
            if t < n:
                stA(tiles[t]); stB(tiles[t])
            if 0 <= t - 1 < n:
                stC(tiles[t - 1], tiles[t - 2] if t - 2 >= 0 else None)
            if t < n:
                stU(tiles[t])
            if 0 <= t - 2 < n:
                stE(tiles[t - 2])
        store_mix(3)

    qB = [qT[:, 0, :], qT[:, 1, :], qx[:, 0, :], qx[:, 1, :]]
    kB = [kT[:, 0, :], kT[:, 1, :], kx[:, 0, :], kx[:, 1, :]]
    BqB = [BqT[0], BqT[1], BqX[0], BqX[1]]
    BkB = [BkT[0], BkT[1], BkX[0], BkX[1]]

    def mixer_b(l):
        arena_reset()
        uq_st, Buqst = carve(768, F32, "uq_st"); uq_st = uq_st.rearrange("p (c n) -> p c n", c=2)
        ukv_st, Bukvst = carve(512, F32, "ukv_st")
        Wuq, BWuq = carve(768, BF16, "Wuq"); Wuq = Wuq.rearrange("p (c n) -> p c n", c=2)
        Wuqs, _ = carve(768, BF16, "Wuqs"); Wuqs = Wuqs.rearrange("p (c n) -> p c n", c=2)
        Wkp, BWkv = carve(384, BF16, "Wkp"); Wkp = Wkp.rearrange("p (h n) -> p h n", h=4)
        Wv, _ = carve(256, BF16, "Wv")
        Wkr, BWkr = carve(768, BF16, "Wkr"); Wkr = Wkr.rearrange("p (c n) -> p c n", c=8)
        Wkrs, _ = carve(768, BF16, "Wkrs"); Wkrs = Wkrs.rearrange("p (c n) -> p c n", c=8)
        gqk, Bgqk = carve(4, F32, "gqk")
        cqn, Bcqn = carve(1536, BF16, "cqn"); cqn = cqn.rearrange("p (c n) -> p c n", c=3)
        krr, Bkrr = carve(SEQ, BF16, "krr")
        load_pieces(l, "B")
        flush_store()
        arena_gate()
        dma(gqk[:, 0:2], gq_d[l].rearrange("(c p) -> p c", p=128), [], [Bgqk], "cst3", allow_slow_non_contiguous=True)
        dma(gqk[:, 2:3], gkv_d[l].rearrange("(c p) -> p c", p=128), [], [Bgqk], "cst3", allow_slow_non_contiguous=True)
        dma(uq_st, wuq_d[l].rearrange("(c p) n -> p c n", p=128), [], [Buqst], "cst3")
        dma(ukv_st, wukv_d[l], [], [Bukvst], "cst3")
        tt("pool", Wuq, uq_st, bc_last(gqk[:, 0:2], 384), ALU.mult, [Buqst, Bgqk], [BWuq])
        cp("pool", Wuqs, Wuq, [BWuq], [BWuq])
        for hh in range(4):
            b = hh * 96 + 64
            cp("pool", Wuqs[:, :, b:b + 16], Wuq[:, :, b + 16:b + 32], [BWuq], [BWuq])
            cp("pool", Wuqs[:, :, b + 16:b + 32], Wuq[:, :, b:b + 16], [BWuq], [BWuq])
        memset(Wkp, 0.0, [BWkv])
        ukv4 = ukv_st.rearrange("p (h d) -> p h d", h=4)
        ts("dve", Wkp[:, :, 0:64], ukv4[:, :, 0:64], gqk[:, 2:3], None, ALU.mult, None, [Bukvst, Bgqk], [BWkv])
        ts("dve", Wv.rearrange("p (h d) -> p h d", h=4), ukv4[:, :, 64:128], gqk[:, 2:3], None, ALU.mult, None, [Bukvst, Bgqk], [BWkv])
        memset(Wkr, 0.0, [BWkr]); memset(Wkrs, 0.0, [BWkr])
        cp("pool", Wkr[:, :, 64:96], Wbf[:, :, 384:416], BWr(384, 32), [BWkr])
        cp("pool", Wkrs[:, :, 64:80], Wbf[:, :, 400:416], BWr(384, 32), [BWkr])
        cp("pool", Wkrs[:, :, 80:96], Wbf[:, :, 384:400], BWr(384, 32), [BWkr])
        v_ones()
        sc = 96.0 ** -0.5
        for tc in range(4):
            cs = slice(tc * 512, (tc + 1) * 512)
            tab, Btb_ = load_tabs(1, tc)
            banks = [mbank(), mbank(), sbank()]
            sq = []
            for j in range(3):
                proj_fm(128 * j, 128, tc, banks[j])
                t_, B_ = tf()
                act(t_[:, :], PS[banks[j]][:, :], AF.Square, [BPS[banks[j]]], [B_])
                sq.append((t_, B_))
            ssq = obank(); ssk = obank()
            mm(PS[ssq][:, :], cfa("ones"), sq[0][0][:, :], True, False, [Bcf, sq[0][1]], [BPS[ssq]])
            mm(PS[ssq][:, :], cfa("ones"), sq[1][0][:, :], False, True, [Bcf, sq[1][1]], [BPS[ssq]])
            mm(PS[ssk][:, :], cfa("ones"), sq[2][0][:, :], True, True, [Bcf, sq[2][1]], [BPS[ssk]])
            for (sbk, denom, js) in ((ssq, 256.0, (0, 1)), (ssk, 128.0, (2,))):
                r_, Br_ = tf()
                act(r_[:, :], PS[sbk][:, :], AF.Ln, [BPS[sbk]], [Br_], scale=1.0 / denom, bias=EPS)
                act(r_[:, :], r_[:, :], AF.Exp, [Br_], [Br_], scale=-0.5)
                for j in js:
                    tt("dve", cqn[:, j, :], PS[banks[j]][:, :], r_[:, :], ALU.mult, [BPS[banks[j]], Br_], [Bcqn])
            b1 = mbank(); b2 = mbank()
            for (bank, W_) in ((b1, Wkr), (b2, Wkrs)):
                for c in range(8):
                    mm(PS[bank][0:96, :], W_[:, c, :], hT[:, c, cs], c == 0, c == 7, [BWkr] + BhT[4 * tc:4 * tc + 4], [BPS[bank]])
            t1, B1 = tf(); t2, B2 = tf()
            tt("dve", t1[64:96, :], PS[b1][64:96, :], tab[64:96, 0, :], ALU.mult, [BPS[b1], Btb_], [B1])
            tt("dve", t2[64:96, :], PS[b2][64:96, :], tab[64:96, 1, :], ALU.mult, [BPS[b2], Btb_], [B2])
            tt("pool", krr[64:96, cs], t1[64:96, :], t2[64:96, :], ALU.add, [B1, B2], [Bkrr])
            for t4 in range(4):
                t16 = 4 * tc + t4
                bank = mbank()
                mm(PS[bank][:, 0:256], cqn[:, 2, t4 * 128:(t4 + 1) * 128], Wv, True, True, [Bcqn, BWkv], [BPS[bank]])
                v_heads(t16, bank)
            for hh in range(4):
                b1 = mbank(); b2 = mbank()
                for (bank, W_) in ((b1, Wuq), (b2, Wuqs)):
                    for c in range(2):
                        mm(PS[bank][0:96, :], W_[:, c, hh * 96:(hh + 1) * 96], cqn[:, c, :], c == 0, c == 1, [BWuq, Bcqn], [BPS[bank]])
                act(qB[hh][0:64, cs], PS[b1][0:64, :], AF.Copy, [BPS[b1]], [BqB[hh][tc]], scale=sc)
                t1, B1 = tf(); t2, B2 = tf()
                stt("dve", t1[64:96, :], PS[b1][64:96, :], sc, tab[64:96, 0, :], ALU.mult, ALU.mult, [BPS[b1], Btb_], [B1])
                stt("dve", t2[64:96, :], PS[b2][64:96, :], sc, tab[64:96, 1, :], ALU.mult, ALU.mult, [BPS[b2], Btb_], [B2])
                tt("pool", qB[hh][64:96, cs], t1[64:96, :], t2[64:96, :], ALU.add, [B1, B2], [BqB[hh][tc]])
                bank = mbank()
                mm(PS[bank][0:96, :], Wkp[:, hh, :], cqn[:, 2, :], True, True, [BWkv, Bcqn], [BPS[bank]])
                cp("dve", kB[hh][0:64, cs], PS[bank][0:64, :], [BPS[bank]], [BkB[hh][tc]])
                cp("pool", kB[hh][64:96, cs], krr[64:96, cs], [Bkrr], [BkB[hh][tc]])
        if debug and l == 0:
            dma(dbg_d[0], qB[0], BqB[0], [], "dbg")
            dma(dbg_d[1], kB[0], BkB[0], [], "dbg")
            dma(dbg_d[2], qB[3], BqB[3], [], "dbg")
            dma(dbg_d[3], kB[3], BkB[3], [], "dbg")
        pipe = Pipe()
        gate_head(416, 0)
        for h in range(4):
            for qc in range(4):
                ob = obank()
                tb_ = table_bias("Tc", qc, 7)

                e_, l_ = epilogue_norm(ob, h, qc, light_act=True)

                def late(l_=l_, h=h, qc=qc):
                    l_()
                    if qc == 3 and h < 3:
                        gate_head(416, h + 1)
                post = (e_, late)
                attn_tiles(pipe, ob, qB[h][0:96, qc * 512:(qc + 1) * 512], [BqB[h][qc]],
                           lambda kb, h=h: (kB[h][0:96, kb * 128:(kb + 1) * 128], [BkB[h][kb // 4]]),
                           list(range(4 * qc + 4)),
                           lambda kb, qc=qc, tb_=tb_: (tb_(kb) if kb >= 4 * qc else []),
                           lambda kb, h=h: (Vaug[:, kb, h, :], [BV[kb]]), 128, post=post, crange=causal_range(qc))
        prefetch_w(l, "B", "C")
        pipe.run(skew=3)
        store_mix(1)

    def mixer_c(l):
        arena_reset()
        V76s, BVs = carve(16 * 128, BF16, "V76s"); V76s = V76s.rearrange("p (t n) -> p t n", t=16)
        V76w, BVw = carve(16 * 128, BF16, "V76w"); V76w = V76w.rearrange("p (t n) -> p t n", t=16)
        vcT = qx[:, 1, :]
        nselT, BnselT = carve(SEQ, BF16, "nselT")
        posT, BposT = carve(64, F32, "posT")
        W1b = [carve(1024, BF16, "W1b%d" % i) for i in range(2)]
        w2st, Bw2st = carve(128, F32, "w2st")
        w2b, Bw2b = carve(256, BF16, "w2b")
        hidS, BhidS = carve(256, BF16, "hidS")
        kcc, Bkcc = carve(128, BF16, "kcc"); vcc, Bvcc = carve(64, BF16, "vcc")
        xp_off = ar["ptr"]
        Xp, BXp = carve(32 * 127, BF16, "Xp")
        pcs = [(2144, 64, 704), (2272, 64, 768), (2336, 12, 832), (2348, 128, 844), (2476, 128, 972), (2016, 64, 640),
               (1696, 128, 0), (1824, 128, 128), (1952, 64, 256), (1952, 64, 320), (2080, 64, 384), (2080, 64, 448),
               (2208, 64, 512), (2208, 64, 576)]
        load_pieces(l, "C")
        flush_store()
        arena_gate()
        qz = [qT[:, 0, :], qT[:, 1, :], kT[:, 0, :], kT[:, 1, :]]
        Bqz = [BqT[0], BqT[1], BkT[0], BkT[1]]
        for h_ in range(4):
            r0_ = 64 * (1 - h_ % 2)
            for tc_ in range(4):
                memset(qz[h_][r0_:r0_ + 64, tc_ * 512:(tc_ + 1) * 512], 0.0, [Bqz[h_][tc_]])
        memset(nselT[:, :], 0.0, [BnselT])
        KC2, KS2, KW2 = kx[:, 0, :], kx[:, 1, :], qx[:, 0, :]
        BKC2, BKS2, BKW2 = BkX[0], BkX[1], BqX[0]
        memset(V76s[:, :, 64:128], 1.0, [BVs]); memset(V76w[:, :, 64:128], 1.0, [BVw])

        def v_c(t16, bank):
            cp("dve", V76s[:, t16, 0:64], PS[bank][:, 0:64], [BPS[bank]], [BVs])
            cp("dve", V76w[:, t16, 0:64], PS[bank][:, 64:128], [BPS[bank]], [BVw])
        proj_v(704, 128, v_c)
        for tc in range(4):
            tab, Btb_ = load_tabs(0, tc)
            cs = slice(tc * 512, (tc + 1) * 512)
            for pair in range(2):
                bank = mbank()
                proj_fm(128 * pair, 128, tc, bank)
                rope_evac(bank, None, None, tab, Btb_, 0.125,
                          split=[(0, qz[2 * pair][:, cs], Bqz[2 * pair][tc]), (64, qz[2 * pair + 1][:, cs], Bqz[2 * pair + 1][tc])])
            for (c0, dst, Bd) in ((256, KC2, BKC2), (384, KS2, BKS2), (512, KW2, BKW2)):
                bank = mbank()
                proj_fm(c0, 128, tc, bank)
                rope_evac(bank, dst[:, cs], Bd[tc], tab, Btb_, 1.0)
            bank = mbank()
            proj_fm(640, 64, tc, bank)
            cp("dve", vcT[0:64, cs], PS[bank][0:64, :], [BPS[bank]], [BqX[1][tc]])

        def compress(srcT, Bsrc, pos_dram, w1_d, w2_d, is_k):
            dma(posT[0:64, 0:32], pos_dram[l].rearrange("l d -> d l"), [], [BposT], "cst4", allow_slow_non_contiguous=True)
            base = srcT[0:64, 0:1]
            in0 = bass.AP(base.tensor, base.offset, [list(base.ap[0]), [1, 32], [16, 127]])
            tt("pool", Xp.rearrange("p (l n) -> p l n", l=32)[0:64], in0, bc_last(posT[0:64, 0:32], 127), ALU.add, Bsrc + [BposT], [BXp])
            Xp3 = Xp.rearrange("p (l n) -> p l n", l=32)
            hb = [mbank(), mbank()]
            for pc in range(8):
                si = nxt("st", 2)
                dma(stage_flat(si, 0, 64, 1024).rearrange("p (l h) -> p l h", l=4),
                    w1_d[l, pc * 256:(pc + 1) * 256, :].rearrange("(l d) h -> d l h", d=64), [], [Bst[si]], "st%d" % si)
                wb, Bwb = W1b[pc % 2]
                cp("act" if pc % 2 else "dve", wb[0:64, :], stage_flat(si, 0, 64, 1024), [Bst[si]], [Bwb])
                wb3 = wb.rearrange("p (l h) -> p l h", l=4)
                for li in range(4):
                    lidx = pc * 4 + li
                    for half in range(2):
                        mm(PS[hb[half]][:, 0:127], wb3[0:64, li, half * 128:(half + 1) * 128], Xp3[0:64, lidx, :],
                           lidx == 0, lidx == 31, [Bwb, BXp], [BPS[hb[half]]])
            hs3 = hidS.rearrange("p (c n) -> p c n", c=2)
            for half in range(2):
                act(hs3[:, half, 0:127], PS[hb[half]][:, 0:127], AF.Silu, [BPS[hb[half]]], [BhidS])
            dma(w2st.rearrange("p (c n) -> p c n", c=2), w2_d[l].rearrange("(c p) n -> p c n", p=128), [], [Bw2st], "cst4")
            w2b3 = w2b.rearrange("p (c n) -> p c n", c=2)
            cp("pool", w2b3[:, :, 0:64], w2st.rearrange("p (c n) -> p c n", c=2), [Bw2st], [Bw2b])
            cp("pool", w2b3[:, :, 64:128], w2st.rearrange("p (c n) -> p c n", c=2), [Bw2st], [Bw2b])
            bank = mbank()
            if is_k:
                for c in range(2):
                    mm(PS[bank][:, 0:127], w2b3[:, c, :], hs3[:, c, 0:127], c == 0, c == 1, [Bw2b, BhidS], [BPS[bank]])
                cp("dve", kcc[:, 0:127], PS[bank][:, 0:127], [BPS[bank]], [Bkcc])
            else:
                for c in range(2):
                    mm(PS[bank][0:127, 0:64], hs3[:, c, 0:127], w2b3[:, c, 0:64], c == 0, c == 1, [Bw2b, BhidS], [BPS[bank]])
                cp("dve", vcc[0:127, :], PS[bank][0:127, 0:64], [BPS[bank]], [Bvcc])
        compress(KC2, BKC2, posk_d, kw1_d, kw2_d, True)
        compress(vcT, BqX[1], posv_d, vw1_d, vw2_d, False)
        save_ptr = ar["ptr"]; ar["ptr"] = xp_off
        impm, Bimpm = carve(128, F32, "impm"); rank, Brank = carve(128, F32, "rank"); cmp3, Bcmp3 = carve(1024, F32, "cmp3")
        ar["ptr"] = max(save_ptr, ar["ptr"])
        memset(small[:, 62:63], 0.0, [BXp, Bimpm, Brank, Bcmp3])

        c3 = cmp3.rearrange("p (a b) -> p a b", a=32)
        def select_blocks(qc, ib):
            cs = slice(qc * 512, (qc + 1) * 512)
            impq, Bimpq = tf()
            act(impq[0:32, :], PS[ib][0:32, :], AF.Copy, [BPS[ib]], [Bimpq])
            tb_ = mbank()
            for t4 in range(4):
                op("pe", lambda e, t4=t4, tb_=tb_, impq=impq: e.transpose(PS[tb_][:, t4 * 32:(t4 + 1) * 32], impq[0:32, t4 * 128:(t4 + 1) * 128], cfa("identf", 0, 32, 0, 32)),
                   [Bimpq, Bcf], [BPS[tb_]])
            tt("dve", impm, PS[tb_][:, 0:128], cfa("NF", 0, 128, qc * 128, (qc + 1) * 128), ALU.mult, [BPS[tb_], Bcf], [Bimpm])
            tt("dve", impm, impm, cfa("ADDT", 0, 128, qc * 128, (qc + 1) * 128), ALU.add, [Bimpm, Bcf], [Bimpm])
            for t4 in range(4):
                x_ = impm[:, t4 * 32:t4 * 32 + 1]
                xj2 = bass.AP(x_.tensor, x_.offset, [list(x_.ap[0]), [0, 32], [1, 32]])
                xj = bass.AP(x_.tensor, x_.offset, [list(x_.ap[0]), [1, 32], [0, 32]])
                tt("dve", c3, xj2, xj, ALU.is_gt, [Bimpm], [Bcmp3])
                op("dve", lambda e, t4=t4: e.reduce_sum(out=rank[:, t4 * 32:(t4 + 1) * 32], in_=c3, axis=AX.X), [Bcmp3], [Brank])
            ts("dve", rank, rank, 15.5, None, ALU.is_ge, None, [Brank], [Brank])

        def select_part2(qc):
            cs = slice(qc * 512, (qc + 1) * 512)
            tb2 = mbank()
            for t4 in range(4):
                op("pe", lambda e, t4=t4, tb2=tb2: e.transpose(PS[tb2][0:32, t4 * 128:(t4 + 1) * 128], rank[:, t4 * 32:(t4 + 1) * 32], cfa("identf")),
                   [Brank, Bcf], [BPS[tb2]])
            cp("dve", nselT[0:32, cs], PS[tb2][0:32, :], [BPS[tb2]], [BnselT])

        items = [dict(qc=qc, h=h) for qc in range(4) for h in range(4)]
        pending_sel = []
        ibs = {}

        def cA(T):
            qc, h = T["qc"], T["h"]
            cs = slice(qc * 512, (qc + 1) * 512)
            sbk = sbank(); T["sbk"] = sbk
            mm(PS[sbk][0:127, :], kcc[:, 0:127], qz[h][:, cs], True, False, [Bkcc, Bqz[h][qc]], [BPS[sbk]])
            mm(PS[sbk][0:127, :], cba("ident", 0, 127, 0, 127), cba("Tcmp", 0, 127, qc * 512, (qc + 1) * 512), False, True, [BCB], [BPS[sbk]])
            T["E"] = pbuf()
            E_, BE_ = T["E"]
            act(E_[0:127, :], PS[sbk][0:127, :], AF.Exp, [BPS[sbk]], [BE_])

        def cC(T):
            E_, BE_ = T["E"]
            db = mbank()
            mm(PS[db][:, :], cba("onesb", 0, 127), E_[0:127, :], True, True, [BCB, BE_], [BPS[db]])
            r_, Br_ = tf()
            act(r_[:, :], PS[db][:, :], AF.Ln, [BPS[db]], [Br_], bias=1e-30)
            act(r_[:, :], r_[:, :], AF.Exp, [Br_], [Br_], scale=-1.0)
            tt("pool", E_[0:127, :], E_[0:127, :], r_[0:127, :], ALU.mult, [BE_, Br_], [BE_])

        def cE(T):
            qc, h = T["qc"], T["h"]
            Pn, BPn = T["E"]
            if h == 0:
                ibs[qc] = obank()
            ib = ibs[qc]
            mm(PS[ib][0:32, :], cba("OV", 0, 127), Pn[0:127, :], h == 0, h == 3, [BCB, BPn], [BPS[ib]])
            cb_ = mbank()
            mm(PS[cb_][0:64, :], vcc[0:127, 0:64], Pn[0:127, :], True, True, [Bvcc, BPn], [BPS[cb_]])
            oc_, Boc_ = tf()
            cp("act", oc_[0:64, :], PS[cb_][0:64, :], [BPS[cb_]], [Boc_])
            dma(ocmp_d[h * 4 + qc], oc_[0:64, :], [Boc_], [Bocmp[h * 4 + qc]], "ocst")
            if h == 3:
                select_blocks(qc, ib)
                pending_sel.append([3, qc])

        n_it = len(items)
        for t in range(n_it + 2):
            if t < n_it:
                cA(items[t])
            if 0 <= t - 1 < n_it:
                cC(items[t - 1])
            if 0 <= t - 2 < n_it:
                cE(items[t - 2])
            for ps_ in list(pending_sel):
                if ps_[0] == 0:
                    select_part2(ps_[1]); pending_sel.remove(ps_)
                else:
                    ps_[0] -= 1
        for ps_ in pending_sel:
            select_part2(ps_[1])
        pools["M"] = [7]; pools["O"] = [3, 4, 5, 6]
        pipe = Pipe()
        gate_head(844, 0)
        for h in range(4):
            pair, po = h // 2, 64 * (h % 2)
            for qc in range(4):
                cs = slice(qc * 512, (qc + 1) * 512)
                qa = qz[h][:, cs]; qb_ = [Bqz[h][qc]]
                osb = obank(); owb = obank()
                tbc = table_bias("Tc", qc, 7)

                def bias_s(kb, qc=qc, tbc=tbc, cs=cs):
                    bl = [(cba("EXPNEG", 0, 128, kb * 128, (kb + 1) * 128), nselT[:, cs], [BCB, BnselT])]
                    if kb >= 4 * qc:
                        bl += tbc(kb)
                    return bl

                def early(h=h, qc=qc, pair=pair, po=po, cs=cs, osb=osb, owb=owb):
                    acc, Bacc = xt[0][:, 0:512], Bxt[0]
                    dma(acc[po:po + 64, :], ocmp_d[h * 4 + qc], [Bocmp[h * 4 + qc]], [Bacc], "ocld")
                    w1_, Bw1_ = xt[1][:, 0:512], Bxt[1]; w2_, Bw2_ = xt[1][:, 512:1024], Bxt[1]; sgl, Bsgl = xt[0][:, 512:1024], Bxt[0]
                    gb_ = mbank()
                    proj_fm(832, 12, qc, gb_)
                    act(sgl[64:76, :], PS[gb_][0:12, :], AF.Exp, [BPS[gb_]], [Bsgl], scale=-1.0)
                    act(sgl[64:76, :], sgl[64:76, :], AF.Ln, [Bsgl], [Bsgl], bias=1.0)
                    act(sgl[64:76, :], sgl[64:76, :], AF.Exp, [Bsgl], [Bsgl], scale=-1.0)
                    recip_act(w1_[64:76, :], PS[osb][64:76, :], [BPS[osb]], [Bw1_])
                    tt("dve", w1_[64:76, :], w1_[64:76, :], sgl[64:76, :], ALU.mult, [Bw1_, Bsgl], [Bw1_])
                    recip_act(w2_[64:76, :], PS[owb][64:76, :], [BPS[owb]], [Bw2_])
                    tt("dve", w2_[64:76, :], w2_[64:76, :], sgl[64:76, :], ALU.mult, [Bw2_, Bsgl], [Bw2_])
                    for i, (w_, Bw_) in enumerate(((sgl, Bsgl), (w1_, Bw1_), (w2_, Bw2_))):
                        ts("dve", w_[64:76, :], w_[64:76, :], cfa("SELCOL", 64, 76, 3 * h + i, 3 * h + i + 1), None, ALU.mult, None, [Bw_, Bcf], [Bw_])

                def late(h=h, qc=qc, pair=pair, po=po, cs=cs, osb=osb, owb=owb):
                    acc, Bacc = xt[0][:, 0:512], Bxt[0]
                    w1_, Bw1_ = xt[1][:, 0:512], Bxt[1]; w2_, Bw2_ = xt[1][:, 512:1024], Bxt[1]; sgl, Bsgl = xt[0][:, 512:1024], Bxt[0]
                    bbs = [mbank(), sbank(), sbank()]
                    for i, (wsrc, Bws) in enumerate(((sgl[64:76, :], Bsgl), (w1_[64:76, :], Bw1_), (w2_[64:76, :], Bw2_))):
                        mm(PS[bbs[i]][:, :], cfa("ones", 64, 76, 0, 128), wsrc, True, True, [Bcf, Bws], [BPS[bbs[i]]])
                    for i, obr in enumerate((None, osb, owb)):
                        bb = bbs[i]
                        if obr is None:
                            tt("dve", acc[po:po + 64, :], PS[bb][po:po + 64, :], acc[po:po + 64, :], ALU.mult, [Bacc, BPS[bb]], [Bacc])
                        else:
                            t_, Bt_ = tf()
                            evac_rows(obr, po, t_, Bt_, src0=0, engs=("dve", "dve"))
                            tt("dve", t_[po:po + 64, :], PS[bb][po:po + 64, :], t_[po:po + 64, :], ALU.mult, [Bt_, BPS[bb]], [Bt_])
                            tt("pool", acc[po:po + 64, :], acc[po:po + 64, :], t_[po:po + 64, :], ALU.add, [Bacc, Bt_], [Bacc])
                    tt("dve", mixT[po:po + 64, pair, cs], acc[po:po + 64, :], sg[po:po + 64, cs], ALU.mult, [Bacc, Bsg[qc]], [Bmix[pair][qc]])
                    if qc == 3 and h < 3:
                        gate_head(844, h + 1)
                post = (early, late)
                attn_tiles(pipe, osb, qa, qb_, lambda kb: (KS2[:, kb * 128:(kb + 1) * 128], [BKS2[kb // 4]]),
                           list(range(4 * qc + 4)), bias_s, lambda kb: (V76s[:, kb, :], [BVs]), 128, crange=causal_range(qc))
                attn_tiles(pipe, owb, qa, qb_, lambda kb: (KW2[:, kb * 128:(kb + 1) * 128], [BKW2[kb // 4]]),
                           list(range(max(0, 4 * qc - 4), 4 * qc + 4)), table_bias("Tw", qc, 11), lambda kb: (V76w[:, kb, :], [BVw]), 128, post=post, crange=window_range(qc))
        prefetch_w(l, "C", "D")
        pipe.run(skew=3, pskew=3)
        pools["M"] = [5, 6, 7]; pools["O"] = [3, 4]
        store_mix(2)

    def layer(l, xsrc, Bxsrc, xdst, Bxdst):
        for t16 in range(16):
            xi = t16 % 2
            dma(xt[xi][:], xsrc[t16 * 128:(t16 + 1) * 128, :], [Bxsrc], [Bxt[xi]], "xt%d" % xi)
            ssc, Bss = tcol()
            memset(ssc, 0.0, [Bss])
            junk = mlt[0][:].rearrange("p c n -> p (c n)")
            act(junk, xt[xi][:], AF.Square, [Bxt[xi], Bss], [Bmlt[0], Bss], accum_out=ssc)
            act(ssc, ssc, AF.Ln, [Bss], [Bss], scale=1.0 / DM, bias=EPS)
            act(ssc, ssc, AF.Exp, [Bss], [Bss], scale=-0.5)
            xn_, Bxn_ = (xn, BP[0]) if t16 % 2 == 0 else (mlt[1][:].rearrange("p c n -> p (c n)"), Bmlt[1])
            if t16 % 2 == 0:
                Bxn_ = BP[0]
            Bxn_l = [BP[0], BP[1]] if t16 % 2 == 0 else [Bmlt[1]]
            ts("dve", xn_, xt[xi][:], ssc, None, ALU.mult, None, [Bxt[xi], Bss], Bxn_l)
            bank = mbank()
            pv = PS[bank][:].bitcast(BF16)
            for c in range(8):
                op("pe", lambda e, c=c, pv=pv, xn_=xn_: e.transpose(pv[:, c * 128:(c + 1) * 128], xn_[:, c * 128:(c + 1) * 128], identb),
                   Bxn_l + [BCB], [BPS[bank]])
            cp("act" if t16 % 2 else "dve", hT[:, :, t16 * 128:(t16 + 1) * 128], pv.rearrange("p (c t) -> p c t", c=8), [BPS[bank]], [BhT[t16]])
        for m, (name, fn) in enumerate((("A", mixer_a), ("B", mixer_b), ("C", mixer_c), ("D", mixer_d))):
            if name in mixers:
                fn(l)
            else:
                zero_mixer(m)
        load_pieces(l, "O")
        flush_store()
        gpost = sg[:, 0:DM]; Bgpost = Bsg[0]
        dma(gpost, g_post_d[l:l + 1, :].partition_broadcast(128), [], [Bsg[0], Bsg[1]], "cst2")
        xb = [xt[0][:], xt[1][:], qx[:, 0, :].bitcast(F32), qx[:, 1, :].bitcast(F32)]
        Bxb = [[Bxt[0]], [Bxt[1]], BqX[0], BqX[1]]

        def xload(t16):
            xj = t16 % 4
            op("pool", lambda e: e.dma_start(out=xb[xj], in_=xsrc[t16 * 128:(t16 + 1) * 128, :]), [Bxsrc], Bxb[xj], dma="xs%d" % xj)
        for t_ in range(4):
            xload(t_)
        for t16 in range(16):
            xi = t16 % 2
            xj = t16 % 4
            dma(mlt[xi][:, 0:6, :], mix_d[t16, :, 0:6, :], [Bmixd[t16]], [Bmlt[xi]], "mlt%d" % xi)
            b0 = nxt("P5", 8); b1 = nxt("P5", 8)
            ssc, Bss = tcol(); ssc2, Bss2 = tcol()
            for half, bank in ((0, b0), (1, b1)):
                for c in range(8):
                    lt_ = mlt[xi][:, c, :] if c < 6 else mixT[:, c - 6, t16 * 128:(t16 + 1) * 128]
                    lb_ = Bmlt[xi] if c < 6 else Bmix[c - 6][t16 // 4]
                    mm(PS[bank][:, :], lt_, Wbf[:, c, half * 512:(half + 1) * 512], c == 0, c == 7,
                       BWr(half * 512, 512) + [lb_], [BPS[bank]])
            op("dve", lambda e, a=ssc: e.memset(a, 0.0), [], [Bss]); op("dve", lambda e, a=ssc2: e.memset(a, 0.0), [], [Bss2])
            act(xn[:, 0:512], PS[b0][:, :], AF.Square, [BPS[b0], Bss], [BP[0], Bss], accum_out=ssc)
            act(xn[:, 512:1024], PS[b1][:, :], AF.Square, [BPS[b1], Bss2], [BP[1], Bss2], accum_out=ssc2)
            tt("dve", ssc, ssc, ssc2, ALU.add, [Bss, Bss2], [Bss])
            act(ssc, ssc, AF.Ln, [Bss], [Bss], scale=1.0 / DM, bias=EPS)
            act(ssc, ssc, AF.Exp, [Bss], [Bss], scale=-0.5)
            for half, bank in ((0, b0), (1, b1)):
                hs = slice(half * 512, (half + 1) * 512)
                t_, Bt_ = tf()
                stt("dve", t_[:, :], PS[bank][:, :], ssc, gpost[:, hs], ALU.mult, ALU.mult, [BPS[bank], Bss, Bsg[0], Bsg[1]], [Bt_])
                tt("dve", xb[xj][:, hs], xb[xj][:, hs], t_[:, :], ALU.add, [Bt_] + Bxb[xj], Bxb[xj])
            op("pool", lambda e, t16=t16, xj=xj: e.dma_start(out=xdst[t16 * 128:(t16 + 1) * 128, :], in_=xb[xj]), Bxb[xj], [Bxdst], dma="xs%d" % xj)
            if t16 + 4 < 16:
                xload(t16 + 4)

    if nlayers == 1:
        layer(0, x_d, Buf("xin"), out_d, Bout)
    else:
        layer(0, x_d, Buf("xin"), x1_d, Bx1)
        layer(1, x1_d, Bx1, out_d, Bout)
    S.emit()
    if debug:
        print("sbuf bytes remaining", nc.sbuf_bytes_remaining, {k: len(v) for k, v in S.prog.items()})
    return nc


_NC_CACHE = {}


def _in_maps(inputs):
    cf = _build_consts()
    maps = []
    shared = {k: np.ascontiguousarray(np.asarray(v, dtype=np.float32)) for k, v in inputs.items() if k not in ("x", "positions")}
    x = np.asarray(inputs["x"], dtype=np.float32)
    pos = np.asarray(inputs["positions"]).astype(np.int32)
    for b in range(x.shape[0]):
        m = dict(shared)
        m["x"] = np.ascontiguousarray(x[b])
        m["positions"] = np.ascontiguousarray(pos[b:b + 1])
        m["cff"] = cf["_f"]
        m["cfb"] = cf["_b"]
        maps.append(m)
    return maps


def kernel(**inputs):
    if "nc" not in _NC_CACHE:
        _NC_CACHE["nc"] = build()
    nc = _NC_CACHE["nc"]
    res = run_bass_kernel_spmd(nc, _in_maps(inputs), core_ids=list(range(NCORES)))
    return np.stack([np.asarray(r["out"], dtype=np.float32) for r in res.results], axis=0)
```

```python
import math
import numpy as np
import concourse.bass as bass
import concourse.mybir as mybir
from concourse.bass_utils import run_bass_kernel_spmd

F32 = mybir.dt.float32
BF16 = mybir.dt.bfloat16
I32 = mybir.dt.int32
ALU = mybir.AluOpType
AF = mybir.ActivationFunctionType
AX = mybir.AxisListType

SAME_ENGINE_SYNC = True
SEQ = 2048
DM = 1024
DIN = 3628
NEG = -30000.0
EPS = 1e-6
NCORES = 8


class Buf:
    __slots__ = ("name", "w", "r")

    def __init__(self, name):
        self.name = name
        self.w = {}
        self.r = {}


class Sched:
    def __init__(self, nc):
        self.nc = nc
        self.eng = dict(pe=nc.tensor, act=nc.scalar, dve=nc.vector, pool=nc.gpsimd, sp=nc.sync)
        self.prog = {k: [] for k in self.eng}
        self.csem = {k: nc.alloc_semaphore("c_" + k) for k in ("pe", "act", "dve", "pool")}
        self.cnt = {k: 0 for k in self.csem}
        self.waited = {k: {} for k in self.eng}
        self.dsem = {}
        self.dtotal = {}

    def _dma_sem(self, key):
        if key not in self.dsem:
            self.dsem[key] = self.nc.alloc_semaphore("d_" + key)
            self.dtotal[key] = 0
        return self.dsem[key]

    def op(self, e, fn, reads=(), writes=(), dma=None):
        if e != "pe":
            writes = list(writes) + [b for b in reads if b.name.startswith("ps") and b not in writes]
        deps = {}

        def need(tok):
            k, sem, val = tok
            if k not in deps or deps[k][1] < val:
                deps[k] = (sem, val)

        for b in reads:
            for tok in b.w.values():
                need(tok)
        for b in writes:
            for tok in b.w.values():
                need(tok)
            for tok in b.r.values():
                need(tok)
        waits = []
        for k, (sem, val) in deps.items():
            if k in self.dtotal:
                val = self.dtotal[k]
            elif k == e and (e == "pe" or not SAME_ENGINE_SYNC):
                continue
            if self.waited[e].get(k, 0) < val:
                waits.append((sem, val))
                self.waited[e][k] = val
        if dma is not None:
            sem = self._dma_sem(dma)
            self.dtotal[dma] += 16
            tok = (dma, sem, self.dtotal[dma])
            inc = 16
        else:
            self.cnt[e] += 1
            tok = (e, self.csem[e], self.cnt[e])
            inc = 1
        self.prog[e].append((waits, fn, tok[1], inc))
        for b in reads:
            b.r[tok[0]] = tok
        for b in writes:
            b.w[tok[0]] = tok
        return tok

    def emit(self):
        nc = self.nc
        fin = []
        for k, sem in self.dsem.items():
            if self.waited["sp"].get(k, 0) < self.dtotal[k]:
                fin.append((sem, self.dtotal[k]))

        def run(eng, k):
            for waits, fn, sem, inc in self.prog[k]:
                for s, v in waits:
                    eng.wait_ge(s, v)
                fn(eng).then_inc(sem, inc)
            if k == "sp":
                for s, v in fin:
                    eng.wait_ge(s, v)

        with nc.Block() as block:
            @block.tensor
            def _(eng):
                run(eng, "pe")

            @block.scalar
            def _(eng):
                run(eng, "act")

            @block.vector
            def _(eng):
                run(eng, "dve")

            @block.gpsimd
            def _(eng):
                run(eng, "pool")

            @block.sync
            def _(eng):
                run(eng, "sp")


CF = {}


def _build_consts():
    if CF:
        return CF
    import ml_dtypes
    k = np.arange(128)[:, None]
    q = np.arange(128)[None, :]
    offs = {"f": 0, "b": 0}
    cols = {"f": [], "b": []}

    def add(kind, name, arr):
        arr = np.asarray(arr, np.float32)
        a = np.zeros((128, arr.shape[1]), np.float32)
        a[: arr.shape[0]] = arr
        CF[name] = (kind, offs[kind], arr.shape[1])
        cols[kind].append(a)
        offs[kind] += arr.shape[1]

    zero = np.zeros((128, 128)); neg = np.full((128, 128), NEG)
    tri = np.where(q >= k, 0.0, NEG); tris = np.where(q > k, 0.0, NEG); atri = np.where(q < k, 0.0, NEG)
    add("b", "ident", np.eye(128))
    add("b", "Tc", np.concatenate([neg, neg, neg, tri, zero, zero, zero], 1))
    add("b", "Ts", np.concatenate([neg, neg, neg, tris, zero, zero, zero], 1))
    add("b", "Tw", np.concatenate([neg] * 3 + [tri, zero, zero, zero, atri] + [neg] * 3, 1))
    bh, bl = [], []
    for dl in range(-3, 9):
        if dl < 0:
            bh.append(zero); continue
        d = 128 * dl + q - k
        c = ((d >= 0) & (d <= 128)).astype(np.float64) + ((d >= 0) & (d % 4 == 0) & (d <= 512)) + ((d >= 0) & (d % 16 == 0))
        bh.append(c)
    add("b", "Ca", np.concatenate(bh, 1))
    add("b", "negU", np.where(k >= q, -1.0, 0.0))
    add("b", "negones", -np.ones((128, 128)))
    add("b", "onesb", np.ones((128, 128)))
    n = np.arange(127)[:, None]; j = np.arange(32)[None, :]
    add("b", "OV", ((16 * n < 64 * j + 64) & (16 * n + 32 > 64 * j)).astype(np.float32))
    t = np.arange(SEQ)[None, :]
    add("b", "Tcmp", np.where(t >= 16 * n + 31, 0.0, NEG))
    e = np.zeros((32, 16 * 128))
    for kb in range(16):
        for kk in range(128):
            e[2 * kb + (kk >= 64), kb * 128 + kk] = NEG
    add("b", "EXPNEG", e)
    add("f", "ones", np.ones((128, 128)))
    add("f", "identf", np.eye(128))
    nf = np.zeros((128, 16 * 32)); ad = np.zeros((128, 16 * 32))
    for tt in range(16):
        for p in range(128):
            bt = (128 * tt + p) // 64
            for jj in range(32):
                forced = jj == 0 or jj == bt or jj == bt - 1
                fut = jj > bt
                nf[p, tt * 32 + jj] = 0.0 if (forced or fut) else 1.0
                ad[p, tt * 32 + jj] = 1e9 if forced else (-1e9 if fut else 0.0)
    add("f", "NF", nf); add("f", "ADDT", ad)
    sr = np.zeros((128, 12))
    for r in range(12):
        sr[64 + r, r] = 1.0
    add("f", "SELCOL", sr)
    p = np.arange(128)
    inv32 = (10000.0 ** (-(np.arange(32, dtype=np.float32)) / np.float32(32))).astype(np.float32)
    inv16 = (10000.0 ** (-(np.arange(16, dtype=np.float32)) / np.float32(16))).astype(np.float32)
    add("f", "inv32", inv32[p % 32][:, None]); add("f", "inv16", inv16[p % 16][:, None])
    add("f", "sgn32", np.where(p % 64 < 32, -1.0, 1.0)[:, None]); add("f", "sgn16", np.where(p % 32 < 16, -1.0, 1.0)[:, None])
    CF["_f"] = np.concatenate(cols["f"], 1); CF["_b"] = np.concatenate(cols["b"], 1)
    return CF


def build(nlayers=2, debug=False, mixers="ABCD"):
    cf = _build_consts()
    NF_, NB_ = cf["_f"].shape[1], cf["_b"].shape[1]
    nc = bass.Bass("TRN2", target_bir_lowering=False)
    S = Sched(nc)

    def din(name, shape, dt=F32):
        return nc.dram_tensor(name, list(shape), dt, kind="ExternalInput")

    x_d = din("x", (SEQ, DM)); pos_d = din("positions", (1, SEQ), I32)
    w_in_d = din("w_in", (2, DM, DIN)); w_out_d = din("w_out", (2, DM, DM))
    g_pre_d = din("g_pre", (2, DM)); g_post_d = din("g_post", (2, DM))
    gq_d = din("mla_g_q", (2, 256)); gkv_d = din("mla_g_kv", (2, 128))
    wuq_d = din("mla_w_uq", (2, 256, 384)); wukv_d = din("mla_w_ukv", (2, 128, 512))
    posk_d = din("nsa_pos_k", (2, 32, 64)); posv_d = din("nsa_pos_v", (2, 32, 64))
    kw1_d = din("nsa_k_w1", (2, 2048, 256)); kw2_d = din("nsa_k_w2", (2, 256, 64))
    vw1_d = din("nsa_v_w1", (2, 2048, 256)); vw2_d = din("nsa_v_w2", (2, 256, 64))
    cff_d = din("cff", (128, NF_)); cfb_d = din("cfb", (128, NB_))
    out_d = nc.dram_tensor("out", [SEQ, DM], F32, kind="ExternalOutput")
    x1_d = nc.dram_tensor("x1s", [SEQ, DM], F32)
    tab_d = nc.dram_tensor("tabs_dram", [4, 128, SEQ], F32)
    mix_d = nc.dram_tensor("mix_dram", [16, 128, 8, 128], BF16, kind="ExternalOutput" if debug else "Internal")
    dbg_d = nc.dram_tensor("dbg", [4, 128, SEQ], BF16, kind="ExternalOutput") if debug else None
    ocmp_d = nc.dram_tensor("ocmp_dram", [16, 64, 512], F32)
    Bocmp = [Buf("ocmp%d" % i) for i in range(16)]
    Bx1 = Buf("x1"); Btab = Buf("tab"); Bout = Buf("out"); Bmixd = [Buf("mixd%d" % i) for i in range(16)]

    _cnt = [0]

    def sb(shape, dt, name=None):
        _cnt[0] += 1
        return nc.alloc_sbuf_tensor(name or ("t%d" % _cnt[0]), list(shape), dt)

    def op(e, fn, reads=(), writes=(), dma=None):
        return S.op(e, fn, reads, writes, dma)

    def dma(out, in_, reads, writes, key, **kw):
        op("sp", lambda e: e.dma_start(out=out, in_=in_, **kw), reads, writes, dma=key)

    def mm(out, lhsT, rhs, start, stop, reads, writes):
        op("pe", lambda e: e.matmul(out, lhsT=lhsT, rhs=rhs, start=start, stop=stop, skip_group_check=True), reads, writes)

    def act(out, in_, func, reads, writes, **kw):
        op("act", lambda e: e.activation(out=out, in_=in_, func=func, **kw), reads, writes)

    def tt(eng, out, in0, in1, alu, reads, writes):
        op(eng, lambda e: e.tensor_tensor(out=out, in0=in0, in1=in1, op=alu), reads, writes)

    def ts(eng, out, in0, s1, s2, op0, op1, reads, writes):
        if op1 is None:
            op(eng, lambda e: e.tensor_single_scalar(out=out, in_=in0, scalar=s1, op=op0), reads, writes)
        else:
            op(eng, lambda e: e.tensor_scalar(out=out, in0=in0, scalar1=s1, scalar2=s2, op0=op0, op1=op1), reads, writes)

    def stt(eng, out, in0, scalar, in1, op0, op1, reads, writes):
        op(eng, lambda e: e.scalar_tensor_tensor(out=out, in0=in0, scalar=scalar, in1=in1, op0=op0, op1=op1), reads, writes)

    def cp(eng, out, in_, reads, writes):
        if eng == "act":
            op(eng, lambda e: e.activation(out=out, in_=in_, func=AF.Copy), reads, writes)
        else:
            op(eng, lambda e: e.tensor_copy(out=out, in_=in_), reads, writes)

    def recip(out, in_, reads, writes):
        op("dve", lambda e: e.reciprocal(out=out, in_=in_), reads, writes)

    def recip_act(out, in_, reads, writes):
        act(out, in_, AF.Ln, reads, writes)
        act(out, out, AF.Exp, writes, writes, scale=-1.0)

    def memset(ap, val, writes):
        op("pool", lambda e: e.memset(ap, val), [], writes)

    def bc2(ap2):
        a = ap2.ap
        return bass.AP(ap2.tensor, ap2.offset, [list(a[0]), [0, 2], list(a[1])])

    def bc_last(ap2, n):
        a = ap2.ap
        return bass.AP(ap2.tensor, ap2.offset, [list(a[0]), list(a[1]), [0, n]])

    rot = {}

    def nxt(kind, n):
        v = rot.get(kind, 0)
        rot[kind] = (v + 1) % n
        return v

    cff = sb((128, NF_), F32, "cff_sb"); Bcf = Buf("cf")
    dma(cff[:], cff_d[:], [], [Bcf], "cst")
    CB = sb((128, NB_), BF16, "CB"); BCB = Buf("CB")
    stage = [sb((128, 8, 128), F32, "stage%d" % i) for i in range(2)]; Bst = [Buf("st%d" % i) for i in range(2)]

    def stage_flat(si, r0, r1, n):
        return stage[si][r0:r1].rearrange("p c n -> p (c n)")[:, 0:n]

    o = 0
    while o < NB_:
        n = min(1024, NB_ - o)
        si = nxt("st", 2)
        dma(stage_flat(si, 0, 128, n), cfb_d[:, o:o + n], [], [Bst[si]], "st%d" % si)
        cp("pool", CB[:, o:o + n], stage_flat(si, 0, 128, n), [Bst[si]], [BCB])
        o += n

    def cfa(name, r0=0, r1=128, c0=0, c1=None):
        _, o_, n_ = cf[name]
        c1 = n_ if c1 is None else c1
        return cff[r0:r1, o_ + c0:o_ + c1]

    def cba(name, r0=0, r1=128, c0=0, c1=None):
        _, o_, n_ = cf[name]
        c1 = n_ if c1 is None else c1
        return CB[r0:r1, o_ + c0:o_ + c1]

    identb = cba("ident")
    PS = [nc.alloc_psum_tensor("ps%d" % i, [128, 512], F32) for i in range(8)]
    BPS = [Buf("ps%d" % i) for i in range(8)]

    hT = sb((128, 8, SEQ), BF16, "hT"); BhT = [Buf("hT%d" % i) for i in range(16)]
    mixT = sb((128, 2, SEQ), BF16, "mixT"); Bmix = [[Buf("mix%d_%d" % (c, qc)) for qc in range(4)] for c in range(2)]
    mlt = [sb((128, 8, 128), BF16, "mlt%d" % i) for i in range(2)]; Bmlt = [Buf("mlt%d" % i) for i in range(2)]
    Wbf = sb((128, 8, 1100), BF16, "Wbf"); BWp = [Buf("W%d" % i) for i in range(9)]

    def BWr(c0, n):
        return BWp[c0 // 128:(c0 + n - 1) // 128 + 1]
    pend = {"store": None}

    def flush_store():
        if pend["store"] is not None:
            f = pend["store"]; pend["store"] = None
            f()
    xt = [sb((128, DM), F32, "xt%d" % i) for i in range(2)]; Bxt = [Buf("xt%d" % i) for i in range(2)]
    gpre = sb((128, 16), F32, "gpre"); Bgpre = Buf("gpre")
    dma(gpre[:, 0:8], g_pre_d[0].rearrange("(c p) -> p c", p=128), [], [Bgpre], "cst", allow_slow_non_contiguous=True)
    dma(gpre[:, 8:16], g_pre_d[1].rearrange("(c p) -> p c", p=128), [], [Bgpre], "cst", allow_slow_non_contiguous=True)
    small = sb((128, 64), F32, "small"); Bsmall = [Buf("small%d" % i) for i in range(64)]
    tabs = [sb((128, 2, 512), F32, "tabs%d" % i) for i in range(2)]; Btabs = [Buf("tabs%d" % i) for i in range(2)]
    sg = sb((128, SEQ), F32, "sg"); Bsg = [Buf("sg%d" % i) for i in range(4)]
    qT = sb((128, 2, SEQ), BF16, "qT"); BqT = [[Buf("qT%d_%d" % (a, b)) for b in range(4)] for a in range(2)]
    kT = sb((128, 2, SEQ), BF16, "kT"); BkT = [[Buf("kT%d_%d" % (a, b)) for b in range(4)] for a in range(2)]
    qx = sb((128, 2, SEQ), BF16, "qx"); BqX = [[Buf("qx%d_%d" % (a, b)) for b in range(4)] for a in range(2)]
    kx = sb((128, 2, SEQ), BF16, "kx"); BkX = [[Buf("kx%d_%d" % (a, b)) for b in range(4)] for a in range(2)]
    Vaug = sb((128, 16, 4, 128), BF16, "Vaug"); BV = [Buf("V%d" % i) for i in range(16)]
    Vv = Vaug[:].rearrange("p t (a h) n -> p t a h n", h=2)
    Pbig = sb((128, 4, 512), BF16, "Pbig")
    Pt = [Pbig[:, i, :] for i in range(4)]; BP = [Buf("P%d" % i) for i in range(4)]
    xn = Pbig[:, 0:2, :].rearrange("p a n -> p (a n)")
    NT = 5
    tmpf = [sb((128, 512), F32, "tmpf%d" % i) for i in range(NT)]; Btmp = [Buf("tmpf%d" % i) for i in range(NT)]
    ARENA_N = 14 * 1024
    arena = sb((128, ARENA_N), BF16, "arena")
    ar = {"ptr": 0, "cur": [], "prev": []}

    def arena_reset():
        ar["ptr"] = 0
        ar["prev"] = ar["prev"] + ar["cur"]
        ar["cur"] = []

    def carve(nelem, dt, name):
        nb = nelem * (4 if dt in (F32, I32) else 2)
        nb = (nb + 63) // 64 * 64
        o_ = ar["ptr"]
        assert o_ + nb // 2 <= ARENA_N, ("arena overflow", name)
        ar["ptr"] += nb // 2
        a = arena[:, o_:o_ + nb // 2]
        if dt != BF16:
            a = a.bitcast(dt)
        B = Buf(name)
        ar["cur"].append(B)
        return a[:, 0:nelem], B

    def arena_gate():
        memset(small[:, 63:64], 0.0, ar["prev"] + ar["cur"])
        ar["prev"] = []

    def tcol():
        i = nxt("sm", 60)
        return small[:, i:i + 1], Bsmall[i]

    def tf():
        i = nxt("T", NT)
        return tmpf[i], Btmp[i]

    def pbuf():
        i = nxt("P", 4)
        return Pt[i], BP[i]

    pools = {"M": [5, 6, 7], "O": [3, 4]}

    def mbank():
        p = pools["M"]
        return p[nxt("M%d" % len(p), len(p))]

    def sbank():
        return nxt("S", 3)

    def obank():
        p = pools["O"]
        return p[nxt("O%d" % len(p), len(p))]

    TWO_PI = 2.0 * math.pi
    for c4 in range(4):
        cs = slice(c4 * 512, (c4 + 1) * 512)
        ki, Bki = xt[0][:, 0:512], Bxt[0]; pf, Bpf = xt[0][:, 512:1024], Bxt[0]
        kii = ki.bitcast(I32)
        dma(kii, pos_d[0:1, cs].partition_broadcast(128), [], [Bki], "tabp")
        cp("dve", pf[:], kii, [Bki], [Bpf])
        for ti, (invn, sgnn, phase) in enumerate([("inv32", None, math.pi / 2), ("inv32", "sgn32", 0.0),
                                                  ("inv16", None, math.pi / 2), ("inv16", "sgn16", 0.0)]):
            an, Ban = xt[1][:, 0:512], Bxt[1]; tb, Btb = xt[1][:, 512:1024], Bxt[1]
            ts("dve", an[:], pf[:], cfa(invn), phase, ALU.mult, ALU.add, [Bpf, Bcf], [Ban])
            ts("dve", tb[:], an[:], 1.0 / TWO_PI, None, ALU.mult, None, [Ban], [Btb])
            cp("dve", kii, tb[:], [Btb], [Bki])
            cp("dve", tb[:], kii, [Bki], [Btb])
            stt("dve", an[:], tb[:], -TWO_PI, an[:], ALU.mult, ALU.add, [Btb, Ban], [Ban])
            ts("dve", an[:], an[:], math.pi, -math.pi, ALU.min, ALU.max, [Ban], [Ban])
            act(tb[:], an[:], AF.Sin, [Ban], [Btb])
            if sgnn:
                ts("dve", tb[:], tb[:], cfa(sgnn), None, ALU.mult, None, [Btb, Bcf], [Btb])
            dma(tab_d[ti, :, cs], tb[:], [Btb], [Btab], "tab")

    def load_w(l, pieces):
        for (sc, n, dc) in pieces:
            si = nxt("st", 2)
            dma(stage[si][:, :, 0:n], w_in_d[l, :, sc:sc + n].rearrange("(c p) n -> p c n", p=128), [], [Bst[si]], "st%d" % si)
            tt("pool", Wbf[:, :, dc:dc + n], stage[si][:, :, 0:n], bc_last(gpre[:, l * 8:(l + 1) * 8], n), ALU.mult,
               [Bst[si], Bgpre], BWr(dc, n))

    def pieces_range(s0, n, d0):
        out = []
        o_ = 0
        while o_ < n:
            m = min(128, n - o_)
            out.append((s0 + o_, m, d0 + o_))
            o_ += m
        return out

    C_PCS = [(2144, 64, 704), (2272, 64, 768), (2336, 12, 832), (2348, 128, 844), (2476, 128, 972), (2016, 64, 640),
             (1696, 128, 0), (1824, 128, 128), (1952, 64, 256), (1952, 64, 320), (2080, 64, 384), (2080, 64, 448),
             (2208, 64, 512), (2208, 64, 576)]
    W_PLAN = {"A": pieces_range(0, 1024, 0), "B": pieces_range(1024, 672, 0), "C": C_PCS, "D": pieces_range(2604, 1024, 0),
              "O": [(p * 128, 128, p * 128) for p in range(8)]}
    W_GATE = {"A": (768, 1024), "B": (416, 672), "C": (832, 1100), "D": (768, 1024)}
    w_done = set()

    def load_pieces(l, name, filt=None):
        for p in W_PLAN[name]:
            key = (l, name, p)
            if key in w_done or (filt is not None and not filt(p)):
                continue
            w_done.add(key)
            if name == "O":
                si = nxt("st", 2)
                dma(stage[si][:, :, :], w_out_d[l, :, p[0]:p[0] + 128].rearrange("(c p) n -> p c n", p=128), [], [Bst[si]], "st%d" % si)
                cp("pool", Wbf[:, :, p[2]:p[2] + 128], stage[si][:, :, :], [Bst[si]], BWr(p[2], 128))
            else:
                load_w(l, [p])

    def prefetch_w(l, cur, nxt_):
        g0, g1 = W_GATE[cur]
        load_pieces(l, nxt_, lambda p: p[2] + p[1] <= g0 or p[2] >= g1)

    def proj_fm(col0, M, tc, bank):
        for c in range(8):
            mm(PS[bank][0:M, :], Wbf[:, c, col0:col0 + M], hT[:, c, tc * 512:(tc + 1) * 512], c == 0, c == 7,
               BWr(col0, M) + BhT[4 * tc:4 * tc + 4], [BPS[bank]])

    def load_tabs(which, tc):
        i = nxt("tab", 2)
        cs = slice(tc * 512, (tc + 1) * 512)
        dma(tabs[i][:, 0, :], tab_d[2 * which, :, cs], [Btab], [Btabs[i]], "tabl%d" % i)
        dma(tabs[i][:, 1, :], tab_d[2 * which + 1, :, cs], [Btab], [Btabs[i]], "tabl%d" % i)
        return tabs[i], Btabs[i]

    def rope_evac(bank, dst, Bdst, tab, Btb_, scale, split=None):
        t1, B1 = tf(); t2, B2 = tf()
        stt("dve", t1[:, :], PS[bank][:, :], scale, tab[:, 0, :], ALU.mult, ALU.mult, [BPS[bank], Btb_], [B1])
        for b in range(4):
            src = b + 1 if b % 2 == 0 else b - 1
            if True:
                act(t2[32 * b:32 * b + 32, :], PS[bank][32 * src:32 * src + 32, :], AF.Copy, [BPS[bank]], [B2], scale=scale)
            else:
                ts("dve", t2[32 * b:32 * b + 32, :], PS[bank][32 * src:32 * src + 32, :], scale, None, ALU.mult, None, [BPS[bank]], [B2])
        tt("dve", t2[:, :], t2[:, :], tab[:, 1, :], ALU.mult, [B2, Btb_], [B2])
        if split is None:
            tt("pool", dst, t1[:, :], t2[:, :], ALU.add, [B1, B2], [Bdst])
        else:
            for (r0, d_, Bd_) in split:
                tt("pool", d_[r0:r0 + 64, :], t1[r0:r0 + 64, :], t2[r0:r0 + 64, :], ALU.add, [B1, B2], [Bd_])

    def proj_v(col0, ncol, dst_fn):
        for t16 in range(16):
            bank = mbank()
            for c in range(8):
                mm(PS[bank][:, 0:ncol], hT[:, c, t16 * 128:(t16 + 1) * 128], Wbf[:, c, col0:col0 + ncol], c == 0, c == 7,
                   BWr(col0, ncol) + [BhT[t16]], [BPS[bank]])
            dst_fn(t16, bank)

    def v_heads(t16, bank):
        pv4 = PS[bank][:, 0:256].rearrange("p (a h d) -> p a h d", a=2, h=2)
        cp("dve", Vv[:, t16, :, 0, 64:128], pv4[:, :, 0, :], [BPS[bank]], [BV[t16]])
        cp("act", Vv[:, t16, :, 1, 0:64], pv4[:, :, 1, :], [BPS[bank]], [BV[t16]])

    def v_ones():
        memset(Vv[:, :, :, 0, 0:64], 1.0, BV)
        memset(Vv[:, :, :, 1, 64:128], 1.0, BV)

    def gate_head(col0, h):
        if h % 2:
            return
        for tc in range(4):
            bank = mbank()
            proj_fm(col0 + 64 * h, 128, tc, bank)
            act(sg[:, tc * 512:(tc + 1) * 512], PS[bank][:, :], AF.Silu, [BPS[bank]], [Bsg[tc]])

    def evac_rows(ob, po, dst, Bdst, src0=None, engs=("dve", "act")):
        if src0 is None:
            src0 = 64 - po
        for b in range(2):
            cp(engs[b], dst[po + 32 * b:po + 32 * b + 32, :], PS[ob][src0 + 32 * b:src0 + 32 * b + 32, :], [BPS[ob]], [Bdst])

    def epilogue_norm(ob, h, qc, light_act=False):
        po = 64 * (h % 2)
        cs = slice(qc * 512, (qc + 1) * 512)
        st = {}

        def early():
            st["t1"] = tf(); st["t2"] = tf()
            t1, B1 = st["t1"]; t2, B2 = st["t2"]
            if light_act:
                recip(t1[po:po + 64, :], PS[ob][po:po + 64, :], [BPS[ob]], [B1])
                evac_rows(ob, po, t2, B2, engs=("dve", "dve"))
            else:
                recip_act(t1[po:po + 64, :], PS[ob][po:po + 64, :], [BPS[ob]], [B1])
                evac_rows(ob, po, t2, B2, engs=("dve", "dve"))

        def late():
            t1, B1 = st["t1"]; t2, B2 = st["t2"]
            tt("pool", t2[po:po + 64, :], t2[po:po + 64, :], sg[po:po + 64, cs], ALU.mult, [B2, Bsg[qc]], [B2])
            tt("dve", mixT[po:po + 64, h // 2, cs], t1[po:po + 64, :], t2[po:po + 64, :], ALU.mult, [B1, B2], [Bmix[h // 2][qc]])
        return early, late

    class Pipe:
        def __init__(self):
            self.items = []

        def add(self, A, B, C, post=None):
            self.items.append((A, B, C, post))

        def run(self, skew=2, pskew=3):
            n = len(self.items)
            late = {}
            for t in range(n + skew + pskew + 1):
                if t < n:
                    self.items[t][0]()
                    self.items[t][1]()
                j = t - skew
                if 0 <= j < n:
                    self.items[j][2]()
                    if self.items[j][3]:
                        pa, pb = self.items[j][3]
                        pa()
                        late.setdefault(min(t + pskew, n + skew + pskew), []).append(pb)
                for f in late.pop(t, []):
                    f()

    def attn_tiles(pipe, ob, qa, qbufs, kfn, kbs, bias_fn, vfn, M, rows=128, maskfn=None, post=None, crange=None):
        for i, kb in enumerate(kbs):
            sbk = sbank()
            P_, BP_ = pbuf()
            c0, c1 = crange(kb) if crange is not None else (0, 512)

            def A(kb=kb, sbk=sbk, c0=c0, c1=c1):
                ka, kbufs = kfn(kb)
                bl = bias_fn(kb)
                mm(PS[sbk][0:rows, c0:c1], ka, qa[:, c0:c1], True, len(bl) == 0, qbufs + kbufs, [BPS[sbk]])
                for bi, (bl_l, bl_r, bl_b) in enumerate(bl):
                    mm(PS[sbk][0:rows, c0:c1], bl_l, bl_r[:, c0:c1], False, bi == len(bl) - 1, bl_b, [BPS[sbk]])

            def B(kb=kb, sbk=sbk, P_=P_, BP_=BP_, c0=c0, c1=c1):
                act(P_[0:rows, c0:c1], PS[sbk][0:rows, c0:c1], AF.Exp, [BPS[sbk]], [BP_])
                if maskfn is not None:
                    ma, mb = maskfn(kb)
                    tt("dve", P_[0:rows, c0:c1], P_[0:rows, c0:c1], ma[:, c0:c1], ALU.mult, [BP_] + mb, [BP_])

            def C(kb=kb, i=i, P_=P_, BP_=BP_, c0=c0, c1=c1):
                va, vbufs = vfn(kb)
                mm(PS[ob][0:M, c0:c1], va, P_[0:rows, c0:c1], i == 0, i == len(kbs) - 1, vbufs + [BP_], [BPS[ob]])

            pipe.add(A, B, C, post if i == len(kbs) - 1 else None)

    def causal_range(qc):
        def f(kb):
            d = kb - 4 * qc
            return (128 * d, 512) if d > 0 else (0, 512)
        return f

    def window_range(qc):
        def f(kb):
            d = kb - 4 * qc
            if d >= 0:
                return (128 * d, 512)
            return (0, 128 * (d + 5))
        return f

    def table_bias(name, qc, nblk):
        def f(kb):
            d0 = 4 * qc - kb + 3
            if d0 + 4 > nblk:
                d0 = nblk - 4
            return [(identb, cba(name, 0, 128, d0 * 128, d0 * 128 + 512), [BCB])]
        return f

    def zero_kz():
        for h in range(4):
            r0 = 64 * (1 - h % 2)
            for tc in range(4):
                memset(kB[h][r0:r0 + 64, tc * 512:(tc + 1) * 512], 0.0, [BkB[h][tc]])

    def store_mix(m):
        def f():
            if m == 3 and not debug:
                return
            for t16 in range(16):
                dma(mix_d[t16, :, 2 * m:2 * m + 2, :], mixT[:, :, t16 * 128:(t16 + 1) * 128],
                    [Bmix[0][t16 // 4], Bmix[1][t16 // 4]], [Bmixd[t16]], "mixst")
        flush_store()
        pend["store"] = f

    def zero_mixer(m):
        for c in range(2):
            for qc in range(4):
                memset(mixT[:, c, qc * 512:(qc + 1) * 512], 0.0, [Bmix[c][qc]])
        store_mix(m)

    def mixer_a(l):
        import os
        KSTOP = int(os.environ.get("KSTOP", "99"))
        load_pieces(l, "A")
        flush_store()
        if KSTOP <= 1:
            return zero_mixer(0)
        v_ones()
        zero_kz()
        proj_v(512, 256, v_heads)
        if KSTOP <= 2:
            return zero_mixer(0)
        for tc in range(4):
            tab, Btb_ = load_tabs(0, tc)
            cs = slice(tc * 512, (tc + 1) * 512)
            for pair in range(2):
                bank = mbank()
                proj_fm(128 * pair, 128, tc, bank)
                rope_evac(bank, qT[:, pair, cs], BqT[pair][tc], tab, Btb_, 0.125)
                bank = mbank()
                proj_fm(256 + 128 * pair, 128, tc, bank)
                rope_evac(bank, None, None, tab, Btb_, 1.0,
                          split=[(0, kB[2 * pair][:, cs], BkB[2 * pair][tc]), (64, kB[2 * pair + 1][:, cs], BkB[2 * pair + 1][tc])])
        pipe = Pipe()
        gate_head(768, 0)
        for h in range(4):
            pair, po = h // 2, 64 * (h % 2)
            for qc in range(4):
                ob = obank()

                def mask_a(kb, qc=qc):
                    d0 = min(4 * qc - kb + 3, 8)
                    return cba("Ca", 0, 128, d0 * 128, d0 * 128 + 512), [BCB]

                e_, l_ = epilogue_norm(ob, h, qc)

                def late(l_=l_, h=h, qc=qc):
                    l_()
                    if qc == 3 and h < 3:
                        gate_head(768, h + 1)
                post = (e_, late)
                attn_tiles(pipe, ob, qT[:, pair, qc * 512:(qc + 1) * 512], [BqT[pair][qc]],
                           lambda kb, h=h: (kB[h][:, kb * 128:(kb + 1) * 128], [BkB[h][kb // 4]]),
                           list(range(4 * qc + 4)), lambda kb: [],
                           lambda kb, h=h: (Vaug[:, kb, h, :], [BV[kb]]), 128, maskfn=mask_a, post=post, crange=causal_range(qc))
        prefetch_w(l, "A", "B")
        pipe.run(skew=3)
        store_mix(0)

    def mixer_d(l):
        arena_reset()
        Ssum, BSs = carve(512, F32, "Ssum"); Ssb, BSsb = carve(512, BF16, "Ssb")
        spb = [carve(512, BF16, "spb%d" % i) for i in range(3)]
        Ssb2 = [(Ssb, BSsb), carve(512, BF16, "Ssb1")]
        load_pieces(l, "D")
        flush_store()
        arena_gate()
        zero_kz()
        proj_v(512, 256, v_heads)
        for tc in range(4):
            cs = slice(tc * 512, (tc + 1) * 512)
            for pair in range(2):
                bank = mbank()
                proj_fm(128 * pair, 128, tc, bank)
                act(qT[:, pair, cs], PS[bank][:, :], AF.Copy, [BPS[bank]], [BqT[pair][tc]], scale=0.125)
                bank = mbank()
                proj_fm(256 + 128 * pair, 128, tc, bank)
                cp("dve", kB[2 * pair][0:64, cs], PS[bank][0:64, :], [BPS[bank]], [BkB[2 * pair][tc]])
                cp("dve", kB[2 * pair + 1][64:128, cs], PS[bank][64:128, :], [BPS[bank]], [BkB[2 * pair + 1][tc]])
        negU = cba("negU"); negones = cba("negones")
        tiles = []
        for h in range(4):
            pair, po = h // 2, 64 * (h % 2)
            for qc in range(4):
                ob = obank()
                kbs = list(range(4 * qc + 3, -1, -1))
                for i, kb in enumerate(kbs):
                    tiles.append(dict(h=h, pair=pair, po=po, qc=qc, ob=ob, i=i, kb=kb, n=len(kbs), zb=sbank(), eb=mbank(),
                                      sp=spb[len(tiles) % 3], ssb=Ssb2[len(tiles) % 2], P=pbuf()))

        def stA(T):
            cs = slice(T["qc"] * 512, (T["qc"] + 1) * 512); ks_ = slice(T["kb"] * 128, (T["kb"] + 1) * 128)
            po, pair, zb = T["po"], T["pair"], T["zb"]
            diag = T["kb"] >= 4 * T["qc"]; d0 = 4 * T["qc"] - T["kb"] + 3
            rd = [BqT[pair][T["qc"]], BkB[T["h"]][T["kb"] // 4]]
            c0 = 128 * max(0, T["kb"] - 4 * T["qc"]); T["c0"] = c0
            qs = slice(T["qc"] * 512 + c0, (T["qc"] + 1) * 512)
            mm(PS[zb][:, c0:], kB[T["h"]][:, ks_], qT[:, pair, qs], True, not diag, rd, [BPS[zb]])
            if diag:
                mm(PS[zb][:, c0:], identb, cba("Ts", 0, 128, d0 * 128 + c0, d0 * 128 + 512), False, True, [BCB], [BPS[zb]])

        def stB(T):
            t1, B1 = tf()
            c0 = T["c0"]
            act(t1[:, c0:], PS[T["zb"]][:, c0:], AF.Exp, [BPS[T["zb"]]], [B1])
            sp_, Bsp_ = T["sp"]
            act(sp_[:, c0:], t1[:, c0:], AF.Ln, [B1], [Bsp_], bias=1.0)

        def stC(T, prev):
            cs = slice(T["qc"] * 512, (T["qc"] + 1) * 512); ks_ = slice(T["kb"] * 128, (T["kb"] + 1) * 128)
            po, pair, eb = T["po"], T["pair"], T["eb"]
            diag = T["kb"] >= 4 * T["qc"]; d0 = 4 * T["qc"] - T["kb"] + 3
            rd = [BqT[pair][T["qc"]], BkB[T["h"]][T["kb"] // 4]]
            sp_, Bsp_ = T["sp"]
            c0 = T["c0"]
            qs = slice(T["qc"] * 512 + c0, (T["qc"] + 1) * 512)
            mm(PS[eb][:, c0:], kB[T["h"]][:, ks_], qT[:, pair, qs], True, False, rd, [BPS[eb]])
            if diag:
                mm(PS[eb][:, c0:], identb, cba("Ts", 0, 128, d0 * 128 + c0, d0 * 128 + 512), False, False, [BCB], [BPS[eb]])
            mm(PS[eb][:, c0:], negU, sp_[:, c0:], False, T["i"] == 0, [BCB, Bsp_], [BPS[eb]])
            if T["i"] > 0:
                ssb_, Bssb_ = prev["ssb"]
                mm(PS[eb][:, c0:], negones, ssb_[:, c0:], False, True, [BCB, Bssb_], [BPS[eb]])
            P_, BP_ = T["P"]
            act(P_[:, c0:], PS[eb][:, c0:], AF.Exp, [BPS[eb]], [BP_])

        def stU(T):
            if T["i"] == T["n"] - 1:
                return
            sp_, Bsp_ = T["sp"]; ssb_, Bssb_ = T["ssb"]
            c0 = T["c0"]
            if T["i"] == 0:
                if c0 > 0:
                    memset(Ssum[:, 0:c0], 0.0, [BSs])
                cp("dve", Ssum[:, c0:], sp_[:, c0:], [Bsp_], [BSs])
            else:
                tt("dve", Ssum[:, c0:], Ssum[:, c0:], sp_[:, c0:], ALU.add, [BSs, Bsp_], [BSs])
            cp("dve", ssb_, Ssum, [BSs], [Bssb_])

        def stE(T):
            P_, BP_ = T["P"]
            ob, h, kb = T["ob"], T["h"], T["kb"]
            c0 = T["c0"]
            mm(PS[ob][:, c0:], Vaug[:, kb, h, :], P_[:, c0:], T["i"] == 0, T["i"] == T["n"] - 1, [BV[kb], BP_], [BPS[ob]])
            if T["i"] == T["n"] - 1:
                cs = slice(T["qc"] * 512, (T["qc"] + 1) * 512)
                t3, B3 = tf()
                po = T["po"]
                evac_rows(ob, po, t3, B3, engs=("dve", "dve"))
                tt("pool", mixT[po:po + 64, T["pair"], cs], t3[po:po + 64, :], sg[po:po + 64, cs], ALU.mult, [B3, Bsg[T["qc"]]], [Bmix[T["pair"]][T["qc"]]])
                if T["qc"] == 3 and h < 3:
                    gate_head(768, h + 1)

        gate_head(768, 0)
        prefetch_w(l, "D", "O")
        n = len(tiles)
        for t in range(n + 2):
            if t < n:
                stA(tiles[t]); stB(tiles[t])
            if 0 <= t - 1 < n:
                stC(tiles[t - 1], tiles[t - 2] if t - 2 >= 0 else None)
            if t < n:
                stU(tiles[t])
            if 0 <= t - 2 < n:
                stE(tiles[t - 2])
        store_mix(3)

    qB = [qT[:, 0, :], qT[:, 1, :], qx[:, 0, :], qx[:, 1, :]]
    kB = [kT[:, 0, :], kT[:, 1, :], kx[:, 0, :], kx[:, 1, :]]
    BqB = [BqT[0], BqT[1], BqX[0], BqX[1]]
    BkB = [BkT[0], BkT[1], BkX[0], BkX[1]]

    def mixer_b(l):
        arena_reset()
        uq_st, Buqst = carve(768, F32, "uq_st"); uq_st = uq_st.rearrange("p (c n) -> p c n", c=2)
        ukv_st, Bukvst = carve(512, F32, "ukv_st")
        Wuq, BWuq = carve(768, BF16, "Wuq"); Wuq = Wuq.rearrange("p (c n) -> p c n", c=2)
        Wuqs, _ = carve(768, BF16, "Wuqs"); Wuqs = Wuqs.rearrange("p (c n) -> p c n", c=2)
        Wkp, BWkv = carve(384, BF16, "Wkp"); Wkp = Wkp.rearrange("p (h n) -> p h n", h=4)
        Wv, _ = carve(256, BF16, "Wv")
        Wkr, BWkr = carve(768, BF16, "Wkr"); Wkr = Wkr.rearrange("p (c n) -> p c n", c=8)
        Wkrs, _ = carve(768, BF16, "Wkrs"); Wkrs = Wkrs.rearrange("p (c n) -> p c n", c=8)
        gqk, Bgqk = carve(4, F32, "gqk")
        cqn, Bcqn = carve(1536, BF16, "cqn"); cqn = cqn.rearrange("p (c n) -> p c n", c=3)
        krr, Bkrr = carve(SEQ, BF16, "krr")
        load_pieces(l, "B")
        flush_store()
        arena_gate()
        dma(gqk[:, 0:2], gq_d[l].rearrange("(c p) -> p c", p=128), [], [Bgqk], "cst3", allow_slow_non_contiguous=True)
        dma(gqk[:, 2:3], gkv_d[l].rearrange("(c p) -> p c", p=128), [], [Bgqk], "cst3", allow_slow_non_contiguous=True)
        dma(uq_st, wuq_d[l].rearrange("(c p) n -> p c n", p=128), [], [Buqst], "cst3")
        dma(ukv_st, wukv_d[l], [], [Bukvst], "cst3")
        tt("pool", Wuq, uq_st, bc_last(gqk[:, 0:2], 384), ALU.mult, [Buqst, Bgqk], [BWuq])
        cp("pool", Wuqs, Wuq, [BWuq], [BWuq])
        for hh in range(4):
            b = hh * 96 + 64
            cp("pool", Wuqs[:, :, b:b + 16], Wuq[:, :, b + 16:b + 32], [BWuq], [BWuq])
            cp("pool", Wuqs[:, :, b + 16:b + 32], Wuq[:, :, b:b + 16], [BWuq], [BWuq])
        memset(Wkp, 0.0, [BWkv])
        ukv4 = ukv_st.rearrange("p (h d) -> p h d", h=4)
        ts("dve", Wkp[:, :, 0:64], ukv4[:, :, 0:64], gqk[:, 2:3], None, ALU.mult, None, [Bukvst, Bgqk], [BWkv])
        ts("dve", Wv.rearrange("p (h d) -> p h d", h=4), ukv4[:, :, 64:128], gqk[:, 2:3], None, ALU.mult, None, [Bukvst, Bgqk], [BWkv])
        memset(Wkr, 0.0, [BWkr]); memset(Wkrs, 0.0, [BWkr])
        cp("pool", Wkr[:, :, 64:96], Wbf[:, :, 384:416], BWr(384, 32), [BWkr])
        cp("pool", Wkrs[:, :, 64:80], Wbf[:, :, 400:416], BWr(384, 32), [BWkr])
        cp("pool", Wkrs[:, :, 80:96], Wbf[:, :, 384:400], BWr(384, 32), [BWkr])
        v_ones()
        sc = 96.0 ** -0.5
        for tc in range(4):
            cs = slice(tc * 512, (tc + 1) * 512)
            tab, Btb_ = load_tabs(1, tc)
            banks = [mbank(), mbank(), sbank()]
            sq = []
            for j in range(3):
                proj_fm(128 * j, 128, tc, banks[j])
                t_, B_ = tf()
                act(t_[:, :], PS[banks[j]][:, :], AF.Square, [BPS[banks[j]]], [B_])
                sq.append((t_, B_))
            ssq = obank(); ssk = obank()
            mm(PS[ssq][:, :], cfa("ones"), sq[0][0][:, :], True, False, [Bcf, sq[0][1]], [BPS[ssq]])
            mm(PS[ssq][:, :], cfa("ones"), sq[1][0][:, :], False, True, [Bcf, sq[1][1]], [BPS[ssq]])
            mm(PS[ssk][:, :], cfa("ones"), sq[2][0][:, :], True, True, [Bcf, sq[2][1]], [BPS[ssk]])
            for (sbk, denom, js) in ((ssq, 256.0, (0, 1)), (ssk, 128.0, (2,))):
                r_, Br_ = tf()
                act(r_[:, :], PS[sbk][:, :], AF.Ln, [BPS[sbk]], [Br_], scale=1.0 / denom, bias=EPS)
                act(r_[:, :], r_[:, :], AF.Exp, [Br_], [Br_], scale=-0.5)
                for j in js:
                    tt("dve", cqn[:, j, :], PS[banks[j]][:, :], r_[:, :], ALU.mult, [BPS[banks[j]], Br_], [Bcqn])
            b1 = mbank(); b2 = mbank()
            for (bank, W_) in ((b1, Wkr), (b2, Wkrs)):
                for c in range(8):
                    mm(PS[bank][0:96, :], W_[:, c, :], hT[:, c, cs], c == 0, c == 7, [BWkr] + BhT[4 * tc:4 * tc + 4], [BPS[bank]])
            t1, B1 = tf(); t2, B2 = tf()
            tt("dve", t1[64:96, :], PS[b1][64:96, :], tab[64:96, 0, :], ALU.mult, [BPS[b1], Btb_], [B1])
            tt("dve", t2[64:96, :], PS[b2][64:96, :], tab[64:96, 1, :], ALU.mult, [BPS[b2], Btb_], [B2])
            tt("pool", krr[64:96, cs], t1[64:96, :], t2[64:96, :], ALU.add, [B1, B2], [Bkrr])
            for t4 in range(4):
                t16 = 4 * tc + t4
                bank = mbank()
                mm(PS[bank][:, 0:256], cqn[:, 2, t4 * 128:(t4 + 1) * 128], Wv, True, True, [Bcqn, BWkv], [BPS[bank]])
                v_heads(t16, bank)
            for hh in range(4):
                b1 = mbank(); b2 = mbank()
                for (bank, W_) in ((b1, Wuq), (b2, Wuqs)):
                    for c in range(2):
                        mm(PS[bank][0:96, :], W_[:, c, hh * 96:(hh + 1) * 96], cqn[:, c, :], c == 0, c == 1, [BWuq, Bcqn], [BPS[bank]])
                act(qB[hh][0:64, cs], PS[b1][0:64, :], AF.Copy, [BPS[b1]], [BqB[hh][tc]], scale=sc)
                t1, B1 = tf(); t2, B2 = tf()
                stt("dve", t1[64:96, :], PS[b1][64:96, :], sc, tab[64:96, 0, :], ALU.mult, ALU.mult, [BPS[b1], Btb_], [B1])
                stt("dve", t2[64:96, :], PS[b2][64:96, :], sc, tab[64:96, 1, :], ALU.mult, ALU.mult, [BPS[b2], Btb_], [B2])
                tt("pool", qB[hh][64:96, cs], t1[64:96, :], t2[64:96, :], ALU.add, [B1, B2], [BqB[hh][tc]])
                bank = mbank()
                mm(PS[bank][0:96, :], Wkp[:, hh, :], cqn[:, 2, :], True, True, [BWkv, Bcqn], [BPS[bank]])
                cp("dve", kB[hh][0:64, cs], PS[bank][0:64, :], [BPS[bank]], [BkB[hh][tc]])
                cp("pool", kB[hh][64:96, cs], krr[64:96, cs], [Bkrr], [BkB[hh][tc]])
        if debug and l == 0:
            dma(dbg_d[0], qB[0], BqB[0], [], "dbg")
            dma(dbg_d[1], kB[0], BkB[0], [], "dbg")
            dma(dbg_d[2], qB[3], BqB[3], [], "dbg")
            dma(dbg_d[3], kB[3], BkB[3], [], "dbg")
        pipe = Pipe()
        gate_head(416, 0)
        for h in range(4):
            for qc in range(4):
                ob = obank()
                tb_ = table_bias("Tc", qc, 7)

                e_, l_ = epilogue_norm(ob, h, qc, light_act=True)

                def late(l_=l_, h=h, qc=qc):
                    l_()
                    if qc == 3 and h < 3:
                        gate_head(416, h + 1)
                post = (e_, late)
                attn_tiles(pipe, ob, qB[h][0:96, qc * 512:(qc + 1) * 512], [BqB[h][qc]],
                           lambda kb, h=h: (kB[h][0:96, kb * 128:(kb + 1) * 128], [BkB[h][kb // 4]]),
                           list(range(4 * qc + 4)),
                           lambda kb, qc=qc, tb_=tb_: (tb_(kb) if kb >= 4 * qc else []),
                           lambda kb, h=h: (Vaug[:, kb, h, :], [BV[kb]]), 128, post=post, crange=causal_range(qc))
        prefetch_w(l, "B", "C")
        pipe.run(skew=3)
        store_mix(1)

    def mixer_c(l):
        arena_reset()
        V76s, BVs = carve(16 * 128, BF16, "V76s"); V76s = V76s.rearrange("p (t n) -> p t n", t=16)
        V76w, BVw = carve(16 * 128, BF16, "V76w"); V76w = V76w.rearrange("p (t n) -> p t n", t=16)
        vcT = qx[:, 1, :]
        nselT, BnselT = carve(SEQ, BF16, "nselT")
        posT, BposT = carve(64, F32, "posT")
        W1b = [carve(1024, BF16, "W1b%d" % i) for i in range(2)]
        w2st, Bw2st = carve(128, F32, "w2st")
        w2b, Bw2b = carve(256, BF16, "w2b")
        hidS, BhidS = carve(256, BF16, "hidS")
        kcc, Bkcc = carve(128, BF16, "kcc"); vcc, Bvcc = carve(64, BF16, "vcc")
        xp_off = ar["ptr"]
        Xp, BXp = carve(32 * 127, BF16, "Xp")
        pcs = [(2144, 64, 704), (2272, 64, 768), (2336, 12, 832), (2348, 128, 844), (2476, 128, 972), (2016, 64, 640),
               (1696, 128, 0), (1824, 128, 128), (1952, 64, 256), (1952, 64, 320), (2080, 64, 384), (2080, 64, 448),
               (2208, 64, 512), (2208, 64, 576)]
        load_pieces(l, "C")
        flush_store()
        arena_gate()
        qz = [qT[:, 0, :], qT[:, 1, :], kT[:, 0, :], kT[:, 1, :]]
        Bqz = [BqT[0], BqT[1], BkT[0], BkT[1]]
        for h_ in range(4):
            r0_ = 64 * (1 - h_ % 2)
            for tc_ in range(4):
                memset(qz[h_][r0_:r0_ + 64, tc_ * 512:(tc_ + 1) * 512], 0.0, [Bqz[h_][tc_]])
        memset(nselT[:, :], 0.0, [BnselT])
        KC2, KS2, KW2 = kx[:, 0, :], kx[:, 1, :], qx[:, 0, :]
        BKC2, BKS2, BKW2 = BkX[0], BkX[1], BqX[0]
        memset(V76s[:, :, 64:128], 1.0, [BVs]); memset(V76w[:, :, 64:128], 1.0, [BVw])

        def v_c(t16, bank):
            cp("dve", V76s[:, t16, 0:64], PS[bank][:, 0:64], [BPS[bank]], [BVs])
            cp("dve", V76w[:, t16, 0:64], PS[bank][:, 64:128], [BPS[bank]], [BVw])
        proj_v(704, 128, v_c)
        for tc in range(4):
            tab, Btb_ = load_tabs(0, tc)
            cs = slice(tc * 512, (tc + 1) * 512)
            for pair in range(2):
                bank = mbank()
                proj_fm(128 * pair, 128, tc, bank)
                rope_evac(bank, None, None, tab, Btb_, 0.125,
                          split=[(0, qz[2 * pair][:, cs], Bqz[2 * pair][tc]), (64, qz[2 * pair + 1][:, cs], Bqz[2 * pair + 1][tc])])
            for (c0, dst, Bd) in ((256, KC2, BKC2), (384, KS2, BKS2), (512, KW2, BKW2)):
                bank = mbank()
                proj_fm(c0, 128, tc, bank)
                rope_evac(bank, dst[:, cs], Bd[tc], tab, Btb_, 1.0)
            bank = mbank()
            proj_fm(640, 64, tc, bank)
            cp("dve", vcT[0:64, cs], PS[bank][0:64, :], [BPS[bank]], [BqX[1][tc]])

        def compress(srcT, Bsrc, pos_dram, w1_d, w2_d, is_k):
            dma(posT[0:64, 0:32], pos_dram[l].rearrange("l d -> d l"), [], [BposT], "cst4", allow_slow_non_contiguous=True)
            base = srcT[0:64, 0:1]
            in0 = bass.AP(base.tensor, base.offset, [list(base.ap[0]), [1, 32], [16, 127]])
            tt("pool", Xp.rearrange("p (l n) -> p l n", l=32)[0:64], in0, bc_last(posT[0:64, 0:32], 127), ALU.add, Bsrc + [BposT], [BXp])
            Xp3 = Xp.rearrange("p (l n) -> p l n", l=32)
            hb = [mbank(), mbank()]
            for pc in range(8):
                si = nxt("st", 2)
                dma(stage_flat(si, 0, 64, 1024).rearrange("p (l h) -> p l h", l=4),
                    w1_d[l, pc * 256:(pc + 1) * 256, :].rearrange("(l d) h -> d l h", d=64), [], [Bst[si]], "st%d" % si)
                wb, Bwb = W1b[pc % 2]
                cp("act" if pc % 2 else "dve", wb[0:64, :], stage_flat(si, 0, 64, 1024), [Bst[si]], [Bwb])
                wb3 = wb.rearrange("p (l h) -> p l h", l=4)
                for li in range(4):
                    lidx = pc * 4 + li
                    for half in range(2):
                        mm(PS[hb[half]][:, 0:127], wb3[0:64, li, half * 128:(half + 1) * 128], Xp3[0:64, lidx, :],
                           lidx == 0, lidx == 31, [Bwb, BXp], [BPS[hb[half]]])
            hs3 = hidS.rearrange("p (c n) -> p c n", c=2)
            for half in range(2):
                act(hs3[:, half, 0:127], PS[hb[half]][:, 0:127], AF.Silu, [BPS[hb[half]]], [BhidS])
            dma(w2st.rearrange("p (c n) -> p c n", c=2), w2_d[l].rearrange("(c p) n -> p c n", p=128), [], [Bw2st], "cst4")
            w2b3 = w2b.rearrange("p (c n) -> p c n", c=2)
            cp("pool", w2b3[:, :, 0:64], w2st.rearrange("p (c n) -> p c n", c=2), [Bw2st], [Bw2b])
            cp("pool", w2b3[:, :, 64:128], w2st.rearrange("p (c n) -> p c n", c=2), [Bw2st], [Bw2b])
            bank = mbank()
            if is_k:
                for c in range(2):
                    mm(PS[bank][:, 0:127], w2b3[:, c, :], hs3[:, c, 0:127], c == 0, c == 1, [Bw2b, BhidS], [BPS[bank]])
                cp("dve", kcc[:, 0:127], PS[bank][:, 0:127], [BPS[bank]], [Bkcc])
            else:
                for c in range(2):
                    mm(PS[bank][0:127, 0:64], hs3[:, c, 0:127], w2b3[:, c, 0:64], c == 0, c == 1, [Bw2b, BhidS], [BPS[bank]])
                cp("dve", vcc[0:127, :], PS[bank][0:127, 0:64], [BPS[bank]], [Bvcc])
        compress(KC2, BKC2, posk_d, kw1_d, kw2_d, True)
        compress(vcT, BqX[1], posv_d, vw1_d, vw2_d, False)
        save_ptr = ar["ptr"]; ar["ptr"] = xp_off
        impm, Bimpm = carve(128, F32, "impm"); rank, Brank = carve(128, F32, "rank"); cmp3, Bcmp3 = carve(1024, F32, "cmp3")
        ar["ptr"] = max(save_ptr, ar["ptr"])
        memset(small[:, 62:63], 0.0, [BXp, Bimpm, Brank, Bcmp3])

        c3 = cmp3.rearrange("p (a b) -> p a b", a=32)
        def select_blocks(qc, ib):
            cs = slice(qc * 512, (qc + 1) * 512)
            impq, Bimpq = tf()
            act(impq[0:32, :], PS[ib][0:32, :], AF.Copy, [BPS[ib]], [Bimpq])
            tb_ = mbank()
            for t4 in range(4):
                op("pe", lambda e, t4=t4, tb_=tb_, impq=impq: e.transpose(PS[tb_][:, t4 * 32:(t4 + 1) * 32], impq[0:32, t4 * 128:(t4 + 1) * 128], cfa("identf", 0, 32, 0, 32)),
                   [Bimpq, Bcf], [BPS[tb_]])
            tt("dve", impm, PS[tb_][:, 0:128], cfa("NF", 0, 128, qc * 128, (qc + 1) * 128), ALU.mult, [BPS[tb_], Bcf], [Bimpm])
            tt("dve", impm, impm, cfa("ADDT", 0, 128, qc * 128, (qc + 1) * 128), ALU.add, [Bimpm, Bcf], [Bimpm])
            for t4 in range(4):
                x_ = impm[:, t4 * 32:t4 * 32 + 1]
                xj2 = bass.AP(x_.tensor, x_.offset, [list(x_.ap[0]), [0, 32], [1, 32]])
                xj = bass.AP(x_.tensor, x_.offset, [list(x_.ap[0]), [1, 32], [0, 32]])
                tt("dve", c3, xj2, xj, ALU.is_gt, [Bimpm], [Bcmp3])
                op("dve", lambda e, t4=t4: e.reduce_sum(out=rank[:, t4 * 32:(t4 + 1) * 32], in_=c3, axis=AX.X), [Bcmp3], [Brank])
            ts("dve", rank, rank, 15.5, None, ALU.is_ge, None, [Brank], [Brank])

        def select_part2(qc):
            cs = slice(qc * 512, (qc + 1) * 512)
            tb2 = mbank()
            for t4 in range(4):
                op("pe", lambda e, t4=t4, tb2=tb2: e.transpose(PS[tb2][0:32, t4 * 128:(t4 + 1) * 128], rank[:, t4 * 32:(t4 + 1) * 32], cfa("identf")),
                   [Brank, Bcf], [BPS[tb2]])
            cp("dve", nselT[0:32, cs], PS[tb2][0:32, :], [BPS[tb2]], [BnselT])

        items = [dict(qc=qc, h=h) for qc in range(4) for h in range(4)]
        pending_sel = []
        ibs = {}

        def cA(T):
            qc, h = T["qc"], T["h"]
            cs = slice(qc * 512, (qc + 1) * 512)
            sbk = sbank(); T["sbk"] = sbk
            mm(PS[sbk][0:127, :], kcc[:, 0:127], qz[h][:, cs], True, False, [Bkcc, Bqz[h][qc]], [BPS[sbk]])
            mm(PS[sbk][0:127, :], cba("ident", 0, 127, 0, 127), cba("Tcmp", 0, 127, qc * 512, (qc + 1) * 512), False, True, [BCB], [BPS[sbk]])
            T["E"] = pbuf()
            E_, BE_ = T["E"]
            act(E_[0:127, :], PS[sbk][0:127, :], AF.Exp, [BPS[sbk]], [BE_])

        def cC(T):
            E_, BE_ = T["E"]
            db = mbank()
            mm(PS[db][:, :], cba("onesb", 0, 127), E_[0:127, :], True, True, [BCB, BE_], [BPS[db]])
            r_, Br_ = tf()
            act(r_[:, :], PS[db][:, :], AF.Ln, [BPS[db]], [Br_], bias=1e-30)
            act(r_[:, :], r_[:, :], AF.Exp, [Br_], [Br_], scale=-1.0)
            tt("pool", E_[0:127, :], E_[0:127, :], r_[0:127, :], ALU.mult, [BE_, Br_], [BE_])

        def cE(T):
            qc, h = T["qc"], T["h"]
            Pn, BPn = T["E"]
            if h == 0:
                ibs[qc] = obank()
            ib = ibs[qc]
            mm(PS[ib][0:32, :], cba("OV", 0, 127), Pn[0:127, :], h == 0, h == 3, [BCB, BPn], [BPS[ib]])
            cb_ = mbank()
            mm(PS[cb_][0:64, :], vcc[0:127, 0:64], Pn[0:127, :], True, True, [Bvcc, BPn], [BPS[cb_]])
            oc_, Boc_ = tf()
            cp("act", oc_[0:64, :], PS[cb_][0:64, :], [BPS[cb_]], [Boc_])
            dma(ocmp_d[h * 4 + qc], oc_[0:64, :], [Boc_], [Bocmp[h * 4 + qc]], "ocst")
            if h == 3:
                select_blocks(qc, ib)
                pending_sel.append([3, qc])

        n_it = len(items)
        for t in range(n_it + 2):
            if t < n_it:
                cA(items[t])
            if 0 <= t - 1 < n_it:
                cC(items[t - 1])
            if 0 <= t - 2 < n_it:
                cE(items[t - 2])
            for ps_ in list(pending_sel):
                if ps_[0] == 0:
                    select_part2(ps_[1]); pending_sel.remove(ps_)
                else:
                    ps_[0] -= 1
        for ps_ in pending_sel:
            select_part2(ps_[1])
        pools["M"] = [7]; pools["O"] = [3, 4, 5, 6]
        pipe = Pipe()
        gate_head(844, 0)
        for h in range(4):
            pair, po = h // 2, 64 * (h % 2)
            for qc in range(4):
                cs = slice(qc * 512, (qc + 1) * 512)
                qa = qz[h][:, cs]; qb_ = [Bqz[h][qc]]
                osb = obank(); owb = obank()
                tbc = table_bias("Tc", qc, 7)

                def bias_s(kb, qc=qc, tbc=tbc, cs=cs):
                    bl = [(cba("EXPNEG", 0, 128, kb * 128, (kb + 1) * 128), nselT[:, cs], [BCB, BnselT])]
                    if kb >= 4 * qc:
                        bl += tbc(kb)
                    return bl

                def early(h=h, qc=qc, pair=pair, po=po, cs=cs, osb=osb, owb=owb):
                    acc, Bacc = xt[0][:, 0:512], Bxt[0]
                    dma(acc[po:po + 64, :], ocmp_d[h * 4 + qc], [Bocmp[h * 4 + qc]], [Bacc], "ocld")
                    w1_, Bw1_ = xt[1][:, 0:512], Bxt[1]; w2_, Bw2_ = xt[1][:, 512:1024], Bxt[1]; sgl, Bsgl = xt[0][:, 512:1024], Bxt[0]
                    gb_ = mbank()
                    proj_fm(832, 12, qc, gb_)
                    act(sgl[64:76, :], PS[gb_][0:12, :], AF.Exp, [BPS[gb_]], [Bsgl], scale=-1.0)
                    act(sgl[64:76, :], sgl[64:76, :], AF.Ln, [Bsgl], [Bsgl], bias=1.0)
                    act(sgl[64:76, :], sgl[64:76, :], AF.Exp, [Bsgl], [Bsgl], scale=-1.0)
                    recip_act(w1_[64:76, :], PS[osb][64:76, :], [BPS[osb]], [Bw1_])
                    tt("dve", w1_[64:76, :], w1_[64:76, :], sgl[64:76, :], ALU.mult, [Bw1_, Bsgl], [Bw1_])
                    recip_act(w2_[64:76, :], PS[owb][64:76, :], [BPS[owb]], [Bw2_])
                    tt("dve", w2_[64:76, :], w2_[64:76, :], sgl[64:76, :], ALU.mult, [Bw2_, Bsgl], [Bw2_])
                    for i, (w_, Bw_) in enumerate(((sgl, Bsgl), (w1_, Bw1_), (w2_, Bw2_))):
                        ts("dve", w_[64:76, :], w_[64:76, :], cfa("SELCOL", 64, 76, 3 * h + i, 3 * h + i + 1), None, ALU.mult, None, [Bw_, Bcf], [Bw_])

                def late(h=h, qc=qc, pair=pair, po=po, cs=cs, osb=osb, owb=owb):
                    acc, Bacc = xt[0][:, 0:512], Bxt[0]
                    w1_, Bw1_ = xt[1][:, 0:512], Bxt[1]; w2_, Bw2_ = xt[1][:, 512:1024], Bxt[1]; sgl, Bsgl = xt[0][:, 512:1024], Bxt[0]
                    bbs = [mbank(), sbank(), sbank()]
                    for i, (wsrc, Bws) in enumerate(((sgl[64:76, :], Bsgl), (w1_[64:76, :], Bw1_), (w2_[64:76, :], Bw2_))):
                        mm(PS[bbs[i]][:, :], cfa("ones", 64, 76, 0, 128), wsrc, True, True, [Bcf, Bws], [BPS[bbs[i]]])
                    for i, obr in enumerate((None, osb, owb)):
                        bb = bbs[i]
                        if obr is None:
                            tt("dve", acc[po:po + 64, :], PS[bb][po:po + 64, :], acc[po:po + 64, :], ALU.mult, [Bacc, BPS[bb]], [Bacc])
                        else:
                            t_, Bt_ = tf()
                            evac_rows(obr, po, t_, Bt_, src0=0)
                            tt("dve", t_[po:po + 64, :], PS[bb][po:po + 64, :], t_[po:po + 64, :], ALU.mult, [Bt_, BPS[bb]], [Bt_])
                            tt("pool", acc[po:po + 64, :], acc[po:po + 64, :], t_[po:po + 64, :], ALU.add, [Bacc, Bt_], [Bacc])
                    tt("dve", mixT[po:po + 64, pair, cs], acc[po:po + 64, :], sg[po:po + 64, cs], ALU.mult, [Bacc, Bsg[qc]], [Bmix[pair][qc]])
                    if qc == 3 and h < 3:
                        gate_head(844, h + 1)
                post = (early, late)
                attn_tiles(pipe, osb, qa, qb_, lambda kb: (KS2[:, kb * 128:(kb + 1) * 128], [BKS2[kb // 4]]),
                           list(range(4 * qc + 4)), bias_s, lambda kb: (V76s[:, kb, :], [BVs]), 128, crange=causal_range(qc))
                attn_tiles(pipe, owb, qa, qb_, lambda kb: (KW2[:, kb * 128:(kb + 1) * 128], [BKW2[kb // 4]]),
                           list(range(max(0, 4 * qc - 4), 4 * qc + 4)), table_bias("Tw", qc, 11), lambda kb: (V76w[:, kb, :], [BVw]), 128, post=post, crange=window_range(qc))
        prefetch_w(l, "C", "D")
        pipe.run(skew=3, pskew=3)
        pools["M"] = [5, 6, 7]; pools["O"] = [3, 4]
        store_mix(2)

    def layer(l, xsrc, Bxsrc, xdst, Bxdst):
        for t16 in range(16):
            xi = t16 % 2
            dma(xt[xi][:], xsrc[t16 * 128:(t16 + 1) * 128, :], [Bxsrc], [Bxt[xi]], "xt%d" % xi)
            ssc, Bss = tcol()
            memset(ssc, 0.0, [Bss])
            junk = mlt[0][:].rearrange("p c n -> p (c n)")
            act(junk, xt[xi][:], AF.Square, [Bxt[xi], Bss], [Bmlt[0], Bss], accum_out=ssc)
            act(ssc, ssc, AF.Ln, [Bss], [Bss], scale=1.0 / DM, bias=EPS)
            act(ssc, ssc, AF.Exp, [Bss], [Bss], scale=-0.5)
            xn_, Bxn_ = (xn, BP[0]) if t16 % 2 == 0 else (mlt[1][:].rearrange("p c n -> p (c n)"), Bmlt[1])
            if t16 % 2 == 0:
                Bxn_ = BP[0]
            Bxn_l = [BP[0], BP[1]] if t16 % 2 == 0 else [Bmlt[1]]
            ts("dve", xn_, xt[xi][:], ssc, None, ALU.mult, None, [Bxt[xi], Bss], Bxn_l)
            bank = mbank()
            pv = PS[bank][:].bitcast(BF16)
            for c in range(8):
                op("pe", lambda e, c=c, pv=pv, xn_=xn_: e.transpose(pv[:, c * 128:(c + 1) * 128], xn_[:, c * 128:(c + 1) * 128], identb),
                   Bxn_l + [BCB], [BPS[bank]])
            cp("act" if t16 % 2 else "dve", hT[:, :, t16 * 128:(t16 + 1) * 128], pv.rearrange("p (c t) -> p c t", c=8), [BPS[bank]], [BhT[t16]])
        for m, (name, fn) in enumerate((("A", mixer_a), ("B", mixer_b), ("C", mixer_c), ("D", mixer_d))):
            if name in mixers:
                fn(l)
            else:
                zero_mixer(m)
        load_pieces(l, "O")
        flush_store()
        gpost = sg[:, 0:DM]; Bgpost = Bsg[0]
        dma(gpost, g_post_d[l:l + 1, :].partition_broadcast(128), [], [Bsg[0], Bsg[1]], "cst2")
        xb = [xt[0][:], xt[1][:], qx[:, 0, :].bitcast(F32), qx[:, 1, :].bitcast(F32)]
        Bxb = [[Bxt[0]], [Bxt[1]], BqX[0], BqX[1]]

        def xload(t16):
            xj = t16 % 4
            op("pool", lambda e: e.dma_start(out=xb[xj], in_=xsrc[t16 * 128:(t16 + 1) * 128, :]), [Bxsrc], Bxb[xj], dma="xs%d" % xj)
        for t_ in range(4):
            xload(t_)
        for t16 in range(16):
            xi = t16 % 2
            xj = t16 % 4
            dma(mlt[xi][:, 0:6, :], mix_d[t16, :, 0:6, :], [Bmixd[t16]], [Bmlt[xi]], "mlt%d" % xi)
            b0 = nxt("P5", 8); b1 = nxt("P5", 8)
            ssc, Bss = tcol(); ssc2, Bss2 = tcol()
            for half, bank in ((0, b0), (1, b1)):
                for c in range(8):
                    lt_ = mlt[xi][:, c, :] if c < 6 else mixT[:, c - 6, t16 * 128:(t16 + 1) * 128]
                    lb_ = Bmlt[xi] if c < 6 else Bmix[c - 6][t16 // 4]
                    mm(PS[bank][:, :], lt_, Wbf[:, c, half * 512:(half + 1) * 512], c == 0, c == 7,
                       BWr(half * 512, 512) + [lb_], [BPS[bank]])
            op("dve", lambda e, a=ssc: e.memset(a, 0.0), [], [Bss]); op("dve", lambda e, a=ssc2: e.memset(a, 0.0), [], [Bss2])
            act(xn[:, 0:512], PS[b0][:, :], AF.Square, [BPS[b0], Bss], [BP[0], Bss], accum_out=ssc)
            act(xn[:, 512:1024], PS[b1][:, :], AF.Square, [BPS[b1], Bss2], [BP[1], Bss2], accum_out=ssc2)
            tt("dve", ssc, ssc, ssc2, ALU.add, [Bss, Bss2], [Bss])
            act(ssc, ssc, AF.Ln, [Bss], [Bss], scale=1.0 / DM, bias=EPS)
            act(ssc, ssc, AF.Exp, [Bss], [Bss], scale=-0.5)
            for half, bank in ((0, b0), (1, b1)):
                hs = slice(half * 512, (half + 1) * 512)
                t_, Bt_ = tf()
                stt("dve", t_[:, :], PS[bank][:, :], ssc, gpost[:, hs], ALU.mult, ALU.mult, [BPS[bank], Bss, Bsg[0], Bsg[1]], [Bt_])
                tt("dve", xb[xj][:, hs], xb[xj][:, hs], t_[:, :], ALU.add, [Bt_] + Bxb[xj], Bxb[xj])
            op("pool", lambda e, t16=t16, xj=xj: e.dma_start(out=xdst[t16 * 128:(t16 + 1) * 128, :], in_=xb[xj]), Bxb[xj], [Bxdst], dma="xs%d" % xj)
            if t16 + 4 < 16:
                xload(t16 + 4)

    if nlayers == 1:
        layer(0, x_d, Buf("xin"), out_d, Bout)
    else:
        layer(0, x_d, Buf("xin"), x1_d, Bx1)
        layer(1, x1_d, Bx1, out_d, Bout)
    S.emit()
    if debug:
        print("sbuf bytes remaining", nc.sbuf_bytes_remaining, {k: len(v) for k, v in S.prog.items()})
    return nc


_NC_CACHE = {}


def _in_maps(inputs):
    cf = _build_consts()
    maps = []
    shared = {k: np.ascontiguousarray(np.asarray(v, dtype=np.float32)) for k, v in inputs.items() if k not in ("x", "positions")}
    x = np.asarray(inputs["x"], dtype=np.float32)
    pos = np.asarray(inputs["positions"]).astype(np.int32)
    for b in range(x.shape[0]):
        m = dict(shared)
        m["x"] = np.ascontiguousarray(x[b])
        m["positions"] = np.ascontiguousarray(pos[b:b + 1])
        m["cff"] = cf["_f"]
        m["cfb"] = cf["_b"]
        maps.append(m)
    return maps


def kernel(**inputs):
    if "nc" not in _NC_CACHE:
        _NC_CACHE["nc"] = build()
    nc = _NC_CACHE["nc"]
    res = run_bass_kernel_spmd(nc, _in_maps(inputs), core_ids=list(range(NCORES)))
    return np.stack([np.asarray(r["out"], dtype=np.float32) for r in res.results], axis=0)
```
